# Optimizing a Trainium2 kernel written in Bass

```python
import math
import jax, jax.numpy as jnp
from jax import lax
import numpy as np

D_MODEL = 1024
BATCH = 2
SEQ = 8192
DEPTH = 2

D_FF = 2816
EPS = 1e-6
HG_HEADS = 4
HG_DK = 128
HG_DV = 128
D_HG = HG_HEADS * HG_DK
HG_CHUNK = 64
SSD_HEADS = 8
SSD_HEADDIM = 64
D_SSD = SSD_HEADS * SSD_HEADDIM
SSD_GROUPS = 2
SSD_STATE = 128
SSD_CONV = 4
SSD_CHUNK = 64
D_XBC = D_SSD + 2 * SSD_GROUPS * SSD_STATE
S5_GROUPS = 32
S5_GROUP_CH = 16
D_S5 = S5_GROUPS * S5_GROUP_CH
S5_STATE = 64
D_MIX = D_HG + D_SSD + D_S5
D_IN = 4 * D_HG + D_SSD + D_XBC + SSD_HEADS + D_S5

kernel_name = 'hybrid_hgrn2_ssd_s5_macaron'


def _rmsnorm(x, g):
    x32 = x.astype(jnp.float32)
    y = x32 * lax.rsqrt(jnp.mean(x32 * x32, axis=-1, keepdims=True) + EPS)
    return (y * g.astype(jnp.float32)).astype(x.dtype)


def _swiglu(x, w_gate, w_up, w_down):
    return (jax.nn.silu(x @ w_gate) * (x @ w_up)) @ w_down


def _hgrn2(q_raw, f_raw, i_raw, g_raw, lb, gnorm_w):
    out_dtype = q_raw.dtype
    bsz, seq, _ = q_raw.shape
    nc = seq // HG_CHUNK
    f32 = jnp.float32
    lb = lb.astype(f32)
    q = jax.nn.silu(q_raw.astype(f32))
    log_f = jnp.logaddexp(jnp.log(lb), jnp.log1p(-lb) + jax.nn.log_sigmoid(f_raw.astype(f32)))
    k = -jnp.expm1(log_f)
    v = i_raw.astype(f32)

    def chunks(t, d):
        return t.reshape(bsz, nc, HG_CHUNK, HG_HEADS, d).transpose(1, 0, 3, 2, 4)

    qc, kc, vc = chunks(q, HG_DK), chunks(k, HG_DK), chunks(v, HG_DV)
    bc = jnp.cumsum(chunks(log_f, HG_DK), axis=3)
    causal = jnp.tril(jnp.ones((HG_CHUNK, HG_CHUNK), dtype=bool))[:, :, None]

    def step(S, xs):
        q_, k_, v_, b_ = xs
        o_inter = jnp.einsum('bhtk,bhkv->bhtv', q_ * jnp.exp(b_), S)
        rel = b_[:, :, :, None, :] - b_[:, :, None, :, :]
        decay = jnp.exp(jnp.where(causal, rel, -jnp.inf))
        scores = jnp.einsum('bhtk,bhsk,bhtsk->bhts', q_, k_, decay)
        o_intra = jnp.einsum('bhts,bhsv->bhtv', scores, v_)
        b_last = b_[:, :, -1:, :]
        S_new = jnp.exp(b_last[:, :, 0, :])[..., None] * S + jnp.einsum(
            'bhsk,bhsv->bhkv', k_ * jnp.exp(b_last - b_), v_)
        return S_new, o_inter + o_intra

    S0 = jnp.zeros((bsz, HG_HEADS, HG_DK, HG_DV), f32)
    _, o = lax.scan(step, S0, (qc, kc, vc, bc))
    o = o.transpose(1, 0, 3, 2, 4).reshape(bsz, seq, HG_HEADS, HG_DV)
    o = o * lax.rsqrt(jnp.mean(o * o, axis=-1, keepdims=True) + EPS) * gnorm_w.astype(f32)
    o = o.reshape(bsz, seq, D_HG) * jax.nn.silu(g_raw.astype(f32))
    return o.astype(out_dtype)


def _ssd(z, xbc, dt_raw, conv_w, conv_b, dt_bias, a_log, d_skip, norm_w):
    out_dtype = z.dtype
    bsz, seq, _ = z.shape
    nc = seq // SSD_CHUNK
    hg = SSD_HEADS // SSD_GROUPS
    f32 = jnp.float32
    xbc = lax.conv_general_dilated(
        xbc.astype(f32), conv_w.astype(f32)[:, None, :], window_strides=(1,),
        padding=[(SSD_CONV - 1, 0)], dimension_numbers=('NWC', 'WIO', 'NWC'),
        feature_group_count=D_XBC)
    xbc = jax.nn.silu(xbc + conv_b.astype(f32))
    x = xbc[..., :D_SSD]
    b_in = xbc[..., D_SSD:D_SSD + SSD_GROUPS * SSD_STATE]
    c_in = xbc[..., D_SSD + SSD_GROUPS * SSD_STATE:]
    dt = jax.nn.softplus(dt_raw.astype(f32) + dt_bias.astype(f32))
    a = -jnp.exp(a_log.astype(f32)).reshape(SSD_GROUPS, hg)

    xh = x.reshape(bsz, nc, SSD_CHUNK, SSD_GROUPS, hg, SSD_HEADDIM)
    bm = b_in.reshape(bsz, nc, SSD_CHUNK, SSD_GROUPS, SSD_STATE)
    cm = c_in.reshape(bsz, nc, SSD_CHUNK, SSD_GROUPS, SSD_STATE)
    dtc = dt.reshape(bsz, nc, SSD_CHUNK, SSD_GROUPS, hg)
    a_dt = (dtc * a).transpose(0, 3, 4, 1, 2)
    a_cs = jnp.cumsum(a_dt, axis=-1)
    x_dt = xh * dtc[..., None]
    causal = jnp.tril(jnp.ones((SSD_CHUNK, SSD_CHUNK), dtype=bool))
    l_dec = jnp.exp(jnp.where(causal, a_cs[..., :, None] - a_cs[..., None, :], -jnp.inf))
    scores = jnp.einsum('bclgn,bcsgn->bgcls', cm, bm)
    y_diag = jnp.einsum('bgcls,bghcls,bcsghp->bclghp', scores, l_dec, x_dt)
    decay_states = jnp.exp(a_cs[..., -1:] - a_cs)
    states = jnp.einsum('bcsgn,bghcs,bcsghp->cbghpn', bm, decay_states, x_dt)
    chunk_decay = jnp.exp(a_cs[..., -1]).transpose(3, 0, 1, 2)

    def step(h, xs):
        dec, st = xs
        return dec[..., None, None] * h + st, h

    h0 = jnp.zeros((bsz, SSD_GROUPS, hg, SSD_HEADDIM, SSD_STATE), f32)
    _, prev = lax.scan(step, h0, (chunk_decay, states))
    y_off = jnp.einsum('bclgn,cbghpn,bghcl->bclghp', cm, prev, jnp.exp(a_cs))
    y = (y_diag + y_off).reshape(bsz, seq, D_SSD) + x * jnp.repeat(d_skip.astype(f32), SSD_HEADDIM)
    y = (y * jax.nn.silu(z.astype(f32))).reshape(bsz, seq, SSD_GROUPS, D_SSD // SSD_GROUPS)
    y = y * lax.rsqrt(jnp.mean(y * y, axis=-1, keepdims=True) + EPS)
    y = y.reshape(bsz, seq, D_SSD) * norm_w.astype(f32)
    return y.astype(out_dtype)


def _s5_combine(e1, e2):
    a1r, a1i, b1r, b1i = e1
    a2r, a2i, b2r, b2i = e2
    return (a2r * a1r - a2i * a1i,
            a2r * a1i + a2i * a1r,
            a2r * b1r - a2i * b1i + b2r,
            a2r * b1i + a2i * b1r + b2i)


def _s5(u, a_re, a_im, b_re, b_im, c_re, c_im, d_skip, log_dt, glu_w, glu_b):
    out_dtype = u.dtype
    bsz, seq, _ = u.shape
    f32 = jnp.float32
    u32 = u.astype(f32)
    ug = u32.reshape(bsz, seq, S5_GROUPS, S5_GROUP_CH)
    ar, ai = a_re.astype(f32), a_im.astype(f32)
    delta = jnp.exp(log_dt.astype(f32))[:, None]
    mag = jnp.exp(ar * delta)
    ab_re, ab_im = mag * jnp.cos(ai * delta), mag * jnp.sin(ai * delta)
    den = ar * ar + ai * ai
    nr, ni = ab_re - 1.0, ab_im
    fr = (nr * ar + ni * ai) / den
    fi = (ni * ar - nr * ai) / den
    br, bi = b_re.astype(f32), b_im.astype(f32)
    bb_re = fr[..., None] * br - fi[..., None] * bi
    bb_im = fr[..., None] * bi + fi[..., None] * br
    bu_re = jnp.einsum('blgc,gpc->blgp', ug, bb_re)
    bu_im = jnp.einsum('blgc,gpc->blgp', ug, bb_im)
    elems = (jnp.broadcast_to(ab_re, bu_re.shape), jnp.broadcast_to(ab_im, bu_im.shape), bu_re, bu_im)
    _, _, xr, xi = lax.associative_scan(_s5_combine, elems, axis=1)
    y = jnp.einsum('blgp,gcp->blgc', xr, c_re.astype(f32)) - jnp.einsum('blgp,gcp->blgc', xi, c_im.astype(f32))
    y = y.reshape(bsz, seq, D_S5) + d_skip.astype(f32) * u32
    y = jax.nn.gelu(y)
    y = y * jax.nn.sigmoid(y @ glu_w.astype(f32) + glu_b.astype(f32))
    return y.astype(out_dtype)


def _mixer(h, w_in, w_out, lb, hg_gnorm, ssd_conv_w, ssd_conv_b, ssd_dt_bias, ssd_a_log, ssd_d,
           ssd_norm, s5_a_re, s5_a_im, s5_b_re, s5_b_im, s5_c_re, s5_c_im, s5_d, s5_log_dt,
           s5_glu_w, s5_glu_b):
    sizes = (D_HG, D_HG, D_HG, D_HG, D_SSD, D_XBC, SSD_HEADS, D_S5)
    idx = [int(i) for i in np.cumsum(sizes)[:-1]]
    q, f, i, g, z, xbc, dt, u = jnp.split(h @ w_in, idx, axis=-1)
    o_a = _hgrn2(q, f, i, g, lb, hg_gnorm)
    o_b = _ssd(z, xbc, dt, ssd_conv_w, ssd_conv_b, ssd_dt_bias, ssd_a_log, ssd_d, ssd_norm)
    o_c = _s5(u, s5_a_re, s5_a_im, s5_b_re, s5_b_im, s5_c_re, s5_c_im, s5_d, s5_log_dt, s5_glu_w, s5_glu_b)
    return jnp.concatenate([o_a, o_b, o_c], axis=-1) @ w_out


def setup_inputs(seed: int = 0) -> dict:
    key = jax.random.key(seed)
    ks = jax.random.split(key, 26)
    nrm = jax.random.normal
    f32 = jnp.float32
    dt0 = jnp.exp(jax.random.uniform(ks[12], (DEPTH, SSD_HEADS), f32, math.log(1e-3), math.log(1e-1)))
    return {
        'x': nrm(ks[0], (BATCH, SEQ, D_MODEL), f32),
        'norm_g': 1.0 + 0.02 * nrm(ks[1], (DEPTH, 6, D_MODEL), f32),
        'ffn_w_gate': nrm(ks[2], (DEPTH, 2, D_MODEL, D_FF), f32) * D_MODEL ** -0.5,
        'ffn_w_up': nrm(ks[3], (DEPTH, 2, D_MODEL, D_FF), f32) * D_MODEL ** -0.5,
        'ffn_w_down': nrm(ks[4], (DEPTH, 2, D_FF, D_MODEL), f32) * D_FF ** -0.5,
        'w_in': nrm(ks[5], (DEPTH, D_MODEL, D_IN), f32) * D_MODEL ** -0.5,
        'w_out': nrm(ks[6], (DEPTH, D_MIX, D_MODEL), f32) * D_MIX ** -0.5,
        'hg_lb_logits': 0.1 * nrm(ks[7], (DEPTH, D_HG), f32),
        'hg_gnorm': 1.0 + 0.02 * nrm(ks[8], (DEPTH, HG_DV), f32),
        'ssd_conv_w': nrm(ks[9], (DEPTH, SSD_CONV, D_XBC), f32) * SSD_CONV ** -0.5,
        'ssd_conv_b': 0.01 * nrm(ks[10], (DEPTH, D_XBC), f32),
        'ssd_dt_bias': dt0 + jnp.log(-jnp.expm1(-dt0)),
        'ssd_A_log': jnp.log(jax.random.uniform(ks[11], (DEPTH, SSD_HEADS), f32, 1.0, 16.0)),
        'ssd_D': 1.0 + 0.1 * nrm(ks[13], (DEPTH, SSD_HEADS), f32),
        'ssd_norm': 1.0 + 0.02 * nrm(ks[14], (DEPTH, D_SSD), f32),
        's5_A_re': -0.5 + 0.01 * nrm(ks[15], (DEPTH, S5_GROUPS, S5_STATE), f32),
        's5_A_im': math.pi * jnp.arange(S5_STATE, dtype=f32) + 0.01 * nrm(ks[16], (DEPTH, S5_GROUPS, S5_STATE), f32),
        's5_B_re': nrm(ks[17], (DEPTH, S5_GROUPS, S5_STATE, S5_GROUP_CH), f32) * (2 * S5_GROUP_CH) ** -0.5,
        's5_B_im': nrm(ks[18], (DEPTH, S5_GROUPS, S5_STATE, S5_GROUP_CH), f32) * (2 * S5_GROUP_CH) ** -0.5,
        's5_C_re': nrm(ks[19], (DEPTH, S5_GROUPS, S5_GROUP_CH, S5_STATE), f32) * S5_STATE ** -0.5,
        's5_C_im': nrm(ks[20], (DEPTH, S5_GROUPS, S5_GROUP_CH, S5_STATE), f32) * S5_STATE ** -0.5,
        's5_D': nrm(ks[21], (DEPTH, D_S5), f32),
        's5_log_dt': jax.random.uniform(ks[22], (DEPTH, S5_GROUPS), f32, math.log(1e-3), math.log(1e-1)),
        's5_glu_w': nrm(ks[23], (DEPTH, D_S5, D_S5), f32) * D_S5 ** -0.5,
        's5_glu_b': 0.01 * nrm(ks[24], (DEPTH, D_S5), f32),
    }


def reference(x, norm_g, ffn_w_gate, ffn_w_up, ffn_w_down, w_in, w_out, hg_lb_logits, hg_gnorm,
              ssd_conv_w, ssd_conv_b, ssd_dt_bias, ssd_A_log, ssd_D, ssd_norm, s5_A_re, s5_A_im,
              s5_B_re, s5_B_im, s5_C_re, s5_C_im, s5_D, s5_log_dt, s5_glu_w, s5_glu_b):
    lb_all = jnp.cumsum(jax.nn.softmax(hg_lb_logits.astype(jnp.float32), axis=0), axis=0)
    lb_all = lb_all - lb_all[:1]
    h = x
    for l in range(DEPTH):
        g = norm_g[l]
        y = _swiglu(_rmsnorm(h, g[0]), ffn_w_gate[l, 0], ffn_w_up[l, 0], ffn_w_down[l, 0])
        h = h + 0.5 * _rmsnorm(y, g[1])
        y = _mixer(_rmsnorm(h, g[2]), w_in[l], w_out[l], lb_all[l], hg_gnorm[l],
                   ssd_conv_w[l], ssd_conv_b[l], ssd_dt_bias[l], ssd_A_log[l], ssd_D[l], ssd_norm[l],
                   s5_A_re[l], s5_A_im[l], s5_B_re[l], s5_B_im[l], s5_C_re[l], s5_C_im[l], s5_D[l],
                   s5_log_dt[l], s5_glu_w[l], s5_glu_b[l])
        h = h + _rmsnorm(y, g[3])
        y = _swiglu(_rmsnorm(h, g[4]), ffn_w_gate[l, 1], ffn_w_up[l, 1], ffn_w_down[l, 1])
        h = h + 0.5 * _rmsnorm(y, g[5])
    return h
```

```python
import contextlib
import math
import os
import numpy as np
import concourse.bass as bass
import concourse.mybir as mybir
from concourse.bass_utils import run_bass_kernel_spmd

F32 = mybir.dt.float32
BF16 = mybir.dt.bfloat16
AF = mybir.ActivationFunctionType
ALU = mybir.AluOpType

P = 128
D = 1024
DC = 8
FF = 2816
FC = 22
TS = 1024
NT = TS // 128
NBLK = TS // 512
DIN = 4104
EPS = 1e-6
DEPTH = 2
N_CORES = 8
SEQ = 8192
BATCH = 2


class Buf:
    __slots__ = ("w", "r", "psum")

    def __init__(self):
        self.w = None
        self.r = []
        self.psum = False


_PSUM_NAMES = {"pp", "psm", "pc", "ptt", "prr", "ssp", "pg", "pu", "py", "pu5", "pe5", "py5", "pq", "ptk", "pss",
               "pso", "psS", "pto", "ptx", "pd", "pyo", "px", "pxo"}


class BufTable(dict):
    def __missing__(self, k):
        b = Buf()
        ks = k if isinstance(k, tuple) else (k,)
        b.psum = any(x in _PSUM_NAMES for x in ks if isinstance(x, str))
        self[k] = b
        return b


class _Eng:
    def __init__(self, name):
        self.name = name
        self.sems = []
        self.count = 0
        self.ops = []
        self.seen = {}


SEM_ROLL = 30000
_UC = [0]


def _psum(nc, stack, name, shape, dt):
    shape = list(shape)
    esz = 2 if dt == BF16 else 4
    n = 1
    for d in shape[1:]:
        n *= d
    per_bank = 2048 // esz
    nb = (n * esz + 2047) // 2048
    t = stack.enter_context(nc.psum_tensor(_u(name), [128, nb * per_bank], dt))
    v = t[0:shape[0], 0:n]
    if len(shape) == 3:
        v = v.rearrange("p (a b) -> p a b", b=shape[2])
    elif len(shape) == 4:
        v = v.rearrange("p (a b c) -> p a b c", b=shape[2], c=shape[3])
    return v


def _u(name):
    _UC[0] += 1
    return f"{name}_{_UC[0]}"


class KB:
    def __init__(self, nc, n_dma_ch=32):
        self.nc = nc
        self.eng = {n: _Eng(n) for n in ("tensor", "vector", "scalar", "gpsimd", "sync")}
        self.nsem = 0
        for e in self.eng.values():
            e.sems.append(self._newsem())
        self.dma_chs = {q: [{"sem": self._newsem(), "count": 0, "tok": None} for _ in range(n_dma_ch // 2)] for q in ("hw", "sw")}
        self.dma_rrs = {"hw": 0, "sw": 0}

    def _newsem(self):
        self.nsem += 1
        return self.nsem - 1

    def _collect(self, e, reads, writes, skip_own=False):
        best = {}
        for b in reads:
            if b.w is not None:
                s, v = b.w
                if v > best.get(s, 0):
                    best[s] = v
            if b.psum:
                for (s, v) in b.r:
                    if s not in e.sems and v > best.get(s, 0):
                        best[s] = v
        for b in writes:
            if b.w is not None:
                s, v = b.w
                if v > best.get(s, 0):
                    best[s] = v
            for (s, v) in b.r:
                if v > best.get(s, 0):
                    best[s] = v
        waits = []
        for s, v in best.items():
            if skip_own and s in e.sems:
                continue
            if e.seen.get(s, 0) >= v:
                continue
            e.seen[s] = v
            waits.append((s, v))
        return waits

    def _commit(self, tok, reads, writes):
        for b in reads:
            b.r.append(tok)
            if len(b.r) > 64:
                best = {}
                for (s, v) in b.r:
                    if v > best.get(s, 0):
                        best[s] = v
                b.r = list(best.items())
        for b in writes:
            b.w = tok
            b.r = []

    def op(self, engname, fn, reads=(), writes=()):
        e = self.eng[engname]
        waits = self._collect(e, reads, writes, skip_own=(engname == "tensor"))
        if e.count >= SEM_ROLL:
            e.sems.append(self._newsem())
            e.count = 0
        e.count += 1
        tok = (e.sems[-1], e.count)
        e.ops.append((waits, fn, (tok[0], 1)))
        self._commit(tok, reads, writes)
        return tok

    def dma(self, engname, out, in_, reads=(), writes=(), **kw):
        e = self.eng[engname]
        q = "sw" if engname == "gpsimd" else "hw"
        chs = self.dma_chs[q]
        ch = chs[self.dma_rrs[q]]
        self.dma_rrs[q] = (self.dma_rrs[q] + 1) % len(chs)
        waits = self._collect(e, reads, writes)
        if ch["tok"] is not None:
            s, v = ch["tok"]
            if e.seen.get(s, 0) < v:
                e.seen[s] = v
                waits.append((s, v))
        ch["count"] += 16
        tok = (ch["sem"], ch["count"])
        ch["tok"] = tok
        e.ops.append((waits, lambda eng: eng.dma_start(out=out, in_=in_, **kw), (tok[0], 16)))
        self._commit(tok, reads, writes)
        return tok

    def barrier(self):
        toks = []
        for e in self.eng.values():
            if e.count > 0:
                toks.append((e.sems[-1], e.count))
        for chs in self.dma_chs.values():
            for ch in chs:
                if ch["tok"] is not None:
                    toks.append(ch["tok"])
        for e in self.eng.values():
            waits = []
            for (s, v) in toks:
                if e.seen.get(s, 0) >= v:
                    continue
                e.seen[s] = v
                waits.append((s, v))
            if waits:
                e.ops.append((waits, None, None))

    def wait_all(self, engname, bufs):
        e = self.eng[engname]
        waits = self._collect(e, bufs, ())
        e.ops.append((waits, None, None))

    def replay(self):
        nc = self.nc
        with contextlib.ExitStack() as st:
            sems = [st.enter_context(nc.semaphore(f"s{i}")) for i in range(self.nsem)]
            block = st.enter_context(nc.Block())

            def mk(e):
                def body(eng):
                    for waits, fn, inc in e.ops:
                        for (s, v) in waits:
                            eng.wait_ge(sems[s], v)
                        if fn is not None:
                            fn(eng).then_inc(sems[inc[0]], inc[1])
                return body
            for name in ("sync", "gpsimd", "scalar", "vector", "tensor"):
                e = self.eng[name]
                if e.ops:
                    getattr(block, name)(mk(e))


C_ID = 0
C_ONES = 128
C_TRI = 256
C_UPS = 384
C_MHG = 512
C_MS5 = 640
C_RST = 768
C_NIDX = C_RST + TS
C_JIDX = C_NIDX + 129
C_W = C_JIDX + 24


def make_consts():
    c = np.zeros((128, C_W), np.float32)
    r = np.arange(128)
    c[:, C_ID:C_ID + 128] = np.eye(128)
    c[:, C_ONES:C_ONES + 128] = 1.0
    c[:, C_TRI:C_TRI + 128] = (r[:, None] <= r[None, :])
    c[:, C_UPS:C_UPS + 128] = (r[:, None] > r[None, :])
    c[:, C_MHG:C_MHG + 128] = (r[:, None] <= r[None, :]) & ((r[:, None] // 64) == (r[None, :] // 64))
    c[:, C_MS5:C_MS5 + 128] = ((r[None, :] // 16) >= (r[:, None] // 16))
    rst = np.ones(TS, np.float32)
    rst[::64] = 0.0
    c[:, C_RST:C_RST + TS] = rst[None, :]
    c[:, C_NIDX:C_NIDX + 129] = np.arange(129)[None, :]
    j = np.concatenate([-np.arange(1, 9), np.arange(7, -1, -1), np.arange(1, 9)]).astype(np.float32)
    c[:, C_JIDX:C_JIDX + 24] = j[None, :]
    return c


def FP_G(l, i, c):
    return (l * 6 + i) * 8 + c


FP_CW = 96


def FP_CONVW(l, k, c):
    return FP_CW + (l * 4 + k) * 8 + c


FP_CB0 = FP_CW + 64


def FP_CONVB(l, c):
    return FP_CB0 + l * 8 + c


FP_SD0 = FP_CB0 + 16


def FP_S5D(l, c):
    return FP_SD0 + l * 4 + c


FP_GB0 = FP_SD0 + 8


def FP_GLUB(l, c):
    return FP_GB0 + l * 4 + c


FP_LB0 = FP_GB0 + 8
FP_GS0 = FP_LB0 + 8


def FP_GS(l, f, c):
    return FP_GS0 + (l * 2 + f) * 8 + c


FP_LBV = FP_GS0 + 32


def FP_LB(l, hd):
    return FP_LBV + l * 4 + hd


FP_OMLV = FP_LBV + 8


def FP_OML(l, hd):
    return FP_OMLV + l * 4 + hd


FP_W = FP_OMLV + 8

RP_GN = 0
RP_DTB = 256
RP_A = RP_DTB + 16
RP_D = RP_A + 16
RP_NW = RP_D + 16
RP_W = RP_NW + 1024


def build(n_seg, depth=DEPTH, flags=("ffn", "hg", "ssd", "s5"), dbg=False):
    nc = bass.Bass("TRN2", target_bir_lowering=False)
    NTOK = n_seg * TS

    def din(name, shape, dt=F32):
        return nc.dram_tensor(name, list(shape), dt, kind="ExternalInput").ap()

    x = din("x", [NTOK, D])
    cst_d = din("cst", [128, C_W])
    norm_g = din("norm_g", [DEPTH, 6, D])
    w_gate = din("ffn_w_gate", [DEPTH, 2, D, FF])
    w_up = din("ffn_w_up", [DEPTH, 2, D, FF])
    w_down = din("ffn_w_down", [DEPTH, 2, FF, D])
    w_in = din("w_in", [DEPTH, D, DIN])
    w_out = din("w_out", [DEPTH, 1536, D])
    lb_log = din("hg_lb_logits", [DEPTH, 512])
    hg_gn = din("hg_gnorm", [DEPTH, 128])
    conv_w = din("ssd_conv_w", [DEPTH, 4, 1024])
    conv_b = din("ssd_conv_b", [DEPTH, 1024])
    dt_bias = din("ssd_dt_bias", [DEPTH, 8])
    a_log = din("ssd_A_log", [DEPTH, 8])
    ssd_d = din("ssd_D", [DEPTH, 8])
    ssd_nw = din("ssd_norm", [DEPTH, 512])
    s5_are = din("s5_A_re", [DEPTH, 32, 64])
    s5_aim = din("s5_A_im", [DEPTH, 32, 64])
    s5_bre = din("s5_B_re", [DEPTH, 32, 64, 16])
    s5_bim = din("s5_B_im", [DEPTH, 32, 64, 16])
    s5_cre = din("s5_C_re", [DEPTH, 32, 16, 64])
    s5_cim = din("s5_C_im", [DEPTH, 32, 16, 64])
    s5_dsk = din("s5_D", [DEPTH, 512])
    s5_ldt = din("s5_log_dt", [DEPTH, 32])
    glu_w = din("s5_glu_w", [DEPTH, 512, 512])
    glu_b = din("s5_glu_b", [DEPTH, 512])
    out = nc.dram_tensor("out", [NTOK, D], F32, kind="ExternalOutput").ap()
    dbg_o = nc.dram_tensor("dbg_o", [1536, NTOK], F32, kind="ExternalOutput").ap() if dbg else None

    def dscr(name, shape, dt):
        return nc.dram_tensor(name, list(shape), dt, kind="Internal").ap()

    ud = dscr("ud", [512, TS], BF16)
    yd = dscr("yd", [512, TS], F32)
    tabT = dscr("tabT", [DEPTH, 128, 32 * 128], BF16)
    tabR = dscr("tabR", [DEPTH, 2, 128, 32 * 64], BF16)
    tabO = dscr("tabO", [DEPTH, 2, 64, 32 * 128], BF16)
    tabCS = dscr("tabCS", [DEPTH, 2, 64, 32 * 129], F32)
    tabRho = dscr("tabRho", [DEPTH, 64, 32], F32)

    kb = KB(nc)
    B = BufTable()
    A = kb.op

    def mm(out_, lhsT, rhs, start, stop, reads, writes, **kw):
        A("tensor", lambda e: e.matmul(out_, lhsT=lhsT, rhs=rhs, start=start, stop=stop, **kw), reads, writes)

    def tr(out_, in_, ident, reads, writes):
        A("tensor", lambda e: e.transpose(out=out_, in_=in_, identity=ident), reads, writes)

    def act(out_, in_, func, reads, writes, **kw):
        A("scalar", lambda e: e.activation(out=out_, in_=in_, func=func, **kw), reads, writes)

    def tt(out_, in0, in1, op, reads, writes, eng="vector"):
        A(eng, lambda e: e.tensor_tensor(out=out_, in0=in0, in1=in1, op=op), reads, writes)

    def ts(out_, in0, s1, s2, op0, op1, reads, writes, eng="vector"):
        if op1 is None:
            A(eng, lambda e: e.tensor_scalar(out=out_, in0=in0, scalar1=s1, scalar2=None, op0=op0), reads, writes)
        else:
            A(eng, lambda e: e.tensor_scalar(out=out_, in0=in0, scalar1=s1, scalar2=s2, op0=op0, op1=op1), reads, writes)

    def stt(out_, in0, scalar, in1, op0, op1, reads, writes):
        A("vector", lambda e: e.scalar_tensor_tensor(out=out_, in0=in0, scalar=scalar, in1=in1, op0=op0, op1=op1), reads, writes)

    def scan(out_, d0, d1, init, reads, writes):
        A("vector", lambda e: e.tensor_tensor_scan(out=out_, data0=d0, data1=d1, initial=init, op0=ALU.mult, op1=ALU.add), reads, writes)

    def recip(t_, b_):
        A("vector", lambda e: e.reciprocal(out=t_, in_=t_), [b_], [b_])

    def cp(out_, in_, reads, writes, eng="scalar"):
        if eng == "scalar":
            A("scalar", lambda e: e.copy(out=out_, in_=in_), reads, writes)
        else:
            A(eng, lambda e: e.tensor_copy(out=out_, in_=in_), reads, writes)

    with contextlib.ExitStack() as glob:
        def gsb(name, shape, dt):
            return glob.enter_context(nc.sbuf_tensor(_u(name), list(shape), dt))

        cst = gsb("cst", [128, C_W], F32)
        identb = gsb("identb", [128, 128], BF16)
        onesb = gsb("onesb", [128, 128], BF16)
        FP = gsb("FP", [128, FP_W], F32)
        RP = gsb("RP", [128, RP_W], F32)
        hT = gsb("hT", [128, DC, TS], F32)
        Shg = gsb("Shg", [128, DEPTH, 4, 128], F32)
        Shgb = gsb("Shgb", [128, DEPTH, 4, 128], BF16)
        Hst = gsb("Hst", [128, DEPTH, 512], F32)
        Hstb = gsb("Hstb", [128, DEPTH, 512], BF16)
        halo = gsb("halo", [128, DEPTH, 8, 3], F32)
        Xc = gsb("Xc", [128, DEPTH, 2, 16], F32)
        rho8 = gsb("rho8", [64, DEPTH, 32], F32)
        rho8s = gsb("rho8s", [128, DEPTH, 16], F32)

        ident = cst[:, C_ID:C_ID + 128]
        ones = cst[:, C_ONES:C_ONES + 128]
        tri = cst[:, C_TRI:C_TRI + 128]
        ups = cst[:, C_UPS:C_UPS + 128]
        mhg = cst[:, C_MHG:C_MHG + 128]
        ms5 = cst[:, C_MS5:C_MS5 + 128]
        rstm = cst[:, C_RST:C_RST + TS]

        def fpc(col):
            return FP[:, col:col + 1]

        with contextlib.ExitStack() as sc:
            def sb(name, shape, dt):
                return sc.enter_context(nc.sbuf_tensor(_u(name), list(shape), dt))

            def ps(name, shape, dt=F32):
                return _psum(nc, sc, name, list(shape), dt)

            kb.dma("sync", cst[:], cst_d[:, :], writes=[B["cst"]])
            cp(identb[:], ident, [B["cst"]], [B["identb"]], eng="vector")
            cp(onesb[:], ones, [B["cst"]], [B["onesb"]], eng="vector")
            for t_, nm in ((Shg, "Shg"), (Hst, "Hst"), (halo, "halo"), (Xc, "Xc"), (Shgb, "Shgb"), (Hstb, "Hstb")):
                A("gpsimd", lambda e, t_=t_: e.memset(t_[:], 0.0), (), [B[nm]])
            stgA = sb("stgA", [128, 128], F32)
            stgB = sb("stgB", [128, 128], F32)
            A("vector", lambda e: e.memset(stgA[:], 0.0), (), [B["stgA"]])
            A("vector", lambda e: e.memset(stgB[:], 0.0), (), [B["stgB"]])
            kb.dma("sync", stgA[0:96, :], norm_g.rearrange("l i (c p) -> (l i c) p", p=128), writes=[B["stgA"]])
            kb.dma("sync", stgB[0:64, :], conv_w.rearrange("l k (c p) -> (l k c) p", p=128), writes=[B["stgB"]])
            kb.dma("sync", stgB[64:80, :], conv_b.rearrange("l (c p) -> (l c) p", p=128), writes=[B["stgB"]])
            kb.dma("sync", stgB[80:88, :], s5_dsk.rearrange("l (c p) -> (l c) p", p=128), writes=[B["stgB"]])
            kb.dma("sync", stgB[88:96, :], glu_b.rearrange("l (c p) -> (l c) p", p=128), writes=[B["stgB"]])
            kb.dma("sync", stgB[96:104, :], lb_log.rearrange("l (c p) -> (l c) p", p=128), writes=[B["stgB"]])
            pp = ps("pp", [128, 256])
            tr(pp[:, 0:128], stgA[:], ident, [B["stgA"], B["cst"]], [B["pp"]])
            tr(pp[:, 128:256], stgB[:], ident, [B["stgB"], B["cst"]], [B["pp"]])
            cp(FP[:, 0:96], pp[:, 0:96], [B["pp"]], [B["FP"]])
            cp(FP[:, 96:96 + 104], pp[:, 128:128 + 104], [B["pp"]], [B["FP"]])
            for l in range(DEPTH):
                for f in range(2):
                    c0 = FP_G(l, 1 + 4 * f, 0)
                    ts(FP[:, FP_GS(l, f, 0):FP_GS(l, f, 0) + 8], FP[:, c0:c0 + 8], 0.5, None, ALU.mult, None, [B["FP"]], [B["FP"]])
            A("vector", lambda e: e.memset(FP[:, FP_LBV:FP_LBV + 4], 0.0), (), [B["FP"]])
            tt(FP[:, FP_LBV + 4:FP_LBV + 8], FP[:, FP_LB0 + 4:FP_LB0 + 8], FP[:, FP_LB0:FP_LB0 + 4], ALU.subtract, [B["FP"]], [B["FP"]])
            act(FP[:, FP_LBV + 4:FP_LBV + 8], FP[:, FP_LBV + 4:FP_LBV + 8], AF.Sigmoid, [B["FP"]], [B["FP"]])
            ts(FP[:, FP_OMLV:FP_OMLV + 8], FP[:, FP_LBV:FP_LBV + 8], -1.0, 1.0, ALU.mult, ALU.add, [B["FP"]], [B["FP"]])
            for l in range(DEPTH):
                kb.dma("sync", RP[:, RP_GN + l * 128:RP_GN + (l + 1) * 128], hg_gn[l:l + 1, :].broadcast_to([128, 128]), writes=[B["RP"]])
                kb.dma("sync", RP[:, RP_DTB + l * 8:RP_DTB + (l + 1) * 8], dt_bias[l:l + 1, :].broadcast_to([128, 8]), writes=[B["RP"]])
                kb.dma("sync", RP[:, RP_A + l * 8:RP_A + (l + 1) * 8], a_log[l:l + 1, :].broadcast_to([128, 8]), writes=[B["RP"]])
                kb.dma("sync", RP[:, RP_D + l * 8:RP_D + (l + 1) * 8], ssd_d[l:l + 1, :].broadcast_to([128, 8]), writes=[B["RP"]])
                kb.dma("sync", RP[:, RP_NW + l * 512:RP_NW + (l + 1) * 512], ssd_nw[l:l + 1, :].broadcast_to([128, 512]), writes=[B["RP"]])
            act(RP[:, RP_A:RP_A + 16], RP[:, RP_A:RP_A + 16], AF.Exp, [B["RP"]], [B["RP"]])
            ts(RP[:, RP_A:RP_A + 16], RP[:, RP_A:RP_A + 16], -1.0, None, ALU.mult, None, [B["RP"]], [B["RP"]])

            if "s5" in flags:
                TWO_PI = 2.0 * math.pi
                MAGIC = 12582912.0
                for l in range(depth):
                  with contextlib.ExitStack() as sl:
                    kb.barrier()

                    def sb(name, shape, dt, sl=sl):
                        return sl.enter_context(nc.sbuf_tensor(_u(name), list(shape), dt))

                    def ps(name, shape, dt=F32, sl=sl):
                        return _psum(nc, sl, name, list(shape), dt)
                    an = sb(f"an{l}", [32, 2, 64], F32)
                    kb.dma("sync", an[:, 0, :], s5_are[l], writes=[B["an"]])
                    kb.dma("sync", an[:, 1, :], s5_aim[l], writes=[B["an"]])
                    pa = ps(f"pa{l}", [64, 64])
                    tr(pa[:, 0:32], an[:, 0, :], ident[0:32, 0:32], [B["an"], B["cst"]], [B["psm"]])
                    tr(pa[:, 32:64], an[:, 1, :], ident[0:32, 0:32], [B["an"], B["cst"]], [B["psm"]])
                    aa = sb(f"aa{l}", [64, 2, 32], F32)
                    cp(aa[:, 0, :], pa[:, 0:32], [B["psm"]], [B["aa"]])
                    cp(aa[:, 1, :], pa[:, 32:64], [B["psm"]], [B["aa"]])
                    dl = sb(f"dl{l}", [64, 32], F32)
                    kb.dma("sync", dl[:], s5_ldt[l:l + 1, :].broadcast_to([64, 32]), writes=[B["dl"]])
                    act(dl[:], dl[:], AF.Exp, [B["dl"]], [B["dl"]])
                    ard = sb(f"ard{l}", [64, 32], F32)
                    t1 = sb(f"t1{l}", [64, 32], F32)
                    tt(ard[:], aa[:, 0, :], dl[:], ALU.mult, [B["aa"], B["dl"]], [B["ard"]])
                    tt(t1[:], aa[:, 1, :], dl[:], ALU.mult, [B["aa"], B["dl"]], [B["t1"]])
                    ts(t1[:], t1[:], 1.0 / TWO_PI, None, ALU.mult, None, [B["t1"]], [B["t1"]])

                    def frac_(dst, src, shift, bs):
                        tmpn = "fr_tmp"
                        shp = list(src.shape)
                        tmp = sb(f"frt{l}_{kb.eng['vector'].count}", shp, F32)
                        ts(tmp[:], src, shift, MAGIC, ALU.add, ALU.add, bs, [B[tmpn]])
                        ts(tmp[:], tmp[:], -MAGIC, None, ALU.add, None, [B[tmpn]], [B[tmpn]])
                        stt(dst, src, shift, tmp[:], ALU.add, ALU.subtract, bs + [B[tmpn]], [B["fr_dst"]])

                    jv = cst[0:64, C_JIDX:C_JIDX + 24]
                    mag = sb(f"mag{l}", [64, 32, 24], F32)
                    ang = sb(f"ang{l}", [64, 32, 24], F32)
                    tt(mag[:], ard[:].unsqueeze(2).broadcast_to([64, 32, 24]), jv.unsqueeze(1).broadcast_to([64, 32, 24]), ALU.mult, [B["ard"], B["cst"]], [B["mag"]])
                    act(mag[:], mag[:], AF.Exp, [B["mag"]], [B["mag"]])
                    t1r = sb(f"t1r{l}", [64, 32], F32)
                    frac_(t1r[:], t1[:], 0.0, [B["t1"]])
                    tt(ang[:], t1r[:].unsqueeze(2).broadcast_to([64, 32, 24]), jv.unsqueeze(1).broadcast_to([64, 32, 24]), ALU.mult, [B["fr_dst"], B["cst"]], [B["ang"]])
                    sj = sb(f"sj{l}", [64, 32, 24], F32)
                    cj = sb(f"cj{l}", [64, 32, 24], F32)
                    frac_(sj[:], ang[:], 0.0, [B["ang"]])
                    act(sj[:], sj[:], AF.Sin, [B["fr_dst"]], [B["sj"]], scale=TWO_PI)
                    frac_(cj[:], ang[:], 0.25, [B["ang"]])
                    act(cj[:], cj[:], AF.Sin, [B["fr_dst"]], [B["cj"]], scale=TWO_PI)
                    Lr = sb(f"Lr{l}", [64, 32, 24], F32)
                    Li = sb(f"Li{l}", [64, 32, 24], F32)
                    tt(Lr[:], mag[:], cj[:], ALU.mult, [B["mag"], B["cj"]], [B["Lr"]])
                    tt(Li[:], mag[:], sj[:], ALU.mult, [B["mag"], B["sj"]], [B["Li"]])
                    cp(rho8[:, l, :], mag[:, :, 23], [B["mag"]], [B["rho8"]], eng="vector")
                    kb.dma("sync", tabRho[l], rho8[:, l, :], reads=[B["rho8"]], writes=[B["tabRho"]])
                    fr = sb(f"fr{l}", [64, 32], F32)
                    fi = sb(f"fi{l}", [64, 32], F32)
                    nr = sb(f"nr{l}", [64, 32], F32)
                    den = sb(f"den{l}", [64, 32], F32)
                    tq = sb(f"tq{l}", [64, 32], F32)
                    ar_, ai_ = aa[:, 0, :], aa[:, 1, :]
                    lr1, li1 = Lr[:, :, 16], Li[:, :, 16]
                    ts(nr[:], lr1, -1.0, None, ALU.add, None, [B["Lr"]], [B["nr"]])
                    tt(den[:], ar_, ar_, ALU.mult, [B["aa"]], [B["den"]])
                    tt(tq[:], ai_, ai_, ALU.mult, [B["aa"]], [B["tq"]])
                    tt(den[:], den[:], tq[:], ALU.add, [B["den"], B["tq"]], [B["den"]])
                    recip(den[:], B["den"])
                    tt(fr[:], nr[:], ar_, ALU.mult, [B["nr"], B["aa"]], [B["fr"]])
                    tt(tq[:], li1, ai_, ALU.mult, [B["Li"], B["aa"]], [B["tq"]])
                    tt(fr[:], fr[:], tq[:], ALU.add, [B["fr"], B["tq"]], [B["fr"]])
                    tt(fr[:], fr[:], den[:], ALU.mult, [B["fr"], B["den"]], [B["fr"]])
                    tt(fi[:], li1, ar_, ALU.mult, [B["Li"], B["aa"]], [B["fi"]])
                    tt(tq[:], nr[:], ai_, ALU.mult, [B["nr"], B["aa"]], [B["tq"]])
                    tt(fi[:], fi[:], tq[:], ALU.subtract, [B["fi"], B["tq"]], [B["fi"]])
                    tt(fi[:], fi[:], den[:], ALU.mult, [B["fi"], B["den"]], [B["fi"]])
                    Bn = sb(f"Bn{l}", [64, 2, 32, 16], F32)
                    kb.dma("sync", Bn[:, 0], s5_bre[l].rearrange("g p c -> p g c"), writes=[B["Bn"]])
                    kb.dma("sync", Bn[:, 1], s5_bim[l].rearrange("g p c -> p g c"), writes=[B["Bn"]])
                    Cn = sb(f"Cn{l}", [64, 2, 32, 16], F32)
                    cnat = sb(f"cnat{l}", [128, 2, 4, 64], F32)
                    kb.dma("sync", cnat[:, 0], s5_cre[l].rearrange("(a g) c p -> (g c) a p", a=4), writes=[B["cnat"]])
                    kb.dma("sync", cnat[:, 1], s5_cim[l].rearrange("(a g) c p -> (g c) a p", a=4), writes=[B["cnat"]])
                    pc_ = ps(f"pc{l}", [64, 2, 512])
                    for ri in range(2):
                        for a4 in range(4):
                            tr(pc_[:, ri, a4 * 128:(a4 + 1) * 128], cnat[:, ri, a4, :], ident, [B["cnat"], B["cst"]], [B["pc"]])
                    cp(Cn[:].rearrange("p r g c -> p (r g c)"), pc_[:].rearrange("p r n -> p (r n)"), [B["pc"]], [B["Cn"]])
                    Bb = sb(f"Bb{l}", [64, 2, 32, 16], F32)
                    tw = sb(f"tw{l}", [64, 32, 16], F32)
                    frb = fr[:].unsqueeze(2).broadcast_to([64, 32, 16])
                    fib = fi[:].unsqueeze(2).broadcast_to([64, 32, 16])
                    tt(Bb[:, 0], frb, Bn[:, 0], ALU.mult, [B["fr"], B["Bn"]], [B["Bb"]])
                    tt(tw[:], fib, Bn[:, 1], ALU.mult, [B["fi"], B["Bn"]], [B["tw"]])
                    tt(Bb[:, 0], Bb[:, 0], tw[:], ALU.subtract, [B["Bb"], B["tw"]], [B["Bb"]])
                    tt(Bb[:, 1], frb, Bn[:, 1], ALU.mult, [B["fr"], B["Bn"]], [B["Bb"]])
                    tt(tw[:], fib, Bn[:, 0], ALU.mult, [B["fi"], B["Bn"]], [B["tw"]])
                    tt(Bb[:, 1], Bb[:, 1], tw[:], ALU.add, [B["Bb"], B["tw"]], [B["Bb"]])

                    NG = 8
                    for g0 in range(0, 32, NG):
                        with contextlib.ExitStack() as s2:
                            def sb2(name, shape, dt):
                                return s2.enter_context(nc.sbuf_tensor(_u(name), list(shape), dt))
                            kb.barrier()
                            PBt = sb2(f"PBt{l}_{g0}", [64, 2, NG, 128], F32)
                            PBr = sb2(f"PBr{l}_{g0}", [64, 2, NG, 128], F32)
                            PC = sb2(f"PC{l}_{g0}", [64, 2, NG, 128], F32)
                            Tst = sb2(f"Tst{l}_{g0}", [128, NG, 128], BF16)
                            Rst = sb2(f"Rst{l}_{g0}", [128, 2, NG, 64], BF16)
                            Ost = sb2(f"Ost{l}_{g0}", [64, 2, NG, 128], BF16)
                            v4 = lambda t_, r: t_[:, r].rearrange("p g (s c) -> p g s c", c=16)
                            def cprod2(dst_re, dst_im, j0, M, neg_im, bname):
                                lr = Lr[:, g0:g0 + NG, j0:j0 + 8].unsqueeze(3).broadcast_to([64, NG, 8, 16])
                                li = Li[:, g0:g0 + NG, j0:j0 + 8].unsqueeze(3).broadcast_to([64, NG, 8, 16])
                                mr = M[:, 0, g0:g0 + NG, :].unsqueeze(2).broadcast_to([64, NG, 8, 16])
                                mi = M[:, 1, g0:g0 + NG, :].unsqueeze(2).broadcast_to([64, NG, 8, 16])
                                t2 = sb2(f"cpt{l}_{bname}_{g0}", [64, NG, 8, 16], F32)
                                rds = [B["Lr"], B["Li"], B["Bb"], B["Cn"]]
                                tt(dst_re, lr, mr, ALU.mult, rds, [B[bname]])
                                tt(t2[:], li, mi, ALU.mult, rds, [B[bname + "t"]])
                                tt(dst_re, dst_re, t2[:], ALU.subtract, [B[bname], B[bname + "t"]], [B[bname]])
                                tt(dst_im, lr, mi, ALU.mult, rds, [B[bname + "i"]])
                                tt(t2[:], li, mr, ALU.mult, rds, [B[bname + "t"]])
                                if neg_im:
                                    stt(dst_im, dst_im, -1.0, t2[:], ALU.mult, ALU.subtract, [B[bname + "i"], B[bname + "t"]], [B[bname + "i"]])
                                else:
                                    tt(dst_im, dst_im, t2[:], ALU.add, [B[bname + "i"], B[bname + "t"]], [B[bname + "i"]])
                            cprod2(v4(PBt, 0), v4(PBt, 1), 0, Bb, False, "PBt")
                            cprod2(v4(PBr, 0), v4(PBr, 1), 8, Bb, False, "PBr")
                            cprod2(v4(PC, 0), v4(PC, 1), 16, Cn, True, "PC")
                            ptt = _psum(nc, s2, f"ptt{l}_{g0}", [128, 2, 512], F32)
                            prr = _psum(nc, s2, f"prr{l}_{g0}", [128, 2, 512], F32)
                            for gg in range(NG):
                                bk, of = gg // 4, (gg % 4) * 128
                                mm(ptt[:, bk, of:of + 128], PBt[:, 0, gg, :], PC[:, 0, gg, :], True, False, [B["PBt"], B["PC"]], [B["ptt", bk]])
                                mm(ptt[:, bk, of:of + 128], PBt[:, 1, gg, :], PC[:, 1, gg, :], False, True, [B["PBti"], B["PCi"]], [B["ptt", bk]])
                            for bk in range(2):
                                tt(Tst[:, bk * 4:(bk + 1) * 4, :], ptt[:, bk, :].rearrange("p (g n) -> p g n", n=128),
                                   ms5.unsqueeze(1).broadcast_to([128, 4, 128]), ALU.mult, [B["ptt", bk], B["cst"]], [B["Tst"]])
                            for gg in range(NG):
                                for ri in range(2):
                                    mm(prr[:, ri, gg * 64:(gg + 1) * 64], PBr[:, ri, gg, :], ident[0:64, 0:64], True, True,
                                       [B["PBr"], B["PBri"], B["cst"]], [B["prr", ri]])
                            for ri in range(2):
                                cp(Rst[:, ri].rearrange("p g n -> p (g n)"), prr[:, ri, :], [B["prr", ri]], [B["Rst"]])
                            cp(Ost[:].rearrange("p r g n -> p (r g n)"), PC[:].rearrange("p r g n -> p (r g n)"), [B["PC"], B["PCi"]], [B["Ost"]], eng="vector")
                            kb.dma("sync", tabT[l, :, g0 * 128:(g0 + NG) * 128], Tst[:].rearrange("p g n -> p (g n)"), reads=[B["Tst"]], writes=[B["tabT"]])
                            for ri in range(2):
                                kb.dma("sync", tabR[l, ri, :, g0 * 64:(g0 + NG) * 64], Rst[:, ri].rearrange("p g n -> p (g n)"), reads=[B["Rst"]], writes=[B["tabR"]])
                                kb.dma("sync", tabO[l, ri, :, g0 * 128:(g0 + NG) * 128], Ost[:, ri].rearrange("p g n -> p (g n)"), reads=[B["Ost"]], writes=[B["tabO"]])
                            t8 = sb2(f"t8{l}_{g0}", [64, NG], F32)
                            t8r = sb2(f"t8r{l}_{g0}", [64, NG], F32)
                            ts(t8[:], t1r[:, g0:g0 + NG], 8.0, None, ALU.mult, None, [B["fr_dst"]], [B["t8"]])
                            tmp8 = sb2(f"tmp8{l}_{g0}", [64, NG], F32)
                            ts(tmp8[:], t8[:], MAGIC, None, ALU.add, None, [B["t8"]], [B["tmp8"]])
                            ts(tmp8[:], tmp8[:], -MAGIC, None, ALU.add, None, [B["tmp8"]], [B["tmp8"]])
                            tt(t8r[:], t8[:], tmp8[:], ALU.subtract, [B["t8"], B["tmp8"]], [B["t8r"]])
                            an2 = sb2(f"an2{l}_{g0}", [64, NG, 129], F32)
                            nv = cst[0:64, C_NIDX:C_NIDX + 129]
                            tt(an2[:], t8r[:].unsqueeze(2).broadcast_to([64, NG, 129]), nv.unsqueeze(1).broadcast_to([64, NG, 129]), ALU.mult, [B["t8r"], B["cst"]], [B["an2"]])
                            cs2 = sb2(f"cs2{l}_{g0}", [64, 2, NG, 129], F32)
                            tm2 = sb2(f"tm2{l}_{g0}", [64, NG, 129], F32)
                            for ri, shift in ((0, 0.25), (1, 0.0)):
                                ts(tm2[:], an2[:], shift, MAGIC, ALU.add, ALU.add, [B["an2"]], [B["tm2"]])
                                ts(tm2[:], tm2[:], -MAGIC, None, ALU.add, None, [B["tm2"]], [B["tm2"]])
                                stt(cs2[:, ri], an2[:], shift, tm2[:], ALU.add, ALU.subtract, [B["an2"], B["tm2"]], [B["cs2"]])
                                act(cs2[:, ri], cs2[:, ri], AF.Sin, [B["cs2"]], [B["cs2"]], scale=TWO_PI)
                                kb.dma("sync", tabCS[l, ri, :, g0 * 129:(g0 + NG) * 129], cs2[:, ri].rearrange("p g n -> p (g n)"), reads=[B["cs2"]], writes=[B["tabCS"]])
                    kb.barrier()
        kb.barrier()
        if "s5" in flags:
            for l in range(depth):
                for dq in range(2):
                    for h in range(2):
                        kb.dma("sync", rho8s[h * 64:(h + 1) * 64, l, dq * 8:(dq + 1) * 8], tabRho[l, :, dq * 16 + h * 8:dq * 16 + h * 8 + 8],
                               reads=[B["tabRho"]], writes=[B["rho8s"]])

        def rms_stats(src_ap, src_bufs, sq8, ssp, rstd, tag):
            act(sq8[:], src_ap, AF.Square, list(src_bufs), [B[tag, "sq8"]])
            for c in range(DC):
                mm(ssp[:], onesb[:], sq8[:, c, :], c == 0, c == DC - 1, [B["onesb"], B[tag, "sq8"]], [B[tag, "ssp"]])
            act(rstd[:], ssp[:], AF.Ln, [B[tag, "ssp"]], [B[tag, "rstd"]], scale=1.0 / D, bias=EPS)
            act(rstd[:], rstd[:], AF.Exp, [B[tag, "rstd"]], [B[tag, "rstd"]], scale=-0.5)

        def prenorm(l, i, xn, sq8, ssp, rstd):
            for blk in range(NBLK):
                cs = slice(blk * 512, (blk + 1) * 512)
                rms_stats(hT[:, :, cs], [B["h", c, blk] for c in range(DC)], sq8, ssp, rstd, "pre")
                for c in range(DC):
                    stt(xn[:, c, cs], hT[:, c, cs], fpc(FP_G(l, i, c)), rstd[:], ALU.mult, ALU.mult,
                        [B["h", c, blk], B["FP"], B["pre", "rstd"]], [B["xn", c, blk]])

        def post_residual(gcol_fn, blk, ysb, sq8, ssp, rstd, tmpt):
            cs = slice(blk * 512, (blk + 1) * 512)
            rms_stats(ysb[:], [B["ysb", c] for c in range(DC)], sq8, ssp, rstd, "post")
            for m in range(DC):
                t = tmpt[m % 2]
                stt(t[:], ysb[:, m, :], fpc(gcol_fn(m)), rstd[:], ALU.mult, ALU.mult, [B["ysb", m], B["FP"], B["post", "rstd"]], [B["ptmp", m % 2], B["sg", m % 2]])
                tt(hT[:, m, cs], hT[:, m, cs], t[:], ALU.add, [B["h", m, blk], B["ptmp", m % 2]], [B["h", m, blk]])

        def ffn_phase(l, f):
            kb.barrier()
            with contextlib.ExitStack() as sc:
                def sb(name, shape, dt):
                    return sc.enter_context(nc.sbuf_tensor(_u(name), list(shape), dt))

                def ps(name, shape, dt=F32):
                    return _psum(nc, sc, name, list(shape), dt)
                xn = sb("xn", [128, DC, TS], BF16)
                actb = sb("actb", [128, FC, TS], BF16)
                wd = sb("wd", [128, FC, D], BF16)
                NW = 2
                wg = [sb(f"wg{i}", [128, DC, 256], BF16) for i in range(NW)]
                wu = [sb(f"wu{i}", [128, DC, 256], BF16) for i in range(NW)]
                ysb = sb("ysb", [128, DC, 512], BF16)
                sqt = sb("sq8", [128, DC, 512], BF16)
                rstd = sb("rstd", [128, 512], F32)
                sgt = [sb(f"sg{i}", [128, 512], F32) for i in range(2)]
                tmpt = sgt
                ssp = ps("ssp", [128, 512])
                pg = [ps(f"pg{i}", [128, 512]) for i in range(2)]
                pu = [ps(f"pu{i}", [128, 512]) for i in range(2)]
                py = [ps(f"py{i}", [128, 512]) for i in range(2)]

                prenorm(l, 4 * f, xn, sqt, ssp, rstd)
                wgv = w_gate[l, f].rearrange("(c p) n -> p c n", p=128)
                wuv = w_up[l, f].rearrange("(c p) n -> p c n", p=128)
                wdv = w_down[l, f].rearrange("(j p) n -> p j n", p=128)
                cnt = 0
                wd_next = 0
                for jg in range(FC // 2):
                    s = jg % NW
                    if jg == 0:
                        for cq in range(4):
                            kb.dma("gpsimd", wg[s][:, 2 * cq:2 * cq + 2, :], wgv[:, 2 * cq:2 * cq + 2, 0:256], writes=[B["wg", s, cq]])
                        for cq in range(4):
                            kb.dma("gpsimd", wu[s][:, 2 * cq:2 * cq + 2, :], wuv[:, 2 * cq:2 * cq + 2, 0:256], writes=[B["wu", s, cq]])
                    else:
                        kb.dma("gpsimd", wg[s][:], wgv[:, :, jg * 256:(jg + 1) * 256], writes=[B["wg", s, cq] for cq in range(4)])
                        kb.dma("gpsimd", wu[s][:], wuv[:, :, jg * 256:(jg + 1) * 256], writes=[B["wu", s, cq] for cq in range(4)])
                    if jg >= 1:
                        for _ in range(3):
                            if wd_next < FC:
                                kb.dma("gpsimd", wd[:, wd_next, :], wdv[:, wd_next, :], writes=[B["wd", wd_next]])
                                wd_next += 1
                    for blk in range(NBLK):
                        cs = slice(blk * 512, (blk + 1) * 512)
                        for jj in range(2):
                            j = jg * 2 + jj
                            k = cnt % 2
                            cnt += 1
                            for c in range(DC):
                                mm(pg[k][:], wg[s][:, c, jj * 128:(jj + 1) * 128], xn[:, c, cs], c == 0, c == DC - 1,
                                   [B["wg", s, c // 2], B["xn", c, blk]], [B["pg", k]])
                            for c in range(DC):
                                mm(pu[k][:], wu[s][:, c, jj * 128:(jj + 1) * 128], xn[:, c, cs], c == 0, c == DC - 1,
                                   [B["wu", s, c // 2], B["xn", c, blk]], [B["pu", k]])
                            act(sgt[k][:], pg[k][:], AF.Silu, [B["pg", k]], [B["sg", k]])
                            tt(actb[:, j, cs], sgt[k][:], pu[k][:], ALU.mult, [B["sg", k], B["pu", k]], [B["actb", j, blk]])
                while wd_next < FC:
                    kb.dma("gpsimd", wd[:, wd_next, :], wdv[:, wd_next, :], writes=[B["wd", wd_next]])
                    wd_next += 1
                cnt = 0
                for blk in range(NBLK):
                    cs = slice(blk * 512, (blk + 1) * 512)
                    for m in range(DC):
                        k = cnt % 2
                        cnt += 1
                        for j in range(FC):
                            mm(py[k][:], wd[:, j, m * 128:(m + 1) * 128], actb[:, j, cs], j == 0, j == FC - 1,
                               [B["wd", j], B["actb", j, blk]], [B["py", k]])
                        cp(ysb[:, m, :], py[k][:], [B["py", k]], [B["ysb", m]])
                    post_residual(lambda m: FP_GS(l, f, m), blk, ysb, sqt, ssp, rstd, tmpt)

        def mixer_phase(l, seg):
            kb.barrier()
            with contextlib.ExitStack() as sc:
                def sb(name, shape, dt):
                    return sc.enter_context(nc.sbuf_tensor(_u(name), list(shape), dt))

                def ps(name, shape, dt=F32):
                    return _psum(nc, sc, name, list(shape), dt)
                xn = sb("xn", [128, DC, TS], BF16)
                oT = sb("oT", [128, 12, TS], BF16)
                NW = 2
                wsl = [sb(f"wsl{i}", [128, DC, 512], BF16) for i in range(NW)]
                wstate = {"n": 0}
                winv = w_in[l].rearrange("(c p) n -> p c n", p=128)

                def wload(col0, ncols=512):
                    s = wstate["n"] % NW
                    wstate["n"] += 1
                    for cq in range(4):
                        kb.dma("gpsimd", wsl[s][:, 2 * cq:2 * cq + 2, 0:ncols], winv[:, 2 * cq:2 * cq + 2, col0:col0 + ncols], writes=[B["wsl", s, cq]])
                    return s

                pre = {}
                if "s5" in flags:
                    pre["u"] = wload(3592)
                with contextlib.ExitStack() as s0:
                    sqt = s0.enter_context(nc.sbuf_tensor(_u("sq8"), [128, DC, 512], BF16))
                    rstd = s0.enter_context(nc.sbuf_tensor(_u("rstd"), [128, 512], F32))
                    ssp = _psum(nc, s0, "ssp", [128, 512], F32)
                    prenorm(l, 2, xn, sqt, ssp, rstd)
                    kb.barrier()

                if not all(k in flags for k in ("hg", "ssd", "s5")):
                    A("vector", lambda e: e.memset(oT[:], 0.0), (), [B["oT", j] for j in range(12)])

                if "s5" in flags:
                    with contextlib.ExitStack() as s5:
                        def sb5(name, shape, dt):
                            return s5.enter_context(nc.sbuf_tensor(_u(name), list(shape), dt))

                        def ps5(name, shape, dt=F32):
                            return _psum(nc, s5, name, list(shape), dt)
                        uS = sb5("uS", [128, 4, 8, 128], F32)
                        U = sb5("U", [128, 32, 128], BF16)
                        gw = sb5("gw", [128, 4, 512], BF16)
                        pu_ = [ps5(f"pu5{i}", [128, 512]) for i in range(2)]
                        kb.dma("gpsimd", gw[:], glu_w[l].rearrange("(c p) n -> p c n", p=128), writes=[B["gw"]])
                        s = pre["u"]
                        k = 0
                        for ch in range(4):
                            for blk in range(NBLK):
                                pst = pu_[k % 2]
                                pb = B["pu5", k % 2]
                                cs = slice(blk * 512, (blk + 1) * 512)
                                for c in range(DC):
                                    mm(pst[:], wsl[s][:, c, ch * 128:(ch + 1) * 128], xn[:, c, cs], c == 0, c == DC - 1,
                                       [B["wsl", s, c // 2], B["xn", c, blk]], [pb])
                                cp(uS[:, ch, :, blk * 64:(blk + 1) * 64], pst[:].rearrange("p (n s) -> p s n", s=8), [pb], [B["uS", ch]])
                                k += 1
                        if "hg" in flags:
                            pre["f"] = wload(512)
                            pre["q"] = wload(0)
                        for ch in range(4):
                            kb.dma("gpsimd", ud[ch * 128:(ch + 1) * 128, :], uS[:, ch].rearrange("p s n -> p (s n)"), reads=[B["uS", ch]], writes=[B["ud"]])
                        udv = ud.rearrange("(g c) (s n) -> s c g n", c=16, s=8)
                        for s_ in range(8):
                            kb.dma("sync", U[s_ * 16:(s_ + 1) * 16, :, :], udv[s_], reads=[B["ud"]], writes=[B["U"]])
                        ydv = yd.rearrange("(g c) (t n) -> t c g n", c=16, t=8)
                        with contextlib.ExitStack() as s6:
                            def sb6(name, shape, dt):
                                return s6.enter_context(nc.sbuf_tensor(_u(name), list(shape), dt))
                            kb.barrier()
                            cs2 = sb6("cs2m", [128, 2, 8, 129], F32)
                            Tm = sb6("Tm", [128, 16, 128], BF16)
                            Rm = sb6("Rm", [128, 2, 16, 64], BF16)
                            Om = sb6("Om", [128, 2, 8, 128], BF16)
                            Ysb = sb6("Ysb", [128, 16, 128], F32)
                            Eq = sb6("Eq", [128, 2, 8, 128], F32)
                            Zq = sb6("Zq", [128, 2, 8, 129], F32)
                            Xq = Zq
                            Vq = sb6("Vq", [128, 2, 8, 129], F32)
                            Xb = sb6("Xb", [128, 2, 8, 128], BF16)
                            tq_ = sb6("tq5", [128, 8, 129], F32)
                            pe_ = [_psum(nc, s6, f"pe5{i}", [128, 2, 512], F32) for i in range(2)]
                            py_ = [_psum(nc, s6, f"py5{i}", [128, 512], F32) for i in range(2)]

                            def load_tabs(dq_):
                                g0_ = dq_ * 16
                                kb.dma("sync", Tm[:].rearrange("p g n -> p (g n)"), tabT[l, :, g0_ * 128:(g0_ + 16) * 128], reads=[B["tabT"]], writes=[B["Tm"]])
                                for ri in range(2):
                                    kb.dma("sync", Rm[:, ri].rearrange("p g n -> p (g n)"), tabR[l, ri, :, g0_ * 64:(g0_ + 16) * 64], reads=[B["tabR"]], writes=[B["Rm"]])
                                    for h in range(2):
                                        gh = g0_ + h * 8
                                        kb.dma("sync", Om[h * 64:(h + 1) * 64, ri].rearrange("p g n -> p (g n)"), tabO[l, ri, :, gh * 128:(gh + 8) * 128],
                                               reads=[B["tabO"]], writes=[B["Om"]])

                            def load_cs(dq_):
                                for ri in range(2):
                                    for h in range(2):
                                        gh = dq_ * 16 + h * 8
                                        kb.dma("sync", cs2[h * 64:(h + 1) * 64, ri].rearrange("p g n -> p (g n)"), tabCS[l, ri, :, gh * 129:(gh + 8) * 129],
                                               reads=[B["tabCS"]], writes=[B["cs2m"]])
                            load_tabs(0)
                            load_cs(0)
                            for dq in range(2):
                                G0 = dq * 16
                                XS = slice(dq * 8, (dq + 1) * 8)
                                if dq > 0:
                                    load_tabs(dq)
                                for hb in range(2):
                                    for h in range(2):
                                        hp = slice(h * 64, (h + 1) * 64)
                                        for gg in range(4):
                                            idx = hb * 4 + gg
                                            g = G0 + h * 8 + idx
                                            for ri in range(2):
                                                mm(pe_[hb][hp, ri, gg * 128:(gg + 1) * 128], Rm[:, ri, h * 8 + idx, :], U[:, g, :], True, True,
                                                   [B["Rm"], B["U"]], [B["pe5", hb]])
                                    cp(Eq[:, :, hb * 4:(hb + 1) * 4, :], pe_[hb][:].rearrange("p r (g n) -> p r g n", n=128), [B["pe5", hb]], [B["Eq"]])
                                cc = cs2[:, 0, :, 1:129]
                                ss_ = cs2[:, 1, :, 1:129]
                                Er, Ei = Eq[:, 0], Eq[:, 1]
                                t128 = tq_[:, :, 0:128]
                                tt(Zq[:, 0, :, 0:128], cc, Er, ALU.mult, [B["cs2m"], B["Eq"]], [B["Zq0"]])
                                tt(t128, ss_, Ei, ALU.mult, [B["cs2m"], B["Eq"]], [B["tq5"]])
                                tt(Zq[:, 0, :, 0:128], Zq[:, 0, :, 0:128], t128, ALU.add, [B["Zq0"], B["tq5"]], [B["Zq0"]])
                                tt(Zq[:, 1, :, 0:128], cc, Ei, ALU.mult, [B["cs2m"], B["Eq"]], [B["Zq1"]])
                                tt(t128, ss_, Er, ALU.mult, [B["cs2m"], B["Eq"]], [B["tq5"]])
                                tt(Zq[:, 1, :, 0:128], Zq[:, 1, :, 0:128], t128, ALU.subtract, [B["Zq1"], B["tq5"]], [B["Zq1"]])
                                for ri in range(2):
                                    cp(Vq[:, ri, :, 0], Xc[:, l, ri, XS], [B["Xc"]], [B["Vq", ri]], eng="vector")
                                    for idx in range(8):
                                        c_ = dq * 8 + idx
                                        scan(Vq[:, ri, idx, 1:129], rho8s[:, l, c_:c_ + 1].broadcast_to([128, 128]),
                                             Zq[:, ri, idx, 0:128], Xc[:, l, ri, c_:c_ + 1],
                                             [B["rho8s"], B["Zq0"], B["Zq1"], B["Xc"]], [B["Vq", ri]])
                                ca = cs2[:, 0]
                                sa = cs2[:, 1]
                                tt(Xq[:, 0], ca, Vq[:, 0], ALU.mult, [B["cs2m"], B["Vq", 0]], [B["Xq0"], B["Zq0"], B["Zq1"]])
                                tt(tq_[:], sa, Vq[:, 1], ALU.mult, [B["cs2m"], B["Vq", 1]], [B["tq5"]])
                                tt(Xq[:, 0], Xq[:, 0], tq_[:], ALU.subtract, [B["Xq0"], B["tq5"]], [B["Xq0"]])
                                tt(Xq[:, 1], ca, Vq[:, 1], ALU.mult, [B["cs2m"], B["Vq", 1]], [B["Xq1"], B["Zq0"], B["Zq1"]])
                                tt(tq_[:], sa, Vq[:, 0], ALU.mult, [B["cs2m"], B["Vq", 0]], [B["tq5"]])
                                tt(Xq[:, 1], Xq[:, 1], tq_[:], ALU.add, [B["Xq1"], B["tq5"]], [B["Xq1"]])
                                for ri in range(2):
                                    cp(Xb[:, ri], Xq[:, ri, :, 0:128], [B["Xq0"], B["Xq1"]], [B["Xb"]])
                                    cp(Xc[:, l, ri, XS], Xq[:, ri, :, 128], [B["Xq0"], B["Xq1"], B["Vq", 0], B["Vq", 1]], [B["Xc"]], eng="vector")
                                if dq + 1 < 2:
                                    load_cs(dq + 1)
                                for quad in range(4):
                                    pyq = py_[quad % 2]
                                    bq = B["py5", quad % 2]
                                    for gg in range(4):
                                        gl16 = quad * 4 + gg
                                        h, idx = gl16 // 8, gl16 % 8
                                        hp = slice(h * 64, (h + 1) * 64)
                                        g = G0 + gl16
                                        o_ = pyq[:, gg * 128:(gg + 1) * 128]
                                        mm(o_, Tm[:, gl16, :], U[:, g, :], True, False, [B["Tm"], B["U"]], [bq])
                                        mm(o_, Om[hp, 0, idx, :], Xb[hp, 0, idx, :], False, False, [B["Om"], B["Xb"]], [bq])
                                        mm(o_, Om[hp, 1, idx, :], Xb[hp, 1, idx, :], False, True, [B["Om"], B["Xb"]], [bq])
                                    cp(Ysb[:, quad * 4:quad * 4 + 4, :], pyq[:].rearrange("p (g n) -> p g n", n=128), [bq], [B["Ysb"]])
                                for t_ in range(8):
                                    kb.dma("sync", ydv[t_][:, G0:G0 + 16, :], Ysb[t_ * 16:(t_ + 1) * 16, :, :], reads=[B["Ysb"]], writes=[B["yd"]])
                        kb.barrier()
                        with contextlib.ExitStack() as s7:
                            def sb7(name, shape, dt):
                                return s7.enter_context(nc.sbuf_tensor(_u(name), list(shape), dt))
                            yS = sb7("yS", [128, 4, TS], F32)
                            y2 = sb7("y2", [128, 4, TS], F32)
                            y2b = sb7("y2b", [128, 4, TS], BF16)
                            t5 = [sb7(f"t5{i}", [128, TS], F32) for i in range(2)]
                            sgl = [sb7(f"sgl{i}", [128, 512], F32) for i in range(2)]
                            for ch in range(4):
                                kb.dma("sync", yS[:, ch, :], yd[ch * 128:(ch + 1) * 128, :], reads=[B["yd"]], writes=[B["yS", ch]])
                            KG = 2.0 * math.sqrt(2.0 / math.pi)
                            for ch in range(4):
                                t = t5[ch % 2]
                                tb = B["t5", ch % 2]
                                y1 = yS[:, ch, :]
                                stt(y1, uS[:, ch].rearrange("p s n -> p (s n)"), fpc(FP_S5D(l, ch)), y1, ALU.mult, ALU.add,
                                    [B["uS", ch], B["FP"], B["yS", ch]], [B["yS", ch]])
                                tt(t[:], y1, y1, ALU.mult, [B["yS", ch]], [tb])
                                ts(t[:], t[:], 0.044715, 1.0, ALU.mult, ALU.add, [tb], [tb])
                                tt(t[:], t[:], y1, ALU.mult, [tb, B["yS", ch]], [tb])
                                act(t[:], t[:], AF.Sigmoid, [tb], [tb], scale=KG)
                                tt(y2[:, ch, :], y1, t[:], ALU.mult, [B["yS", ch], tb], [B["y2", ch]])
                                cp(y2b[:, ch, :], y2[:, ch, :], [B["y2", ch]], [B["y2b", ch]])
                            k = 0
                            for oc in range(4):
                                for blk in range(NBLK):
                                    cs = slice(blk * 512, (blk + 1) * 512)
                                    pst = pu_[k % 2]
                                    for ic in range(4):
                                        mm(pst[:], gw[:, ic, oc * 128:(oc + 1) * 128], y2b[:, ic, cs], ic == 0, ic == 3,
                                           [B["gw"], B["y2b", ic]], [B["pu5", k % 2]])
                                    sg = sgl[k % 2]
                                    act(sg[:], pst[:], AF.Sigmoid, [B["pu5", k % 2]], [B["sgl", k % 2]], bias=fpc(FP_GLUB(l, oc)))
                                    o_ = oT[:, 8 + oc, :].rearrange("p (n t) -> p t n", t=8)[:, blk * 4:(blk + 1) * 4, :]
                                    tt(o_, y2[:, oc, cs].rearrange("p (t n) -> p t n", n=128), sg[:].rearrange("p (t n) -> p t n", n=128), ALU.mult,
                                       [B["y2", oc], B["sgl", k % 2]], [B["oT", 8 + oc]])
                                    k += 1
                        kb.barrier()

                if "hg" in flags:
                    with contextlib.ExitStack() as s1:
                        def sb1(name, shape, dt):
                            return s1.enter_context(nc.sbuf_tensor(_u(name), list(shape), dt))

                        def ps1(name, shape, dt=F32):
                            return _psum(nc, s1, name, list(shape), dt)
                        qtT = sb1("qtT", [128, 4, TS], BF16)
                        ktT = sb1("ktT", [128, 4, TS], BF16)
                        elast = sb1("elast", [128, 4, TS // 64], F32)
                        vtok = sb1("vtok", [128, NT, 512], BF16)
                        wgt = sb1("wgt", [128, NT, 512], F32)
                        fT2 = [sb1(f"fT{i}", [128, TS], F32) for i in range(2)]
                        lf2 = [sb1(f"lf{i}", [128, TS], F32) for i in range(2)]
                        bb2 = [sb1(f"bb{i}", [128, TS], F32) for i in range(2)]
                        qs2 = [sb1(f"qs{i}", [128, TS], F32) for i in range(2)]
                        pq = [ps1(f"pq{i}", [128, 512]) for i in range(2)]
                        sf = pre["f"] if "f" in pre else wload(512)
                        sq_ = pre["q"] if "q" in pre else wload(0)
                        kqc = [0]

                        def hg_prep_step(step, hd, S):
                            fT, lf, bb, qs = fT2[S], lf2[S], bb2[S], qs2[S]
                            bf, bl, bbb, bq = B["fT", S], B["lf", S], B["bb", S], B["qs", S]
                            if step == 0:
                                for blk in range(NBLK):
                                    cs = slice(blk * 512, (blk + 1) * 512)
                                    pst = pq[kqc[0] % 2]
                                    pb = B["pq", kqc[0] % 2]
                                    kqc[0] += 1
                                    for c in range(DC):
                                        mm(pst[:], wsl[sf][:, c, hd * 128:(hd + 1) * 128], xn[:, c, cs], c == 0, c == DC - 1,
                                           [B["wsl", sf, c // 2], B["xn", c, blk]], [pb])
                                    act(fT[:, cs], pst[:], AF.Sigmoid, [pb], [bf])
                            elif step == 1:
                                ts(fT[:], fT[:], fpc(FP_OML(l, hd)), fpc(FP_LB(l, hd)), ALU.mult, ALU.add, [bf, B["FP"]], [bf])
                            elif step == 2:
                                act(lf[:], fT[:], AF.Ln, [bf], [bl])
                            elif step == 3:
                                scan(bb[:], rstm, lf[:], 0.0, [B["cst"], bl], [bbb])
                            elif step == 4:
                                act(lf[:], bb[:], AF.Exp, [bbb], [bl])
                                act(bb[:], bb[:], AF.Exp, [bbb], [bbb], scale=-1.0)
                            elif step == 5:
                                ts(fT[:], fT[:], -1.0, 1.0, ALU.mult, ALU.add, [bf], [bf])
                                tt(ktT[:, hd, :], fT[:], bb[:], ALU.mult, [bf, bbb], [B["ktT", hd]])
                                cp(elast[:, hd, :], lf[:].rearrange("p (n c) -> p n c", c=64)[:, :, 63], [bl], [B["elast"]], eng="vector")
                            elif step == 6:
                                for blk in range(NBLK):
                                    cs = slice(blk * 512, (blk + 1) * 512)
                                    pst = pq[kqc[0] % 2]
                                    pb = B["pq", kqc[0] % 2]
                                    kqc[0] += 1
                                    for c in range(DC):
                                        mm(pst[:], wsl[sq_][:, c, hd * 128:(hd + 1) * 128], xn[:, c, cs], c == 0, c == DC - 1,
                                           [B["wsl", sq_, c // 2], B["xn", c, blk]], [pb])
                                    act(qs[:, cs], pst[:], AF.Silu, [pb], [bq])
                            elif step == 7:
                                tt(qtT[:, hd, :], qs[:], lf[:], ALU.mult, [bq, bl], [B["qtT", hd]])

                        for hp in range(2):
                            for step in range(8):
                                for S in range(2):
                                    hg_prep_step(step, hp * 2 + S, S)
                        kq = kqc[0]
                        si = wload(1024)
                        sg_ = wload(1536)
                        gnb = RP[:, RP_GN + l * 128:RP_GN + (l + 1) * 128].unsqueeze(1).broadcast_to([128, 4, 128])
                        for it in range(NT):
                            blk = it // 4
                            pst = pq[kq % 2]
                            pb = B["pq", kq % 2]
                            kq += 1
                            for c in range(DC):
                                mm(pst[:], xn[:, c, it * 128:(it + 1) * 128], wsl[si][:, c, :], c == 0, c == DC - 1,
                                   [B["wsl", si, c // 2], B["xn", c, blk]], [pb])
                            cp(vtok[:, it, :], pst[:], [pb], [B["vtok", it]])
                            pst = pq[kq % 2]
                            pb = B["pq", kq % 2]
                            kq += 1
                            for c in range(DC):
                                mm(pst[:], xn[:, c, it * 128:(it + 1) * 128], wsl[sg_][:, c, :], c == 0, c == DC - 1,
                                   [B["wsl", sg_, c // 2], B["xn", c, blk]], [pb])
                            act(wgt[:, it, :], pst[:], AF.Silu, [pb], [B["wgt", it]])
                            tt(wgt[:, it, :].rearrange("p (h v) -> p h v", v=128), wgt[:, it, :].rearrange("p (h v) -> p h v", v=128), gnb, ALU.mult,
                               [B["wgt", it], B["RP"]], [B["wgt", it]])
                        if "ssd" in flags:
                            pre["x0"] = wload(2560)
                            pre["x1"] = wload(3072)
                        ptk = ps1("ptk", [128, 4, 128], BF16)
                        pss = ps1("pss", [128, 4, 128])
                        pso = ps1("pso", [128, 512])
                        psS = ps1("psS", [128, 8, 128])
                        pto = ps1("pto", [128, 4, 128], BF16)
                        ktok2 = [sb1(f"ktok{i}", [128, 4, 128], BF16) for i in range(2)]
                        smk2 = [sb1(f"smk{i}", [128, 4, 128], BF16) for i in range(2)]
                        stmp = sb1("stmp", [128, 4, 128], F32)
                        ssq = sb1("ssq", [128, 4], F32)
                        junk = sb1("junk", [128, 128], F32)
                        og = sb1("og", [128, 512], BF16)

                        def hg_s1(it):
                            k2 = it % 2
                            tc_ = slice(it * 128, (it + 1) * 128)
                            for hd in range(4):
                                tr(ptk[:, hd, :], ktT[:, hd, tc_], identb[:], [B["ktT", hd], B["identb"]], [B["ptk"]])
                            cp(ktok2[k2][:], ptk[:], [B["ptk"]], [B["ktok", k2]], eng="vector")
                            for hd in range(4):
                                mm(pss[:, hd, :], ktT[:, hd, tc_], qtT[:, hd, tc_], True, True, [B["ktT", hd], B["qtT", hd]], [B["pss"]])
                            tt(smk2[k2][:], pss[:], mhg.unsqueeze(1).broadcast_to([128, 4, 128]), ALU.mult, [B["pss"], B["cst"]], [B["smk", k2]])

                        def hg_s2(it):
                            k2 = it % 2
                            tc_ = slice(it * 128, (it + 1) * 128)
                            ktok, smk = ktok2[k2], smk2[k2]
                            for half in range(2):
                                rs_ = slice(half * 64, (half + 1) * 64)
                                ch_i = it * 2 + half
                                for hd in range(4):
                                    hc = slice(hd * 128, (hd + 1) * 128)
                                    mm(pso[rs_, hc], smk[rs_, hd, rs_], vtok[rs_, it, hc], True, False, [B["smk", k2], B["vtok", it]], [B["pso"]])
                                    mm(pso[rs_, hc], qtT[:, hd, it * 128 + half * 64:it * 128 + (half + 1) * 64], Shgb[:, l, hd, :], False, True,
                                       [B["qtT", hd], B["Shgb", hd]], [B["pso"]])
                                    pS_ = psS[:, (hd % 2) * 4, :]
                                    pSb = B["psS", hd % 2]
                                    mm(pS_, ktok[rs_, hd, :], vtok[rs_, it, hc], True, True, [B["ktok", k2], B["vtok", it]], [pSb])
                                    e_ = elast[:, hd, ch_i:ch_i + 1]
                                    ts(stmp[:, hd, :], Shg[:, l, hd, :], e_, None, ALU.mult, None, [B["Shg", hd], B["elast"]], [B["stmp", hd]])
                                    stt(Shg[:, l, hd, :], pS_, e_, stmp[:, hd, :], ALU.mult, ALU.add,
                                        [pSb, B["elast"], B["stmp", hd]], [B["Shg", hd]])
                                    cp(Shgb[:, l, hd, :], Shg[:, l, hd, :], [B["Shg", hd]], [B["Shgb", hd]])
                            for hd in range(4):
                                hc = slice(hd * 128, (hd + 1) * 128)
                                act(junk[:], pso[:, hc], AF.Square, [B["pso"]], [B["junk"], B["ssq"]], accum_out=ssq[:, hd:hd + 1])
                            act(ssq[:], ssq[:], AF.Ln, [B["ssq"]], [B["ssq"]], scale=1.0 / 128, bias=EPS)
                            act(ssq[:], ssq[:], AF.Exp, [B["ssq"]], [B["ssq"]], scale=-0.5)
                            for hd in range(4):
                                hc = slice(hd * 128, (hd + 1) * 128)
                                stt(og[:, hc], pso[:, hc], ssq[:, hd:hd + 1], wgt[:, it, hc], ALU.mult, ALU.mult,
                                    [B["pso"], B["ssq"], B["wgt", it]], [B["og"]])
                            for j in range(4):
                                tr(pto[:, j, :], og[:, j * 128:(j + 1) * 128], identb[:], [B["og"], B["identb"]], [B["pto"]])
                            cp(oT[:, 0:4, tc_], pto[:], [B["pto"]], [B["oT", j] for j in range(4)])

                        hg_s1(0)
                        for it in range(NT):
                            if it + 1 < NT:
                                hg_s1(it + 1)
                            hg_s2(it)
                        kb.barrier()

                if "ssd" in flags:
                    with contextlib.ExitStack() as s2:
                        def sb2(name, shape, dt):
                            return s2.enter_context(nc.sbuf_tensor(_u(name), list(shape), dt))

                        def ps2(name, shape, dt=F32):
                            return _psum(nc, s2, name, list(shape), dt)
                        xT = sb2("xT", [128, 4, TS], BF16)
                        BT = sb2("BT", [128, 2, TS], BF16)
                        CT = sb2("CT", [128, 2, TS], BF16)
                        rawx = [sb2(f"rawx{i}", [128, TS + 3], F32) for i in range(2)]
                        acc = [sb2(f"acc{i}", [128, TS], F32) for i in range(2)]
                        zs = sb2("zs", [128, NT, 512], BF16)
                        dtt = sb2("dtt", [128, NT, 8], F32)
                        adt = sb2("adt", [128, NT, 8], F32)
                        xtok = sb2("xtok", [128, NT, 512], BF16)
                        xdt = sb2("xdt", [128, NT, 512], BF16)
                        Btok = sb2("Btok", [128, NT, 256], BF16)
                        wdt = sb2("wdt", [128, DC, 8], BF16)
                        pq = [ps2(f"pq{i}", [128, 512]) for i in range(2)]
                        kq = 0
                        kb.dma("gpsimd", wdt[:], winv[:, :, 3584:3592], writes=[B["wdt"]])
                        sx = [pre["x0"], pre["x1"]] if "x0" in pre else [wload(2560), wload(3072)]
                        for c8 in range(8):
                            r = rawx[c8 % 2]
                            rb = B["rawx", c8 % 2]
                            cp(r[:, 0:3], halo[:, l, c8, :], [B["halo"]], [rb], eng="vector")
                            for blk in range(NBLK):
                                pst = pq[kq % 2]
                                pb = B["pq", kq % 2]
                                kq += 1
                                cs = slice(blk * 512, (blk + 1) * 512)
                                s = sx[c8 // 4]
                                for c in range(DC):
                                    mm(pst[:], wsl[s][:, c, (c8 % 4) * 128:(c8 % 4 + 1) * 128], xn[:, c, cs], c == 0, c == DC - 1,
                                       [B["wsl", s, c // 2], B["xn", c, blk]], [pb])
                                cp(r[:, 3 + blk * 512:3 + (blk + 1) * 512], pst[:], [pb], [rb])
                            cp(halo[:, l, c8, :], r[:, TS:TS + 3], [rb], [B["halo"]], eng="vector")
                            a_ = acc[c8 % 2]
                            ab = B["acc", c8 % 2]
                            ts(a_[:], r[:, 0:TS], fpc(FP_CONVW(l, 0, c8)), None, ALU.mult, None, [rb, B["FP"]], [ab])
                            for k_ in range(1, 4):
                                stt(a_[:], r[:, k_:TS + k_], fpc(FP_CONVW(l, k_, c8)), a_[:], ALU.mult, ALU.add, [rb, B["FP"], ab], [ab])
                            if c8 < 4:
                                dst, db = xT[:, c8, :], B["xT", c8]
                            elif c8 < 6:
                                dst, db = BT[:, c8 - 4, :], B["BT", c8 - 4]
                            else:
                                dst, db = CT[:, c8 - 6, :], B["CT", c8 - 6]
                            act(dst, a_[:], AF.Silu, [ab, B["FP"]], [db], bias=fpc(FP_CONVB(l, c8)))
                        STOP = int(os.environ.get("SSD_STOP", "99"))
                        if STOP < 99:
                            A("vector", lambda e: e.memset(oT[:, 4:8, :], 0.0), (), [B["oT", j] for j in range(4, 8)])
                        sz = wload(2048) if STOP >= 2 else 0
                        psm = ps2("psm", [128, 512])
                        pdt = psm[:, 0:NT * 8].rearrange("p (t h) -> p t h", h=8)
                        for it in range(NT if STOP >= 2 else 0):
                            blk = it // 4
                            pst = pq[kq % 2]
                            pb = B["pq", kq % 2]
                            kq += 1
                            for c in range(DC):
                                mm(pst[:], xn[:, c, it * 128:(it + 1) * 128], wsl[sz][:, c, :], c == 0, c == DC - 1,
                                   [B["wsl", sz, c // 2], B["xn", c, blk]], [pb])
                            act(zs[:, it, :], pst[:], AF.Silu, [pb], [B["zs", it]])
                            for c in range(DC):
                                mm(pdt[:, it, :], xn[:, c, it * 128:(it + 1) * 128], wdt[:, c, :], c == 0, c == DC - 1,
                                   [B["wdt"], B["xn", c, blk]], [B["psm"]])
                        dtb = RP[:, RP_DTB + l * 8:RP_DTB + (l + 1) * 8].unsqueeze(1).broadcast_to([128, NT, 8])
                        arow = RP[:, RP_A + l * 8:RP_A + (l + 1) * 8].unsqueeze(1).broadcast_to([128, NT, 8])
                        if STOP >= 2:
                            tt(dtt[:], pdt[:], dtb, ALU.add, [B["psm"], B["RP"]], [B["dtt"]])
                            act(dtt[:], dtt[:], AF.Exp, [B["dtt"]], [B["dtt"]])
                            act(dtt[:], dtt[:], AF.Ln, [B["dtt"]], [B["dtt"]], bias=1.0)
                            tt(adt[:], dtt[:], arow, ALU.mult, [B["dtt"], B["RP"]], [B["adt"]])
                        ptx = ps2("ptx", [128, 6, 128], BF16)
                        for it in range(NT if STOP >= 3 else 0):
                            tc_ = slice(it * 128, (it + 1) * 128)
                            for j in range(4):
                                tr(ptx[:, j, :], xT[:, j, tc_], identb[:], [B["xT", j], B["identb"]], [B["ptx"]])
                            for g in range(2):
                                tr(ptx[:, 4 + g, :], BT[:, g, tc_], identb[:], [B["BT", g], B["identb"]], [B["ptx"]])
                            cp(xtok[:, it, :], ptx[:, 0:4, :].rearrange("p j n -> p (j n)"), [B["ptx"]], [B["xtok", it]])
                            cp(Btok[:, it, :], ptx[:, 4:6, :].rearrange("p j n -> p (j n)"), [B["ptx"]], [B["Btok", it]])
                            tt(xdt[:, it, :].rearrange("p (h d) -> p h d", d=64), xtok[:, it, :].rearrange("p (h d) -> p h d", d=64),
                               dtt[:, it, :].unsqueeze(2).broadcast_to([128, 8, 64]), ALU.mult, [B["xtok", it], B["dtt"]], [B["xdt", it]])
                        pa_ = psm[:, 64:80]
                        pd_ = [ps2(f"pd{i}", [128, 4, 128]) for i in range(2)]
                        psc = psm[:, 128:384].rearrange("p (g n) -> p g n", n=128)
                        pyd = pq[1]
                        pyo = ps2("pyo", [128, 512])
                        acs = sb2("acs", [128, 8], F32)
                        ea2 = [sb2(f"ea{i}", [128, 8], F32) for i in range(2)]
                        cd2 = [sb2(f"cd{i}", [128, 8], F32) for i in range(2)]
                        ds2 = [sb2(f"ds{i}", [128, 8], F32) for i in range(2)]
                        rh = [sb2(f"rh{i}", [128, 128], F32) for i in range(2)]
                        LT = sb2("LT", [128, 8, 128], F32)
                        scm = sb2("scm", [128, 2, 128], F32)
                        Wh2 = [sb2(f"Wh{i}", [128, 8, 128], BF16) for i in range(2)]
                        t1_ = sb2("t1s", [128, 512], F32)
                        yy = sb2("yy", [128, 512], F32)
                        xD = sb2("xD", [128, 512], F32)
                        xdtd = sb2("xdtd", [128, 512], BF16)
                        ss2 = sb2("ss2", [128, 2], F32)
                        junk2 = sb2("junk2", [128, 256], F32)
                        yn = sb2("yn", [128, 512], BF16)
                        drow = RP[:, RP_D + l * 8:RP_D + (l + 1) * 8].unsqueeze(2).broadcast_to([128, 8, 64])
                        v8 = lambda ap: ap.rearrange("p (h d) -> p h d", d=64)

                        def ssd_s1(it):
                            k2 = it % 2
                            tc_ = slice(it * 128, (it + 1) * 128)
                            ea, cd, ds_, Wh = ea2[k2], cd2[k2], ds2[k2], Wh2[k2]
                            mm(pa_[:, 0:8], tri, adt[:, it, :], True, True, [B["cst"], B["adt"]], [B["psm"]])
                            mm(pa_[:, 8:16], ones, adt[:, it, :], True, True, [B["cst"], B["adt"]], [B["psm"]])
                            cp(acs[:], pa_[:, 0:8], [B["psm"]], [B["acs"]])
                            act(ea[:], pa_[:, 0:8], AF.Exp, [B["psm"]], [B["ea", k2]])
                            act(cd[:], pa_[:, 8:16], AF.Exp, [B["psm"]], [B["cd", k2]])
                            tt(ds_[:], pa_[:, 8:16], acs[:], ALU.subtract, [B["psm"], B["acs"]], [B["ds", k2]])
                            act(ds_[:], ds_[:], AF.Exp, [B["ds", k2]], [B["ds", k2]])
                            for g in range(2):
                                mm(psc[:, g, :], BT[:, g, tc_], CT[:, g, tc_], True, True, [B["BT", g], B["CT", g]], [B["psm"]])
                            tt(scm[:], psc[:], tri.unsqueeze(1).broadcast_to([128, 2, 128]), ALU.mult, [B["psm"], B["cst"]], [B["scm"]])
                            for h in range(8):
                                r_ = rh[h % 2]
                                ts(r_[:], tri, adt[:, it, h:h + 1], None, ALU.mult, None, [B["cst"], B["adt"]], [B["rh", h % 2]])
                                mm(pd_[h // 4][:, h % 4, :], ups, r_[:], True, True, [B["cst"], B["rh", h % 2]], [B["pd", h // 4]])
                            for k_ in range(2):
                                act(LT[:, k_ * 4:(k_ + 1) * 4, :], pd_[k_][:], AF.Exp, [B["pd", k_]], [B["LT"]])
                            for g in range(2):
                                tt(Wh[:, g * 4:(g + 1) * 4, :], LT[:, g * 4:(g + 1) * 4, :], scm[:, g, :].unsqueeze(1).broadcast_to([128, 4, 128]), ALU.mult,
                                   [B["LT"], B["scm"]], [B["Wh", k2]])

                        def ssd_s2(it):
                            k2 = it % 2
                            tc_ = slice(it * 128, (it + 1) * 128)
                            ea, cd, ds_, Wh = ea2[k2], cd2[k2], ds2[k2], Wh2[k2]
                            for h in range(8):
                                mm(pyd[:, h * 64:(h + 1) * 64], Wh[:, h, :], xdt[:, it, h * 64:(h + 1) * 64], True, True, [B["Wh", k2], B["xdt", it]], [B["pq", 1]])
                            for g in range(2):
                                mm(pyo[:, g * 256:(g + 1) * 256], CT[:, g, tc_], Hstb[:, l, g * 256:(g + 1) * 256], True, True,
                                   [B["CT", g], B["Hstb"]], [B["pyo"]])
                            tt(v8(t1_[:]), v8(pyo[:]), ea[:].unsqueeze(2).broadcast_to([128, 8, 64]), ALU.mult, [B["pyo"], B["ea", k2]], [B["t1s"]])
                            tt(yy[:], pyd[:], t1_[:], ALU.add, [B["pq", 1], B["t1s"]], [B["yy"]])
                            tt(v8(xD[:]), v8(xtok[:, it, :]), drow, ALU.mult, [B["xtok", it], B["RP"]], [B["xD"]])
                            tt(yy[:], yy[:], xD[:], ALU.add, [B["yy"], B["xD"]], [B["yy"]])
                            tt(yy[:], yy[:], zs[:, it, :], ALU.mult, [B["yy"], B["zs", it]], [B["yy"]])
                            for g in range(2):
                                act(junk2[:], yy[:, g * 256:(g + 1) * 256], AF.Square, [B["yy"]], [B["junk2"], B["ss2"]], accum_out=ss2[:, g:g + 1])
                            act(ss2[:], ss2[:], AF.Ln, [B["ss2"]], [B["ss2"]], scale=1.0 / 256, bias=EPS)
                            act(ss2[:], ss2[:], AF.Exp, [B["ss2"]], [B["ss2"]], scale=-0.5)
                            for g in range(2):
                                gc = slice(g * 256, (g + 1) * 256)
                                stt(yn[:, gc], yy[:, gc], ss2[:, g:g + 1], RP[:, RP_NW + l * 512 + g * 256:RP_NW + l * 512 + (g + 1) * 256], ALU.mult, ALU.mult,
                                    [B["yy"], B["ss2"], B["RP"]], [B["yn"]])
                            for j in range(4):
                                tr(ptx[:, j, :], yn[:, j * 128:(j + 1) * 128], identb[:], [B["yn"], B["identb"]], [B["ptx"]])
                            cp(oT[:, 4:8, tc_], ptx[:, 0:4, :], [B["ptx"]], [B["oT", j] for j in range(4, 8)])
                            tt(v8(xdtd[:]), v8(xdt[:, it, :]), ds_[:].unsqueeze(2).broadcast_to([128, 8, 64]), ALU.mult, [B["xdt", it], B["ds", k2]], [B["xdtd"]])
                            pst = pq[0]
                            for g in range(2):
                                gc = slice(g * 256, (g + 1) * 256)
                                mm(pst[:, gc], Btok[:, it, g * 128:(g + 1) * 128], xdtd[:, gc], True, True, [B["Btok", it], B["xdtd"]], [B["pq", 0]])
                            tt(v8(Hst[:, l, :]), v8(Hst[:, l, :]), cd[:].unsqueeze(2).broadcast_to([128, 8, 64]), ALU.mult, [B["Hst"], B["cd", k2]], [B["Hst"]])
                            tt(Hst[:, l, :], Hst[:, l, :], pst[:], ALU.add, [B["Hst"], B["pq", 0]], [B["Hst"]])
                            cp(Hstb[:, l, :], Hst[:, l, :], [B["Hst"]], [B["Hstb"]])

                        ssd_s1(0)
                        for it in range(NT):
                            if it + 1 < NT:
                                ssd_s1(it + 1)
                            ssd_s2(it)
                        kb.barrier()

                if dbg and l == 0:
                    with contextlib.ExitStack() as sd:
                        of = sd.enter_context(nc.sbuf_tensor(_u("of"), [128, 12, TS], F32))
                        cp(of[:].rearrange("p j n -> p (j n)"), oT[:].rearrange("p j n -> p (j n)"), [B["oT", j] for j in range(12)], [B["of"]], eng="vector")
                        kb.dma("sync", dbg_o.rearrange("(j p) n -> p j n", p=128)[:, :, seg * TS:(seg + 1) * TS], of[:], reads=[B["of"]], writes=[B["dbg_o"]])
                        kb.barrier()

                with contextlib.ExitStack() as s3:
                    def sb3(name, shape, dt):
                        return s3.enter_context(nc.sbuf_tensor(_u(name), list(shape), dt))

                    def ps3(name, shape, dt=F32):
                        return _psum(nc, s3, name, list(shape), dt)
                    wo = sb3("wo", [128, 12, D], BF16)
                    ysb = sb3("ysb", [128, DC, 512], F32)
                    sqt = sb3("sq8", [128, DC, 512], BF16)
                    rstd = sb3("rstd", [128, 512], F32)
                    tmpt = [sb3(f"ptmp{i}", [128, 512], F32) for i in range(2)]
                    ssp = ps3("ssp", [128, 512])
                    py = [ps3(f"py{i}", [128, 512]) for i in range(2)]
                    wov = w_out[l].rearrange("(j p) n -> p j n", p=128)
                    for j in range(12):
                        kb.dma("gpsimd", wo[:, j, :], wov[:, j, :], writes=[B["wo", j]])
                    cnt = 0
                    for blk in range(NBLK):
                        cs = slice(blk * 512, (blk + 1) * 512)
                        for m in range(DC):
                            k = cnt % 2
                            cnt += 1
                            for j in range(12):
                                mm(py[k][:], wo[:, j, m * 128:(m + 1) * 128], oT[:, j, cs], j == 0, j == 11, [B["wo", j], B["oT", j]], [B["py", k]])
                            cp(ysb[:, m, :], py[k][:], [B["py", k]], [B["ysb", m]])
                        post_residual(lambda m: FP_G(l, 3, m), blk, ysb, sqt, ssp, rstd, tmpt)

        def load_x(seg, xs8):
            for it in range(NT):
                r0 = seg * TS + it * 128
                kb.dma("sync", xs8[it][:], x[r0:r0 + 128, :], writes=[B["xs", it]])

        def xpose_in(xs8, px):
            for it in range(NT):
                k = it % 2
                for c in range(DC):
                    tr(px[k][:, c // 4, (c % 4) * 128:(c % 4 + 1) * 128], xs8[it][:, c * 128:(c + 1) * 128], ident, [B["xs", it], B["cst"]], [B["px", k]])
                cp(hT[:, :, it * 128:(it + 1) * 128], px[k][:].rearrange("p a (b n) -> p (a b) n", n=128), [B["px", k]],
                   [B["h", c, it // 4] for c in range(DC)])

        def store_out(seg, xo, pxo):
            for it in range(NT):
                k = it % 2
                r0 = seg * TS + it * 128
                for c in range(DC):
                    tr(pxo[k][:, c // 4, (c % 4) * 128:(c % 4 + 1) * 128], hT[:, c, it * 128:(it + 1) * 128], ident,
                       [B["h", c, it // 4], B["cst"]], [B["pxo", k]])
                cp(xo[k][:], pxo[k][:].rearrange("p a n -> p (a n)"), [B["pxo", k]], [B["xo", k]])
                kb.dma("sync", out[r0:r0 + 128, :], xo[k][:], reads=[B["xo", k]], writes=[B["out", seg, it]])

        kb.barrier()
        with contextlib.ExitStack() as sc:
            xs8 = [sc.enter_context(nc.sbuf_tensor(_u(f"xs{i}"), [128, D], F32)) for i in range(NT)]
            px = [_psum(nc, sc, f"px{i}", [128, 2, 512], F32) for i in range(2)]
            load_x(0, xs8)
            xpose_in(xs8, px)
        for seg in range(n_seg):
            for l in range(depth):
                if "ffn" in flags:
                    ffn_phase(l, 0)
                if any(k in flags for k in ("hg", "ssd", "s5")):
                    mixer_phase(l, seg)
                if "ffn" in flags:
                    ffn_phase(l, 1)
            kb.barrier()
            with contextlib.ExitStack() as sc:
                nxt = seg + 1 < n_seg
                xo = [sc.enter_context(nc.sbuf_tensor(_u(f"xo{i}"), [128, D], F32)) for i in range(2)]
                pxo = [_psum(nc, sc, f"pxo{i}", [128, 2, 512], F32) for i in range(2)]
                if nxt:
                    xs8 = [sc.enter_context(nc.sbuf_tensor(_u(f"xs{i}"), [128, D], F32)) for i in range(NT)]
                    px = [_psum(nc, sc, f"px{i}", [128, 2, 512], F32) for i in range(2)]
                    load_x(seg + 1, xs8)
                store_out(seg, xo, pxo)
                if nxt:
                    xpose_in(xs8, px)
        kb.barrier()
        kb.replay()
    return nc


PARAM_NAMES = ["norm_g", "ffn_w_gate", "ffn_w_up", "ffn_w_down", "w_in", "w_out", "hg_lb_logits", "hg_gnorm",
               "ssd_conv_w", "ssd_conv_b", "ssd_dt_bias", "ssd_A_log", "ssd_D", "ssd_norm", "s5_A_re", "s5_A_im",
               "s5_B_re", "s5_B_im", "s5_C_re", "s5_C_im", "s5_D", "s5_log_dt", "s5_glu_w", "s5_glu_b"]


def run(inputs, n_seg, depth=DEPTH, flags=("ffn", "hg", "ssd", "s5"), dbg=False):
    x = np.ascontiguousarray(np.asarray(inputs["x"], dtype=np.float32))
    n_seq = x.shape[0]
    nc = build(n_seg, depth, flags, dbg)
    cst = make_consts()
    params = {k: np.ascontiguousarray(np.asarray(inputs[k], dtype=np.float32)) for k in PARAM_NAMES}
    in_maps = []
    for c in range(N_CORES):
        m = {"x": x[c % n_seq], "cst": cst}
        m.update(params)
        in_maps.append(m)
    res = run_bass_kernel_spmd(nc, in_maps, core_ids=list(range(N_CORES)))
    outs = np.stack([res.results[c]["out"] for c in range(n_seq)], axis=0)
    if dbg:
        return outs, np.stack([res.results[c]["dbg_o"] for c in range(n_seq)], axis=0)
    return outs


def kernel(**inputs):
    x = np.asarray(inputs["x"])
    n_seg = x.shape[1] // TS
    return run(inputs, n_seg).astype(np.float32)
```

```python
import contextlib
import math
import os
import numpy as np
import concourse.bass as bass
import concourse.mybir as mybir
from concourse.bass_utils import run_bass_kernel_spmd

F32 = mybir.dt.float32
BF16 = mybir.dt.bfloat16
AF = mybir.ActivationFunctionType
ALU = mybir.AluOpType

P = 128
D = 1024
DC = 8
FF = 2816
FC = 22
TS = 1024
NT = TS // 128
NBLK = TS // 512
DIN = 4104
EPS = 1e-6
DEPTH = 2
N_CORES = 8
SEQ = 8192
BATCH = 2


class Buf:
    __slots__ = ("w", "r", "psum")

    def __init__(self):
        self.w = None
        self.r = []
        self.psum = False


_PSUM_NAMES = {"pp", "psm", "pc", "ptt", "prr", "ssp", "pg", "pu", "py", "pu5", "pe5", "py5", "pq", "ptk", "pss",
               "pso", "psS", "pto", "ptx", "pd", "pyo", "px", "pxo"}


class BufTable(dict):
    def __missing__(self, k):
        b = Buf()
        ks = k if isinstance(k, tuple) else (k,)
        b.psum = any(x in _PSUM_NAMES for x in ks if isinstance(x, str))
        self[k] = b
        return b


class _Eng:
    def __init__(self, name):
        self.name = name
        self.sems = []
        self.count = 0
        self.ops = []
        self.seen = {}


SEM_ROLL = 30000
_UC = [0]


def _psum(nc, stack, name, shape, dt):
    shape = list(shape)
    esz = 2 if dt == BF16 else 4
    n = 1
    for d in shape[1:]:
        n *= d
    per_bank = 2048 // esz
    nb = (n * esz + 2047) // 2048
    t = stack.enter_context(nc.psum_tensor(_u(name), [128, nb * per_bank], dt))
    v = t[0:shape[0], 0:n]
    if len(shape) == 3:
        v = v.rearrange("p (a b) -> p a b", b=shape[2])
    elif len(shape) == 4:
        v = v.rearrange("p (a b c) -> p a b c", b=shape[2], c=shape[3])
    return v


def _u(name):
    _UC[0] += 1
    return f"{name}_{_UC[0]}"


class KB:
    def __init__(self, nc, n_dma_ch=32):
        self.nc = nc
        self.eng = {n: _Eng(n) for n in ("tensor", "vector", "scalar", "gpsimd", "sync")}
        self.nsem = 0
        for e in self.eng.values():
            e.sems.append(self._newsem())
        self.dma_chs = {q: [{"sem": self._newsem(), "count": 0, "tok": None} for _ in range(n_dma_ch // 2)] for q in ("hw", "sw")}
        self.dma_rrs = {"hw": 0, "sw": 0}

    def _newsem(self):
        self.nsem += 1
        return self.nsem - 1

    def _collect(self, e, reads, writes, skip_own=False):
        best = {}
        for b in reads:
            if b.w is not None:
                s, v = b.w
                if v > best.get(s, 0):
                    best[s] = v
            if b.psum:
                for (s, v) in b.r:
                    if s not in e.sems and v > best.get(s, 0):
                        best[s] = v
        for b in writes:
            if b.w is not None:
                s, v = b.w
                if v > best.get(s, 0):
                    best[s] = v
            for (s, v) in b.r:
                if v > best.get(s, 0):
                    best[s] = v
        waits = []
        for s, v in best.items():
            if skip_own and s in e.sems:
                continue
            if e.seen.get(s, 0) >= v:
                continue
            e.seen[s] = v
            waits.append((s, v))
        return waits

    def _commit(self, tok, reads, writes):
        for b in reads:
            b.r.append(tok)
            if len(b.r) > 64:
                best = {}
                for (s, v) in b.r:
                    if v > best.get(s, 0):
                        best[s] = v
                b.r = list(best.items())
        for b in writes:
            b.w = tok
            b.r = []

    def op(self, engname, fn, reads=(), writes=()):
        e = self.eng[engname]
        waits = self._collect(e, reads, writes, skip_own=(engname == "tensor"))
        if e.count >= SEM_ROLL:
            e.sems.append(self._newsem())
            e.count = 0
        e.count += 1
        tok = (e.sems[-1], e.count)
        e.ops.append((waits, fn, (tok[0], 1)))
        self._commit(tok, reads, writes)
        return tok

    def dma(self, engname, out, in_, reads=(), writes=(), **kw):
        e = self.eng[engname]
        q = "sw" if engname == "gpsimd" else "hw"
        chs = self.dma_chs[q]
        ch = chs[self.dma_rrs[q]]
        self.dma_rrs[q] = (self.dma_rrs[q] + 1) % len(chs)
        waits = self._collect(e, reads, writes)
        if ch["tok"] is not None:
            s, v = ch["tok"]
            if e.seen.get(s, 0) < v:
                e.seen[s] = v
                waits.append((s, v))
        ch["count"] += 16
        tok = (ch["sem"], ch["count"])
        ch["tok"] = tok
        e.ops.append((waits, lambda eng: eng.dma_start(out=out, in_=in_, **kw), (tok[0], 16)))
        self._commit(tok, reads, writes)
        return tok

    def barrier(self):
        toks = []
        for e in self.eng.values():
            if e.count > 0:
                toks.append((e.sems[-1], e.count))
        for chs in self.dma_chs.values():
            for ch in chs:
                if ch["tok"] is not None:
                    toks.append(ch["tok"])
        for e in self.eng.values():
            waits = []
            for (s, v) in toks:
                if e.seen.get(s, 0) >= v:
                    continue
                e.seen[s] = v
                waits.append((s, v))
            if waits:
                e.ops.append((waits, None, None))

    def wait_all(self, engname, bufs):
        e = self.eng[engname]
        waits = self._collect(e, bufs, ())
        e.ops.append((waits, None, None))

    def replay(self):
        nc = self.nc
        with contextlib.ExitStack() as st:
            sems = [st.enter_context(nc.semaphore(f"s{i}")) for i in range(self.nsem)]
            block = st.enter_context(nc.Block())

            def mk(e):
                def body(eng):
                    for waits, fn, inc in e.ops:
                        for (s, v) in waits:
                            eng.wait_ge(sems[s], v)
                        if fn is not None:
                            fn(eng).then_inc(sems[inc[0]], inc[1])
                return body
            for name in ("sync", "gpsimd", "scalar", "vector", "tensor"):
                e = self.eng[name]
                if e.ops:
                    getattr(block, name)(mk(e))


C_ID = 0
C_ONES = 128
C_TRI = 256
C_UPS = 384
C_MHG = 512
C_MS5 = 640
C_RST = 768
C_NIDX = C_RST + TS
C_JIDX = C_NIDX + 129
C_W = C_JIDX + 24


def make_consts():
    c = np.zeros((128, C_W), np.float32)
    r = np.arange(128)
    c[:, C_ID:C_ID + 128] = np.eye(128)
    c[:, C_ONES:C_ONES + 128] = 1.0
    c[:, C_TRI:C_TRI + 128] = (r[:, None] <= r[None, :])
    c[:, C_UPS:C_UPS + 128] = (r[:, None] > r[None, :])
    c[:, C_MHG:C_MHG + 128] = (r[:, None] <= r[None, :]) & ((r[:, None] // 64) == (r[None, :] // 64))
    c[:, C_MS5:C_MS5 + 128] = ((r[None, :] // 16) >= (r[:, None] // 16))
    rst = np.ones(TS, np.float32)
    rst[::64] = 0.0
    c[:, C_RST:C_RST + TS] = rst[None, :]
    c[:, C_NIDX:C_NIDX + 129] = np.arange(129)[None, :]
    j = np.concatenate([-np.arange(1, 9), np.arange(7, -1, -1), np.arange(1, 9)]).astype(np.float32)
    c[:, C_JIDX:C_JIDX + 24] = j[None, :]
    return c


def FP_G(l, i, c):
    return (l * 6 + i) * 8 + c


FP_CW = 96


def FP_CONVW(l, k, c):
    return FP_CW + (l * 4 + k) * 8 + c


FP_CB0 = FP_CW + 64


def FP_CONVB(l, c):
    return FP_CB0 + l * 8 + c


FP_SD0 = FP_CB0 + 16


def FP_S5D(l, c):
    return FP_SD0 + l * 4 + c


FP_GB0 = FP_SD0 + 8


def FP_GLUB(l, c):
    return FP_GB0 + l * 4 + c


FP_LB0 = FP_GB0 + 8
FP_GS0 = FP_LB0 + 8


def FP_GS(l, f, c):
    return FP_GS0 + (l * 2 + f) * 8 + c


FP_LBV = FP_GS0 + 32


def FP_LB(l, hd):
    return FP_LBV + l * 4 + hd


FP_OMLV = FP_LBV + 8


def FP_OML(l, hd):
    return FP_OMLV + l * 4 + hd


FP_W = FP_OMLV + 8

RP_GN = 0
RP_DTB = 256
RP_A = RP_DTB + 16
RP_D = RP_A + 16
RP_NW = RP_D + 16
RP_W = RP_NW + 1024


def build(n_seg, depth=DEPTH, flags=("ffn", "hg", "ssd", "s5"), dbg=False):
    nc = bass.Bass("TRN2", target_bir_lowering=False)
    NTOK = n_seg * TS

    def din(name, shape, dt=F32):
        return nc.dram_tensor(name, list(shape), dt, kind="ExternalInput").ap()

    x = din("x", [NTOK, D])
    cst_d = din("cst", [128, C_W])
    norm_g = din("norm_g", [DEPTH, 6, D])
    w_gate = din("ffn_w_gate", [DEPTH, 2, D, FF])
    w_up = din("ffn_w_up", [DEPTH, 2, D, FF])
    w_down = din("ffn_w_down", [DEPTH, 2, FF, D])
    w_in = din("w_in", [DEPTH, D, DIN])
    w_out = din("w_out", [DEPTH, 1536, D])
    lb_log = din("hg_lb_logits", [DEPTH, 512])
    hg_gn = din("hg_gnorm", [DEPTH, 128])
    conv_w = din("ssd_conv_w", [DEPTH, 4, 1024])
    conv_b = din("ssd_conv_b", [DEPTH, 1024])
    dt_bias = din("ssd_dt_bias", [DEPTH, 8])
    a_log = din("ssd_A_log", [DEPTH, 8])
    ssd_d = din("ssd_D", [DEPTH, 8])
    ssd_nw = din("ssd_norm", [DEPTH, 512])
    s5_are = din("s5_A_re", [DEPTH, 32, 64])
    s5_aim = din("s5_A_im", [DEPTH, 32, 64])
    s5_bre = din("s5_B_re", [DEPTH, 32, 64, 16])
    s5_bim = din("s5_B_im", [DEPTH, 32, 64, 16])
    s5_cre = din("s5_C_re", [DEPTH, 32, 16, 64])
    s5_cim = din("s5_C_im", [DEPTH, 32, 16, 64])
    s5_dsk = din("s5_D", [DEPTH, 512])
    s5_ldt = din("s5_log_dt", [DEPTH, 32])
    glu_w = din("s5_glu_w", [DEPTH, 512, 512])
    glu_b = din("s5_glu_b", [DEPTH, 512])
    out = nc.dram_tensor("out", [NTOK, D], F32, kind="ExternalOutput").ap()
    dbg_o = nc.dram_tensor("dbg_o", [1536, NTOK], F32, kind="ExternalOutput").ap() if dbg else None

    def dscr(name, shape, dt):
        return nc.dram_tensor(name, list(shape), dt, kind="Internal").ap()

    ud = dscr("ud", [512, TS], BF16)
    yd = dscr("yd", [512, TS], F32)
    tabT = dscr("tabT", [DEPTH, 128, 32 * 128], BF16)
    tabR = dscr("tabR", [DEPTH, 2, 128, 32 * 64], BF16)
    tabO = dscr("tabO", [DEPTH, 2, 64, 32 * 128], BF16)
    tabCS = dscr("tabCS", [DEPTH, 2, 64, 32 * 129], F32)
    tabRho = dscr("tabRho", [DEPTH, 64, 32], F32)

    kb = KB(nc)
    B = BufTable()
    A = kb.op

    def mm(out_, lhsT, rhs, start, stop, reads, writes, **kw):
        A("tensor", lambda e: e.matmul(out_, lhsT=lhsT, rhs=rhs, start=start, stop=stop, **kw), reads, writes)

    def tr(out_, in_, ident, reads, writes):
        A("tensor", lambda e: e.transpose(out=out_, in_=in_, identity=ident), reads, writes)

    def act(out_, in_, func, reads, writes, **kw):
        A("scalar", lambda e: e.activation(out=out_, in_=in_, func=func, **kw), reads, writes)

    def tt(out_, in0, in1, op, reads, writes, eng="vector"):
        A(eng, lambda e: e.tensor_tensor(out=out_, in0=in0, in1=in1, op=op), reads, writes)

    def ts(out_, in0, s1, s2, op0, op1, reads, writes, eng="vector"):
        if op1 is None:
            A(eng, lambda e: e.tensor_scalar(out=out_, in0=in0, scalar1=s1, scalar2=None, op0=op0), reads, writes)
        else:
            A(eng, lambda e: e.tensor_scalar(out=out_, in0=in0, scalar1=s1, scalar2=s2, op0=op0, op1=op1), reads, writes)

    def stt(out_, in0, scalar, in1, op0, op1, reads, writes):
        A("vector", lambda e: e.scalar_tensor_tensor(out=out_, in0=in0, scalar=scalar, in1=in1, op0=op0, op1=op1), reads, writes)

    def scan(out_, d0, d1, init, reads, writes):
        A("vector", lambda e: e.tensor_tensor_scan(out=out_, data0=d0, data1=d1, initial=init, op0=ALU.mult, op1=ALU.add), reads, writes)

    def recip(t_, b_):
        A("vector", lambda e: e.reciprocal(out=t_, in_=t_), [b_], [b_])

    def cp(out_, in_, reads, writes, eng="scalar"):
        if eng == "scalar":
            A("scalar", lambda e: e.copy(out=out_, in_=in_), reads, writes)
        else:
            A(eng, lambda e: e.tensor_copy(out=out_, in_=in_), reads, writes)

    with contextlib.ExitStack() as glob:
        def gsb(name, shape, dt):
            return glob.enter_context(nc.sbuf_tensor(_u(name), list(shape), dt))

        cst = gsb("cst", [128, C_W], F32)
        identb = gsb("identb", [128, 128], BF16)
        onesb = gsb("onesb", [128, 128], BF16)
        FP = gsb("FP", [128, FP_W], F32)
        RP = gsb("RP", [128, RP_W], F32)
        hT = gsb("hT", [128, DC, TS], F32)
        Shg = gsb("Shg", [128, DEPTH, 4, 128], F32)
        Shgb = gsb("Shgb", [128, DEPTH, 4, 128], BF16)
        Hst = gsb("Hst", [128, DEPTH, 512], F32)
        Hstb = gsb("Hstb", [128, DEPTH, 512], BF16)
        halo = gsb("halo", [128, DEPTH, 8, 3], F32)
        Xc = gsb("Xc", [128, DEPTH, 2, 16], F32)
        rho8 = gsb("rho8", [64, DEPTH, 32], F32)
        rho8s = gsb("rho8s", [128, DEPTH, 16], F32)

        ident = cst[:, C_ID:C_ID + 128]
        ones = cst[:, C_ONES:C_ONES + 128]
        tri = cst[:, C_TRI:C_TRI + 128]
        ups = cst[:, C_UPS:C_UPS + 128]
        mhg = cst[:, C_MHG:C_MHG + 128]
        ms5 = cst[:, C_MS5:C_MS5 + 128]
        rstm = cst[:, C_RST:C_RST + TS]

        def fpc(col):
            return FP[:, col:col + 1]

        with contextlib.ExitStack() as sc:
            def sb(name, shape, dt):
                return sc.enter_context(nc.sbuf_tensor(_u(name), list(shape), dt))

            def ps(name, shape, dt=F32):
                return _psum(nc, sc, name, list(shape), dt)

            kb.dma("sync", cst[:], cst_d[:, :], writes=[B["cst"]])
            cp(identb[:], ident, [B["cst"]], [B["identb"]], eng="vector")
            cp(onesb[:], ones, [B["cst"]], [B["onesb"]], eng="vector")
            for t_, nm in ((Shg, "Shg"), (Hst, "Hst"), (halo, "halo"), (Xc, "Xc"), (Shgb, "Shgb"), (Hstb, "Hstb")):
                A("gpsimd", lambda e, t_=t_: e.memset(t_[:], 0.0), (), [B[nm]])
            stgA = sb("stgA", [128, 128], F32)
            stgB = sb("stgB", [128, 128], F32)
            A("vector", lambda e: e.memset(stgA[:], 0.0), (), [B["stgA"]])
            A("vector", lambda e: e.memset(stgB[:], 0.0), (), [B["stgB"]])
            kb.dma("sync", stgA[0:96, :], norm_g.rearrange("l i (c p) -> (l i c) p", p=128), writes=[B["stgA"]])
            kb.dma("sync", stgB[0:64, :], conv_w.rearrange("l k (c p) -> (l k c) p", p=128), writes=[B["stgB"]])
            kb.dma("sync", stgB[64:80, :], conv_b.rearrange("l (c p) -> (l c) p", p=128), writes=[B["stgB"]])
            kb.dma("sync", stgB[80:88, :], s5_dsk.rearrange("l (c p) -> (l c) p", p=128), writes=[B["stgB"]])
            kb.dma("sync", stgB[88:96, :], glu_b.rearrange("l (c p) -> (l c) p", p=128), writes=[B["stgB"]])
            kb.dma("sync", stgB[96:104, :], lb_log.rearrange("l (c p) -> (l c) p", p=128), writes=[B["stgB"]])
            pp = ps("pp", [128, 256])
            tr(pp[:, 0:128], stgA[:], ident, [B["stgA"], B["cst"]], [B["pp"]])
            tr(pp[:, 128:256], stgB[:], ident, [B["stgB"], B["cst"]], [B["pp"]])
            cp(FP[:, 0:96], pp[:, 0:96], [B["pp"]], [B["FP"]])
            cp(FP[:, 96:96 + 104], pp[:, 128:128 + 104], [B["pp"]], [B["FP"]])
            for l in range(DEPTH):
                for f in range(2):
                    c0 = FP_G(l, 1 + 4 * f, 0)
                    ts(FP[:, FP_GS(l, f, 0):FP_GS(l, f, 0) + 8], FP[:, c0:c0 + 8], 0.5, None, ALU.mult, None, [B["FP"]], [B["FP"]])
            A("vector", lambda e: e.memset(FP[:, FP_LBV:FP_LBV + 4], 0.0), (), [B["FP"]])
            tt(FP[:, FP_LBV + 4:FP_LBV + 8], FP[:, FP_LB0 + 4:FP_LB0 + 8], FP[:, FP_LB0:FP_LB0 + 4], ALU.subtract, [B["FP"]], [B["FP"]])
            act(FP[:, FP_LBV + 4:FP_LBV + 8], FP[:, FP_LBV + 4:FP_LBV + 8], AF.Sigmoid, [B["FP"]], [B["FP"]])
            ts(FP[:, FP_OMLV:FP_OMLV + 8], FP[:, FP_LBV:FP_LBV + 8], -1.0, 1.0, ALU.mult, ALU.add, [B["FP"]], [B["FP"]])
            for l in range(DEPTH):
                kb.dma("sync", RP[:, RP_GN + l * 128:RP_GN + (l + 1) * 128], hg_gn[l:l + 1, :].broadcast_to([128, 128]), writes=[B["RP"]])
                kb.dma("sync", RP[:, RP_DTB + l * 8:RP_DTB + (l + 1) * 8], dt_bias[l:l + 1, :].broadcast_to([128, 8]), writes=[B["RP"]])
                kb.dma("sync", RP[:, RP_A + l * 8:RP_A + (l + 1) * 8], a_log[l:l + 1, :].broadcast_to([128, 8]), writes=[B["RP"]])
                kb.dma("sync", RP[:, RP_D + l * 8:RP_D + (l + 1) * 8], ssd_d[l:l + 1, :].broadcast_to([128, 8]), writes=[B["RP"]])
                kb.dma("sync", RP[:, RP_NW + l * 512:RP_NW + (l + 1) * 512], ssd_nw[l:l + 1, :].broadcast_to([128, 512]), writes=[B["RP"]])
            act(RP[:, RP_A:RP_A + 16], RP[:, RP_A:RP_A + 16], AF.Exp, [B["RP"]], [B["RP"]])
            ts(RP[:, RP_A:RP_A + 16], RP[:, RP_A:RP_A + 16], -1.0, None, ALU.mult, None, [B["RP"]], [B["RP"]])

            if "s5" in flags:
                TWO_PI = 2.0 * math.pi
                MAGIC = 12582912.0
                for l in range(depth):
                  with contextlib.ExitStack() as sl:
                    kb.barrier()

                    def sb(name, shape, dt, sl=sl):
                        return sl.enter_context(nc.sbuf_tensor(_u(name), list(shape), dt))

                    def ps(name, shape, dt=F32, sl=sl):
                        return _psum(nc, sl, name, list(shape), dt)
                    an = sb(f"an{l}", [32, 2, 64], F32)
                    kb.dma("sync", an[:, 0, :], s5_are[l], writes=[B["an"]])
                    kb.dma("sync", an[:, 1, :], s5_aim[l], writes=[B["an"]])
                    pa = ps(f"pa{l}", [64, 64])
                    tr(pa[:, 0:32], an[:, 0, :], ident[0:32, 0:32], [B["an"], B["cst"]], [B["psm"]])
                    tr(pa[:, 32:64], an[:, 1, :], ident[0:32, 0:32], [B["an"], B["cst"]], [B["psm"]])
                    aa = sb(f"aa{l}", [64, 2, 32], F32)
                    cp(aa[:, 0, :], pa[:, 0:32], [B["psm"]], [B["aa"]])
                    cp(aa[:, 1, :], pa[:, 32:64], [B["psm"]], [B["aa"]])
                    dl = sb(f"dl{l}", [64, 32], F32)
                    kb.dma("sync", dl[:], s5_ldt[l:l + 1, :].broadcast_to([64, 32]), writes=[B["dl"]])
                    act(dl[:], dl[:], AF.Exp, [B["dl"]], [B["dl"]])
                    ard = sb(f"ard{l}", [64, 32], F32)
                    t1 = sb(f"t1{l}", [64, 32], F32)
                    tt(ard[:], aa[:, 0, :], dl[:], ALU.mult, [B["aa"], B["dl"]], [B["ard"]])
                    tt(t1[:], aa[:, 1, :], dl[:], ALU.mult, [B["aa"], B["dl"]], [B["t1"]])
                    ts(t1[:], t1[:], 1.0 / TWO_PI, None, ALU.mult, None, [B["t1"]], [B["t1"]])

                    def frac_(dst, src, shift, bs):
                        tmpn = "fr_tmp"
                        shp = list(src.shape)
                        tmp = sb(f"frt{l}_{kb.eng['vector'].count}", shp, F32)
                        ts(tmp[:], src, shift, MAGIC, ALU.add, ALU.add, bs, [B[tmpn]])
                        ts(tmp[:], tmp[:], -MAGIC, None, ALU.add, None, [B[tmpn]], [B[tmpn]])
                        stt(dst, src, shift, tmp[:], ALU.add, ALU.subtract, bs + [B[tmpn]], [B["fr_dst"]])

                    jv = cst[0:64, C_JIDX:C_JIDX + 24]
                    mag = sb(f"mag{l}", [64, 32, 24], F32)
                    ang = sb(f"ang{l}", [64, 32, 24], F32)
                    tt(mag[:], ard[:].unsqueeze(2).broadcast_to([64, 32, 24]), jv.unsqueeze(1).broadcast_to([64, 32, 24]), ALU.mult, [B["ard"], B["cst"]], [B["mag"]])
                    act(mag[:], mag[:], AF.Exp, [B["mag"]], [B["mag"]])
                    t1r = sb(f"t1r{l}", [64, 32], F32)
                    frac_(t1r[:], t1[:], 0.0, [B["t1"]])
                    tt(ang[:], t1r[:].unsqueeze(2).broadcast_to([64, 32, 24]), jv.unsqueeze(1).broadcast_to([64, 32, 24]), ALU.mult, [B["fr_dst"], B["cst"]], [B["ang"]])
                    sj = sb(f"sj{l}", [64, 32, 24], F32)
                    cj = sb(f"cj{l}", [64, 32, 24], F32)
                    frac_(sj[:], ang[:], 0.0, [B["ang"]])
                    act(sj[:], sj[:], AF.Sin, [B["fr_dst"]], [B["sj"]], scale=TWO_PI)
                    frac_(cj[:], ang[:], 0.25, [B["ang"]])
                    act(cj[:], cj[:], AF.Sin, [B["fr_dst"]], [B["cj"]], scale=TWO_PI)
                    Lr = sb(f"Lr{l}", [64, 32, 24], F32)
                    Li = sb(f"Li{l}", [64, 32, 24], F32)
                    tt(Lr[:], mag[:], cj[:], ALU.mult, [B["mag"], B["cj"]], [B["Lr"]])
                    tt(Li[:], mag[:], sj[:], ALU.mult, [B["mag"], B["sj"]], [B["Li"]])
                    cp(rho8[:, l, :], mag[:, :, 23], [B["mag"]], [B["rho8"]], eng="vector")
                    kb.dma("sync", tabRho[l], rho8[:, l, :], reads=[B["rho8"]], writes=[B["tabRho"]])
                    fr = sb(f"fr{l}", [64, 32], F32)
                    fi = sb(f"fi{l}", [64, 32], F32)
                    nr = sb(f"nr{l}", [64, 32], F32)
                    den = sb(f"den{l}", [64, 32], F32)
                    tq = sb(f"tq{l}", [64, 32], F32)
                    ar_, ai_ = aa[:, 0, :], aa[:, 1, :]
                    lr1, li1 = Lr[:, :, 16], Li[:, :, 16]
                    ts(nr[:], lr1, -1.0, None, ALU.add, None, [B["Lr"]], [B["nr"]])
                    tt(den[:], ar_, ar_, ALU.mult, [B["aa"]], [B["den"]])
                    tt(tq[:], ai_, ai_, ALU.mult, [B["aa"]], [B["tq"]])
                    tt(den[:], den[:], tq[:], ALU.add, [B["den"], B["tq"]], [B["den"]])
                    recip(den[:], B["den"])
                    tt(fr[:], nr[:], ar_, ALU.mult, [B["nr"], B["aa"]], [B["fr"]])
                    tt(tq[:], li1, ai_, ALU.mult, [B["Li"], B["aa"]], [B["tq"]])
                    tt(fr[:], fr[:], tq[:], ALU.add, [B["fr"], B["tq"]], [B["fr"]])
                    tt(fr[:], fr[:], den[:], ALU.mult, [B["fr"], B["den"]], [B["fr"]])
                    tt(fi[:], li1, ar_, ALU.mult, [B["Li"], B["aa"]], [B["fi"]])
                    tt(tq[:], nr[:], ai_, ALU.mult, [B["nr"], B["aa"]], [B["tq"]])
                    tt(fi[:], fi[:], tq[:], ALU.subtract, [B["fi"], B["tq"]], [B["fi"]])
                    tt(fi[:], fi[:], den[:], ALU.mult, [B["fi"], B["den"]], [B["fi"]])
                    Bn = sb(f"Bn{l}", [64, 2, 32, 16], F32)
                    kb.dma("sync", Bn[:, 0], s5_bre[l].rearrange("g p c -> p g c"), writes=[B["Bn"]])
                    kb.dma("sync", Bn[:, 1], s5_bim[l].rearrange("g p c -> p g c"), writes=[B["Bn"]])
                    Cn = sb(f"Cn{l}", [64, 2, 32, 16], F32)
                    cnat = sb(f"cnat{l}", [128, 2, 4, 64], F32)
                    kb.dma("sync", cnat[:, 0], s5_cre[l].rearrange("(a g) c p -> (g c) a p", a=4), writes=[B["cnat"]])
                    kb.dma("sync", cnat[:, 1], s5_cim[l].rearrange("(a g) c p -> (g c) a p", a=4), writes=[B["cnat"]])
                    pc_ = ps(f"pc{l}", [64, 2, 512])
                    for ri in range(2):
                        for a4 in range(4):
                            tr(pc_[:, ri, a4 * 128:(a4 + 1) * 128], cnat[:, ri, a4, :], ident, [B["cnat"], B["cst"]], [B["pc"]])
                    cp(Cn[:].rearrange("p r g c -> p (r g c)"), pc_[:].rearrange("p r n -> p (r n)"), [B["pc"]], [B["Cn"]])
                    Bb = sb(f"Bb{l}", [64, 2, 32, 16], F32)
                    tw = sb(f"tw{l}", [64, 32, 16], F32)
                    frb = fr[:].unsqueeze(2).broadcast_to([64, 32, 16])
                    fib = fi[:].unsqueeze(2).broadcast_to([64, 32, 16])
                    tt(Bb[:, 0], frb, Bn[:, 0], ALU.mult, [B["fr"], B["Bn"]], [B["Bb"]])
                    tt(tw[:], fib, Bn[:, 1], ALU.mult, [B["fi"], B["Bn"]], [B["tw"]])
                    tt(Bb[:, 0], Bb[:, 0], tw[:], ALU.subtract, [B["Bb"], B["tw"]], [B["Bb"]])
                    tt(Bb[:, 1], frb, Bn[:, 1], ALU.mult, [B["fr"], B["Bn"]], [B["Bb"]])
                    tt(tw[:], fib, Bn[:, 0], ALU.mult, [B["fi"], B["Bn"]], [B["tw"]])
                    tt(Bb[:, 1], Bb[:, 1], tw[:], ALU.add, [B["Bb"], B["tw"]], [B["Bb"]])

                    NG = 8
                    for g0 in range(0, 32, NG):
                        with contextlib.ExitStack() as s2:
                            def sb2(name, shape, dt):
                                return s2.enter_context(nc.sbuf_tensor(_u(name), list(shape), dt))
                            kb.barrier()
                            PBt = sb2(f"PBt{l}_{g0}", [64, 2, NG, 128], F32)
                            PBr = sb2(f"PBr{l}_{g0}", [64, 2, NG, 128], F32)
                            PC = sb2(f"PC{l}_{g0}", [64, 2, NG, 128], F32)
                            Tst = sb2(f"Tst{l}_{g0}", [128, NG, 128], BF16)
                            Rst = sb2(f"Rst{l}_{g0}", [128, 2, NG, 64], BF16)
                            Ost = sb2(f"Ost{l}_{g0}", [64, 2, NG, 128], BF16)
                            v4 = lambda t_, r: t_[:, r].rearrange("p g (s c) -> p g s c", c=16)
                            def cprod2(dst_re, dst_im, j0, M, neg_im, bname):
                                lr = Lr[:, g0:g0 + NG, j0:j0 + 8].unsqueeze(3).broadcast_to([64, NG, 8, 16])
                                li = Li[:, g0:g0 + NG, j0:j0 + 8].unsqueeze(3).broadcast_to([64, NG, 8, 16])
                                mr = M[:, 0, g0:g0 + NG, :].unsqueeze(2).broadcast_to([64, NG, 8, 16])
                                mi = M[:, 1, g0:g0 + NG, :].unsqueeze(2).broadcast_to([64, NG, 8, 16])
                                t2 = sb2(f"cpt{l}_{bname}_{g0}", [64, NG, 8, 16], F32)
                                rds = [B["Lr"], B["Li"], B["Bb"], B["Cn"]]
                                tt(dst_re, lr, mr, ALU.mult, rds, [B[bname]])
                                tt(t2[:], li, mi, ALU.mult, rds, [B[bname + "t"]])
                                tt(dst_re, dst_re, t2[:], ALU.subtract, [B[bname], B[bname + "t"]], [B[bname]])
                                tt(dst_im, lr, mi, ALU.mult, rds, [B[bname + "i"]])
                                tt(t2[:], li, mr, ALU.mult, rds, [B[bname + "t"]])
                                if neg_im:
                                    stt(dst_im, dst_im, -1.0, t2[:], ALU.mult, ALU.subtract, [B[bname + "i"], B[bname + "t"]], [B[bname + "i"]])
                                else:
                                    tt(dst_im, dst_im, t2[:], ALU.add, [B[bname + "i"], B[bname + "t"]], [B[bname + "i"]])
                            cprod2(v4(PBt, 0), v4(PBt, 1), 0, Bb, False, "PBt")
                            cprod2(v4(PBr, 0), v4(PBr, 1), 8, Bb, False, "PBr")
                            cprod2(v4(PC, 0), v4(PC, 1), 16, Cn, True, "PC")
                            ptt = _psum(nc, s2, f"ptt{l}_{g0}", [128, 2, 512], F32)
                            prr = _psum(nc, s2, f"prr{l}_{g0}", [128, 2, 512], F32)
                            for gg in range(NG):
                                bk, of = gg // 4, (gg % 4) * 128
                                mm(ptt[:, bk, of:of + 128], PBt[:, 0, gg, :], PC[:, 0, gg, :], True, False, [B["PBt"], B["PC"]], [B["ptt", bk]])
                                mm(ptt[:, bk, of:of + 128], PBt[:, 1, gg, :], PC[:, 1, gg, :], False, True, [B["PBti"], B["PCi"]], [B["ptt", bk]])
                            for bk in range(2):
                                tt(Tst[:, bk * 4:(bk + 1) * 4, :], ptt[:, bk, :].rearrange("p (g n) -> p g n", n=128),
                                   ms5.unsqueeze(1).broadcast_to([128, 4, 128]), ALU.mult, [B["ptt", bk], B["cst"]], [B["Tst"]])
                            for gg in range(NG):
                                for ri in range(2):
                                    mm(prr[:, ri, gg * 64:(gg + 1) * 64], PBr[:, ri, gg, :], ident[0:64, 0:64], True, True,
                                       [B["PBr"], B["PBri"], B["cst"]], [B["prr", ri]])
                            for ri in range(2):
                                cp(Rst[:, ri].rearrange("p g n -> p (g n)"), prr[:, ri, :], [B["prr", ri]], [B["Rst"]])
                            cp(Ost[:].rearrange("p r g n -> p (r g n)"), PC[:].rearrange("p r g n -> p (r g n)"), [B["PC"], B["PCi"]], [B["Ost"]], eng="vector")
                            kb.dma("sync", tabT[l, :, g0 * 128:(g0 + NG) * 128], Tst[:].rearrange("p g n -> p (g n)"), reads=[B["Tst"]], writes=[B["tabT"]])
                            for ri in range(2):
                                kb.dma("sync", tabR[l, ri, :, g0 * 64:(g0 + NG) * 64], Rst[:, ri].rearrange("p g n -> p (g n)"), reads=[B["Rst"]], writes=[B["tabR"]])
                                kb.dma("sync", tabO[l, ri, :, g0 * 128:(g0 + NG) * 128], Ost[:, ri].rearrange("p g n -> p (g n)"), reads=[B["Ost"]], writes=[B["tabO"]])
                            t8 = sb2(f"t8{l}_{g0}", [64, NG], F32)
                            t8r = sb2(f"t8r{l}_{g0}", [64, NG], F32)
                            ts(t8[:], t1r[:, g0:g0 + NG], 8.0, None, ALU.mult, None, [B["fr_dst"]], [B["t8"]])
                            tmp8 = sb2(f"tmp8{l}_{g0}", [64, NG], F32)
                            ts(tmp8[:], t8[:], MAGIC, None, ALU.add, None, [B["t8"]], [B["tmp8"]])
                            ts(tmp8[:], tmp8[:], -MAGIC, None, ALU.add, None, [B["tmp8"]], [B["tmp8"]])
                            tt(t8r[:], t8[:], tmp8[:], ALU.subtract, [B["t8"], B["tmp8"]], [B["t8r"]])
                            an2 = sb2(f"an2{l}_{g0}", [64, NG, 129], F32)
                            nv = cst[0:64, C_NIDX:C_NIDX + 129]
                            tt(an2[:], t8r[:].unsqueeze(2).broadcast_to([64, NG, 129]), nv.unsqueeze(1).broadcast_to([64, NG, 129]), ALU.mult, [B["t8r"], B["cst"]], [B["an2"]])
                            cs2 = sb2(f"cs2{l}_{g0}", [64, 2, NG, 129], F32)
                            tm2 = sb2(f"tm2{l}_{g0}", [64, NG, 129], F32)
                            for ri, shift in ((0, 0.25), (1, 0.0)):
                                ts(tm2[:], an2[:], shift, MAGIC, ALU.add, ALU.add, [B["an2"]], [B["tm2"]])
                                ts(tm2[:], tm2[:], -MAGIC, None, ALU.add, None, [B["tm2"]], [B["tm2"]])
                                stt(cs2[:, ri], an2[:], shift, tm2[:], ALU.add, ALU.subtract, [B["an2"], B["tm2"]], [B["cs2"]])
                                act(cs2[:, ri], cs2[:, ri], AF.Sin, [B["cs2"]], [B["cs2"]], scale=TWO_PI)
                                kb.dma("sync", tabCS[l, ri, :, g0 * 129:(g0 + NG) * 129], cs2[:, ri].rearrange("p g n -> p (g n)"), reads=[B["cs2"]], writes=[B["tabCS"]])
                    kb.barrier()
        kb.barrier()
        if "s5" in flags:
            for l in range(depth):
                for dq in range(2):
                    for h in range(2):
                        kb.dma("sync", rho8s[h * 64:(h + 1) * 64, l, dq * 8:(dq + 1) * 8], tabRho[l, :, dq * 16 + h * 8:dq * 16 + h * 8 + 8],
                               reads=[B["tabRho"]], writes=[B["rho8s"]])

        def rms_stats(src_ap, src_bufs, sq8, ssp, rstd, tag):
            act(sq8[:], src_ap, AF.Square, list(src_bufs), [B[tag, "sq8"]])
            for c in range(DC):
                mm(ssp[:], onesb[:], sq8[:, c, :], c == 0, c == DC - 1, [B["onesb"], B[tag, "sq8"]], [B[tag, "ssp"]])
            act(rstd[:], ssp[:], AF.Ln, [B[tag, "ssp"]], [B[tag, "rstd"]], scale=1.0 / D, bias=EPS)
            act(rstd[:], rstd[:], AF.Exp, [B[tag, "rstd"]], [B[tag, "rstd"]], scale=-0.5)

        def prenorm(l, i, xn, sq8, ssp, rstd):
            for blk in range(NBLK):
                cs = slice(blk * 512, (blk + 1) * 512)
                rms_stats(hT[:, :, cs], [B["h", c, blk] for c in range(DC)], sq8, ssp, rstd, "pre")
                for c in range(DC):
                    stt(xn[:, c, cs], hT[:, c, cs], fpc(FP_G(l, i, c)), rstd[:], ALU.mult, ALU.mult,
                        [B["h", c, blk], B["FP"], B["pre", "rstd"]], [B["xn", c, blk]])

        def post_residual(gcol_fn, blk, ysb, sq8, ssp, rstd, tmpt):
            cs = slice(blk * 512, (blk + 1) * 512)
            rms_stats(ysb[:], [B["ysb", c] for c in range(DC)], sq8, ssp, rstd, "post")
            for m in range(DC):
                t = tmpt[m % 2]
                stt(t[:], ysb[:, m, :], fpc(gcol_fn(m)), rstd[:], ALU.mult, ALU.mult, [B["ysb", m], B["FP"], B["post", "rstd"]], [B["ptmp", m % 2], B["sg", m % 2]])
                tt(hT[:, m, cs], hT[:, m, cs], t[:], ALU.add, [B["h", m, blk], B["ptmp", m % 2]], [B["h", m, blk]])

        def ffn_phase(l, f):
            kb.barrier()
            with contextlib.ExitStack() as sc:
                def sb(name, shape, dt):
                    return sc.enter_context(nc.sbuf_tensor(_u(name), list(shape), dt))

                def ps(name, shape, dt=F32):
                    return _psum(nc, sc, name, list(shape), dt)
                xn = sb("xn", [128, DC, TS], BF16)
                actb = sb("actb", [128, FC, TS], BF16)
                wd = sb("wd", [128, FC, D], BF16)
                NW = 2
                wg = [sb(f"wg{i}", [128, DC, 256], BF16) for i in range(NW)]
                wu = [sb(f"wu{i}", [128, DC, 256], BF16) for i in range(NW)]
                ysb = sb("ysb", [128, DC, 512], BF16)
                sqt = sb("sq8", [128, DC, 512], BF16)
                rstd = sb("rstd", [128, 512], F32)
                sgt = [sb(f"sg{i}", [128, 512], F32) for i in range(2)]
                tmpt = sgt
                ssp = ps("ssp", [128, 512])
                pg = [ps(f"pg{i}", [128, 512]) for i in range(2)]
                pu = [ps(f"pu{i}", [128, 512]) for i in range(2)]
                py = [ps(f"py{i}", [128, 512]) for i in range(2)]

                prenorm(l, 4 * f, xn, sqt, ssp, rstd)
                wgv = w_gate[l, f].rearrange("(c p) n -> p c n", p=128)
                wuv = w_up[l, f].rearrange("(c p) n -> p c n", p=128)
                wdv = w_down[l, f].rearrange("(j p) n -> p j n", p=128)
                cnt = 0
                wd_next = 0
                for jg in range(FC // 2):
                    s = jg % NW
                    if jg == 0:
                        for cq in range(4):
                            kb.dma("gpsimd", wg[s][:, 2 * cq:2 * cq + 2, :], wgv[:, 2 * cq:2 * cq + 2, 0:256], writes=[B["wg", s, cq]])
                        for cq in range(4):
                            kb.dma("gpsimd", wu[s][:, 2 * cq:2 * cq + 2, :], wuv[:, 2 * cq:2 * cq + 2, 0:256], writes=[B["wu", s, cq]])
                    else:
                        kb.dma("gpsimd", wg[s][:], wgv[:, :, jg * 256:(jg + 1) * 256], writes=[B["wg", s, cq] for cq in range(4)])
                        kb.dma("gpsimd", wu[s][:], wuv[:, :, jg * 256:(jg + 1) * 256], writes=[B["wu", s, cq] for cq in range(4)])
                    if jg >= 1:
                        for _ in range(3):
                            if wd_next < FC:
                                kb.dma("gpsimd", wd[:, wd_next, :], wdv[:, wd_next, :], writes=[B["wd", wd_next]])
                                wd_next += 1
                    for blk in range(NBLK):
                        cs = slice(blk * 512, (blk + 1) * 512)
                        for jj in range(2):
                            j = jg * 2 + jj
                            k = cnt % 2
                            cnt += 1
                            for c in range(DC):
                                mm(pg[k][:], wg[s][:, c, jj * 128:(jj + 1) * 128], xn[:, c, cs], c == 0, c == DC - 1,
                                   [B["wg", s, c // 2], B["xn", c, blk]], [B["pg", k]])
                            for c in range(DC):
                                mm(pu[k][:], wu[s][:, c, jj * 128:(jj + 1) * 128], xn[:, c, cs], c == 0, c == DC - 1,
                                   [B["wu", s, c // 2], B["xn", c, blk]], [B["pu", k]])
                            act(sgt[k][:], pg[k][:], AF.Silu, [B["pg", k]], [B["sg", k]])
                            tt(actb[:, j, cs], sgt[k][:], pu[k][:], ALU.mult, [B["sg", k], B["pu", k]], [B["actb", j, blk]])
                while wd_next < FC:
                    kb.dma("gpsimd", wd[:, wd_next, :], wdv[:, wd_next, :], writes=[B["wd", wd_next]])
                    wd_next += 1
                cnt = 0
                for blk in range(NBLK):
                    cs = slice(blk * 512, (blk + 1) * 512)
                    for m in range(DC):
                        k = cnt % 2
                        cnt += 1
                        for j in range(FC):
                            mm(py[k][:], wd[:, j, m * 128:(m + 1) * 128], actb[:, j, cs], j == 0, j == FC - 1,
                               [B["wd", j], B["actb", j, blk]], [B["py", k]])
                        cp(ysb[:, m, :], py[k][:], [B["py", k]], [B["ysb", m]])
                    post_residual(lambda m: FP_GS(l, f, m), blk, ysb, sqt, ssp, rstd, tmpt)

        def mixer_phase(l, seg):
            kb.barrier()
            with contextlib.ExitStack() as sc:
                def sb(name, shape, dt):
                    return sc.enter_context(nc.sbuf_tensor(_u(name), list(shape), dt))

                def ps(name, shape, dt=F32):
                    return _psum(nc, sc, name, list(shape), dt)
                xn = sb("xn", [128, DC, TS], BF16)
                oT = sb("oT", [128, 12, TS], BF16)
                NW = 2
                wsl = [sb(f"wsl{i}", [128, DC, 512], BF16) for i in range(NW)]
                wstate = {"n": 0}
                winv = w_in[l].rearrange("(c p) n -> p c n", p=128)

                def wload(col0, ncols=512):
                    s = wstate["n"] % NW
                    wstate["n"] += 1
                    for cq in range(4):
                        kb.dma("gpsimd", wsl[s][:, 2 * cq:2 * cq + 2, 0:ncols], winv[:, 2 * cq:2 * cq + 2, col0:col0 + ncols], writes=[B["wsl", s, cq]])
                    return s

                pre = {}
                if "s5" in flags:
                    pre["u"] = wload(3592)
                with contextlib.ExitStack() as s0:
                    sqt = s0.enter_context(nc.sbuf_tensor(_u("sq8"), [128, DC, 512], BF16))
                    rstd = s0.enter_context(nc.sbuf_tensor(_u("rstd"), [128, 512], F32))
                    ssp = _psum(nc, s0, "ssp", [128, 512], F32)
                    prenorm(l, 2, xn, sqt, ssp, rstd)
                    kb.barrier()

                if not all(k in flags for k in ("hg", "ssd", "s5")):
                    A("vector", lambda e: e.memset(oT[:], 0.0), (), [B["oT", j] for j in range(12)])

                if "s5" in flags:
                    with contextlib.ExitStack() as s5:
                        def sb5(name, shape, dt):
                            return s5.enter_context(nc.sbuf_tensor(_u(name), list(shape), dt))

                        def ps5(name, shape, dt=F32):
                            return _psum(nc, s5, name, list(shape), dt)
                        uS = sb5("uS", [128, 4, 8, 128], F32)
                        U = sb5("U", [128, 32, 128], BF16)
                        gw = sb5("gw", [128, 4, 512], BF16)
                        pu_ = [ps5(f"pu5{i}", [128, 512]) for i in range(2)]
                        kb.dma("gpsimd", gw[:], glu_w[l].rearrange("(c p) n -> p c n", p=128), writes=[B["gw"]])
                        s = pre["u"]
                        k = 0
                        with contextlib.ExitStack() as s5t:
                            uSb = [s5t.enter_context(nc.sbuf_tensor(_u(f"uSb{i}"), [128, 8, 128], BF16)) for i in range(2)]
                            for ch in range(4):
                                for blk in range(NBLK):
                                    pst = pu_[k % 2]
                                    pb = B["pu5", k % 2]
                                    cs = slice(blk * 512, (blk + 1) * 512)
                                    for c in range(DC):
                                        mm(pst[:], wsl[s][:, c, ch * 128:(ch + 1) * 128], xn[:, c, cs], c == 0, c == DC - 1,
                                           [B["wsl", s, c // 2], B["xn", c, blk]], [pb])
                                    cp(uS[:, ch, :, blk * 64:(blk + 1) * 64], pst[:].rearrange("p (n s) -> p s n", s=8), [pb], [B["uS", ch]])
                                    cp(uSb[ch % 2][:, :, blk * 64:(blk + 1) * 64], pst[:].rearrange("p (n s) -> p s n", s=8), [pb], [B["uSb", ch % 2]])
                                    k += 1
                                kb.dma("sync", ud[ch * 128:(ch + 1) * 128, :], uSb[ch % 2][:].rearrange("p s n -> p (s n)"), reads=[B["uSb", ch % 2]], writes=[B["ud"]])
                            if "hg" in flags:
                                pre["f"] = wload(512)
                                pre["q"] = wload(0)
                        udv = ud.rearrange("(g c) (s n) -> s c g n", c=16, s=8)
                        for s_ in range(8):
                            kb.dma("sync", U[s_ * 16:(s_ + 1) * 16, :, :], udv[s_], reads=[B["ud"]], writes=[B["U"]])
                        ydv = yd.rearrange("(g c) (t n) -> t c g n", c=16, t=8)
                        with contextlib.ExitStack() as s6:
                            def sb6(name, shape, dt):
                                return s6.enter_context(nc.sbuf_tensor(_u(name), list(shape), dt))
                            kb.barrier()
                            cs2 = sb6("cs2m", [128, 2, 8, 129], F32)
                            Tm = sb6("Tm", [128, 16, 128], BF16)
                            Rm = sb6("Rm", [128, 2, 16, 64], BF16)
                            Om = sb6("Om", [128, 2, 8, 128], BF16)
                            Ysb = sb6("Ysb", [128, 16, 128], F32)
                            Eq = sb6("Eq", [128, 2, 8, 128], F32)
                            Zq = sb6("Zq", [128, 2, 8, 129], F32)
                            Xq = Zq
                            Vq = sb6("Vq", [128, 2, 8, 129], F32)
                            Xb = sb6("Xb", [128, 2, 8, 128], BF16)
                            tq_ = sb6("tq5", [128, 8, 129], F32)
                            pe_ = [_psum(nc, s6, f"pe5{i}", [128, 2, 512], F32) for i in range(2)]
                            py_ = [_psum(nc, s6, f"py5{i}", [128, 512], F32) for i in range(2)]

                            def load_tabs(dq_):
                                g0_ = dq_ * 16
                                kb.dma("sync", Tm[:].rearrange("p g n -> p (g n)"), tabT[l, :, g0_ * 128:(g0_ + 16) * 128], reads=[B["tabT"]], writes=[B["Tm"]])
                                for ri in range(2):
                                    kb.dma("sync", Rm[:, ri].rearrange("p g n -> p (g n)"), tabR[l, ri, :, g0_ * 64:(g0_ + 16) * 64], reads=[B["tabR"]], writes=[B["Rm"]])
                                    for h in range(2):
                                        gh = g0_ + h * 8
                                        kb.dma("sync", Om[h * 64:(h + 1) * 64, ri].rearrange("p g n -> p (g n)"), tabO[l, ri, :, gh * 128:(gh + 8) * 128],
                                               reads=[B["tabO"]], writes=[B["Om"]])

                            def load_cs(dq_):
                                for ri in range(2):
                                    for h in range(2):
                                        gh = dq_ * 16 + h * 8
                                        kb.dma("sync", cs2[h * 64:(h + 1) * 64, ri].rearrange("p g n -> p (g n)"), tabCS[l, ri, :, gh * 129:(gh + 8) * 129],
                                               reads=[B["tabCS"]], writes=[B["cs2m"]])
                            load_tabs(0)
                            load_cs(0)
                            for dq in range(2):
                                G0 = dq * 16
                                XS = slice(dq * 8, (dq + 1) * 8)
                                if dq > 0:
                                    load_tabs(dq)
                                for hb in range(2):
                                    for h in range(2):
                                        hp = slice(h * 64, (h + 1) * 64)
                                        for gg in range(4):
                                            idx = hb * 4 + gg
                                            g = G0 + h * 8 + idx
                                            for ri in range(2):
                                                mm(pe_[hb][hp, ri, gg * 128:(gg + 1) * 128], Rm[:, ri, h * 8 + idx, :], U[:, g, :], True, True,
                                                   [B["Rm"], B["U"]], [B["pe5", hb]])
                                    cp(Eq[:, :, hb * 4:(hb + 1) * 4, :], pe_[hb][:].rearrange("p r (g n) -> p r g n", n=128), [B["pe5", hb]], [B["Eq"]])
                                cc = cs2[:, 0, :, 1:129]
                                ss_ = cs2[:, 1, :, 1:129]
                                Er, Ei = Eq[:, 0], Eq[:, 1]
                                t128 = tq_[:, :, 0:128]
                                tt(Zq[:, 0, :, 0:128], cc, Er, ALU.mult, [B["cs2m"], B["Eq"]], [B["Zq0"]])
                                tt(t128, ss_, Ei, ALU.mult, [B["cs2m"], B["Eq"]], [B["tq5"]])
                                tt(Zq[:, 0, :, 0:128], Zq[:, 0, :, 0:128], t128, ALU.add, [B["Zq0"], B["tq5"]], [B["Zq0"]])
                                tt(Zq[:, 1, :, 0:128], cc, Ei, ALU.mult, [B["cs2m"], B["Eq"]], [B["Zq1"]])
                                tt(t128, ss_, Er, ALU.mult, [B["cs2m"], B["Eq"]], [B["tq5"]])
                                tt(Zq[:, 1, :, 0:128], Zq[:, 1, :, 0:128], t128, ALU.subtract, [B["Zq1"], B["tq5"]], [B["Zq1"]])
                                for ri in range(2):
                                    cp(Vq[:, ri, :, 0], Xc[:, l, ri, XS], [B["Xc"]], [B["Vq", ri]], eng="vector")
                                    for idx in range(8):
                                        c_ = dq * 8 + idx
                                        scan(Vq[:, ri, idx, 1:129], rho8s[:, l, c_:c_ + 1].broadcast_to([128, 128]),
                                             Zq[:, ri, idx, 0:128], Xc[:, l, ri, c_:c_ + 1],
                                             [B["rho8s"], B["Zq0"], B["Zq1"], B["Xc"]], [B["Vq", ri]])
                                ca = cs2[:, 0]
                                sa = cs2[:, 1]
                                tt(Xq[:, 0], ca, Vq[:, 0], ALU.mult, [B["cs2m"], B["Vq", 0]], [B["Xq0"], B["Zq0"], B["Zq1"]])
                                tt(tq_[:], sa, Vq[:, 1], ALU.mult, [B["cs2m"], B["Vq", 1]], [B["tq5"]])
                                tt(Xq[:, 0], Xq[:, 0], tq_[:], ALU.subtract, [B["Xq0"], B["tq5"]], [B["Xq0"]])
                                tt(Xq[:, 1], ca, Vq[:, 1], ALU.mult, [B["cs2m"], B["Vq", 1]], [B["Xq1"], B["Zq0"], B["Zq1"]])
                                tt(tq_[:], sa, Vq[:, 0], ALU.mult, [B["cs2m"], B["Vq", 0]], [B["tq5"]])
                                tt(Xq[:, 1], Xq[:, 1], tq_[:], ALU.add, [B["Xq1"], B["tq5"]], [B["Xq1"]])
                                for ri in range(2):
                                    cp(Xb[:, ri], Xq[:, ri, :, 0:128], [B["Xq0"], B["Xq1"]], [B["Xb"]])
                                    cp(Xc[:, l, ri, XS], Xq[:, ri, :, 128], [B["Xq0"], B["Xq1"], B["Vq", 0], B["Vq", 1]], [B["Xc"]], eng="vector")
                                if dq + 1 < 2:
                                    load_cs(dq + 1)
                                for quad in range(4):
                                    pyq = py_[quad % 2]
                                    bq = B["py5", quad % 2]
                                    for gg in range(4):
                                        gl16 = quad * 4 + gg
                                        h, idx = gl16 // 8, gl16 % 8
                                        hp = slice(h * 64, (h + 1) * 64)
                                        g = G0 + gl16
                                        o_ = pyq[:, gg * 128:(gg + 1) * 128]
                                        mm(o_, Tm[:, gl16, :], U[:, g, :], True, False, [B["Tm"], B["U"]], [bq])
                                        mm(o_, Om[hp, 0, idx, :], Xb[hp, 0, idx, :], False, False, [B["Om"], B["Xb"]], [bq])
                                        mm(o_, Om[hp, 1, idx, :], Xb[hp, 1, idx, :], False, True, [B["Om"], B["Xb"]], [bq])
                                    cp(Ysb[:, quad * 4:quad * 4 + 4, :], pyq[:].rearrange("p (g n) -> p g n", n=128), [bq], [B["Ysb"]])
                                for t_ in range(8):
                                    kb.dma("sync", ydv[t_][:, G0:G0 + 16, :], Ysb[t_ * 16:(t_ + 1) * 16, :, :], reads=[B["Ysb"]], writes=[B["yd"]])
                        kb.barrier()
                        with contextlib.ExitStack() as s7:
                            def sb7(name, shape, dt):
                                return s7.enter_context(nc.sbuf_tensor(_u(name), list(shape), dt))
                            yS = sb7("yS", [128, 4, TS], F32)
                            y2 = sb7("y2", [128, 4, TS], F32)
                            y2b = sb7("y2b", [128, 4, TS], BF16)
                            t5 = [sb7(f"t5{i}", [128, TS], F32) for i in range(2)]
                            sgl = [sb7(f"sgl{i}", [128, 512], F32) for i in range(2)]
                            for ch in range(4):
                                kb.dma("sync", yS[:, ch, :], yd[ch * 128:(ch + 1) * 128, :], reads=[B["yd"]], writes=[B["yS", ch]])
                            KG = 2.0 * math.sqrt(2.0 / math.pi)
                            for ch in range(4):
                                t = t5[ch % 2]
                                tb = B["t5", ch % 2]
                                y1 = yS[:, ch, :]
                                stt(y1, uS[:, ch].rearrange("p s n -> p (s n)"), fpc(FP_S5D(l, ch)), y1, ALU.mult, ALU.add,
                                    [B["uS", ch], B["FP"], B["yS", ch]], [B["yS", ch]])
                                tt(t[:], y1, y1, ALU.mult, [B["yS", ch]], [tb])
                                ts(t[:], t[:], 0.044715, 1.0, ALU.mult, ALU.add, [tb], [tb])
                                tt(t[:], t[:], y1, ALU.mult, [tb, B["yS", ch]], [tb])
                                act(t[:], t[:], AF.Sigmoid, [tb], [tb], scale=KG)
                                tt(y2[:, ch, :], y1, t[:], ALU.mult, [B["yS", ch], tb], [B["y2", ch]])
                                cp(y2b[:, ch, :], y2[:, ch, :], [B["y2", ch]], [B["y2b", ch]])
                            k = 0
                            for oc in range(4):
                                for blk in range(NBLK):
                                    cs = slice(blk * 512, (blk + 1) * 512)
                                    pst = pu_[k % 2]
                                    for ic in range(4):
                                        mm(pst[:], gw[:, ic, oc * 128:(oc + 1) * 128], y2b[:, ic, cs], ic == 0, ic == 3,
                                           [B["gw"], B["y2b", ic]], [B["pu5", k % 2]])
                                    sg = sgl[k % 2]
                                    act(sg[:], pst[:], AF.Sigmoid, [B["pu5", k % 2]], [B["sgl", k % 2]], bias=fpc(FP_GLUB(l, oc)))
                                    o_ = oT[:, 8 + oc, :].rearrange("p (n t) -> p t n", t=8)[:, blk * 4:(blk + 1) * 4, :]
                                    tt(o_, y2[:, oc, cs].rearrange("p (t n) -> p t n", n=128), sg[:].rearrange("p (t n) -> p t n", n=128), ALU.mult,
                                       [B["y2", oc], B["sgl", k % 2]], [B["oT", 8 + oc]])
                                    k += 1
                        kb.barrier()

                if "hg" in flags:
                    with contextlib.ExitStack() as s1:
                        def sb1(name, shape, dt):
                            return s1.enter_context(nc.sbuf_tensor(_u(name), list(shape), dt))

                        def ps1(name, shape, dt=F32):
                            return _psum(nc, s1, name, list(shape), dt)
                        qtT = sb1("qtT", [128, 4, TS], BF16)
                        ktT = sb1("ktT", [128, 4, TS], BF16)
                        elast = sb1("elast", [128, 4, TS // 64], F32)
                        vtok = sb1("vtok", [128, NT, 512], BF16)
                        wgt = sb1("wgt", [128, NT, 512], F32)
                        fT2 = [sb1(f"fT{i}", [128, TS], F32) for i in range(2)]
                        lf2 = [sb1(f"lf{i}", [128, TS], F32) for i in range(2)]
                        bb2 = [sb1(f"bb{i}", [128, TS], F32) for i in range(2)]
                        qs2 = [sb1(f"qs{i}", [128, TS], F32) for i in range(2)]
                        pq = [ps1(f"pq{i}", [128, 512]) for i in range(2)]
                        sf = pre["f"] if "f" in pre else wload(512)
                        sq_ = pre["q"] if "q" in pre else wload(0)
                        kqc = [0]

                        def hg_prep_step(step, hd, S):
                            fT, lf, bb, qs = fT2[S], lf2[S], bb2[S], qs2[S]
                            bf, bl, bbb, bq = B["fT", S], B["lf", S], B["bb", S], B["qs", S]
                            if step == 0:
                                for blk in range(NBLK):
                                    cs = slice(blk * 512, (blk + 1) * 512)
                                    pst = pq[kqc[0] % 2]
                                    pb = B["pq", kqc[0] % 2]
                                    kqc[0] += 1
                                    for c in range(DC):
                                        mm(pst[:], wsl[sf][:, c, hd * 128:(hd + 1) * 128], xn[:, c, cs], c == 0, c == DC - 1,
                                           [B["wsl", sf, c // 2], B["xn", c, blk]], [pb])
                                    act(fT[:, cs], pst[:], AF.Sigmoid, [pb], [bf])
                            elif step == 1:
                                ts(fT[:], fT[:], fpc(FP_OML(l, hd)), fpc(FP_LB(l, hd)), ALU.mult, ALU.add, [bf, B["FP"]], [bf])
                            elif step == 2:
                                act(lf[:], fT[:], AF.Ln, [bf], [bl])
                            elif step == 3:
                                scan(bb[:], rstm, lf[:], 0.0, [B["cst"], bl], [bbb])
                            elif step == 4:
                                act(lf[:], bb[:], AF.Exp, [bbb], [bl])
                                act(bb[:], bb[:], AF.Exp, [bbb], [bbb], scale=-1.0)
                            elif step == 5:
                                ts(fT[:], fT[:], -1.0, 1.0, ALU.mult, ALU.add, [bf], [bf])
                                tt(ktT[:, hd, :], fT[:], bb[:], ALU.mult, [bf, bbb], [B["ktT", hd]])
                                cp(elast[:, hd, :], lf[:].rearrange("p (n c) -> p n c", c=64)[:, :, 63], [bl], [B["elast"]], eng="vector")
                            elif step == 6:
                                for blk in range(NBLK):
                                    cs = slice(blk * 512, (blk + 1) * 512)
                                    pst = pq[kqc[0] % 2]
                                    pb = B["pq", kqc[0] % 2]
                                    kqc[0] += 1
                                    for c in range(DC):
                                        mm(pst[:], wsl[sq_][:, c, hd * 128:(hd + 1) * 128], xn[:, c, cs], c == 0, c == DC - 1,
                                           [B["wsl", sq_, c // 2], B["xn", c, blk]], [pb])
                                    act(qs[:, cs], pst[:], AF.Silu, [pb], [bq])
                            elif step == 7:
                                tt(qtT[:, hd, :], qs[:], lf[:], ALU.mult, [bq, bl], [B["qtT", hd]])

                        for hp in range(2):
                            for step in range(8):
                                for S in range(2):
                                    hg_prep_step(step, hp * 2 + S, S)
                        kq = kqc[0]
                        si = wload(1024)
                        sg_ = wload(1536)
                        gnb = RP[:, RP_GN + l * 128:RP_GN + (l + 1) * 128].unsqueeze(1).broadcast_to([128, 4, 128])
                        for it in range(NT):
                            blk = it // 4
                            pst = pq[kq % 2]
                            pb = B["pq", kq % 2]
                            kq += 1
                            for c in range(DC):
                                mm(pst[:], xn[:, c, it * 128:(it + 1) * 128], wsl[si][:, c, :], c == 0, c == DC - 1,
                                   [B["wsl", si, c // 2], B["xn", c, blk]], [pb])
                            cp(vtok[:, it, :], pst[:], [pb], [B["vtok", it]])
                            pst = pq[kq % 2]
                            pb = B["pq", kq % 2]
                            kq += 1
                            for c in range(DC):
                                mm(pst[:], xn[:, c, it * 128:(it + 1) * 128], wsl[sg_][:, c, :], c == 0, c == DC - 1,
                                   [B["wsl", sg_, c // 2], B["xn", c, blk]], [pb])
                            act(wgt[:, it, :], pst[:], AF.Silu, [pb], [B["wgt", it]])
                            tt(wgt[:, it, :].rearrange("p (h v) -> p h v", v=128), wgt[:, it, :].rearrange("p (h v) -> p h v", v=128), gnb, ALU.mult,
                               [B["wgt", it], B["RP"]], [B["wgt", it]])
                        if "ssd" in flags:
                            pre["x0"] = wload(2560)
                            pre["x1"] = wload(3072)
                        ptk = ps1("ptk", [128, 4, 128], BF16)
                        pss = ps1("pss", [128, 4, 128])
                        pso = ps1("pso", [128, 512])
                        psS = ps1("psS", [128, 8, 128])
                        pto = ps1("pto", [128, 4, 128], BF16)
                        ktok2 = [sb1(f"ktok{i}", [128, 4, 128], BF16) for i in range(2)]
                        smk2 = [sb1(f"smk{i}", [128, 4, 128], BF16) for i in range(2)]
                        stmp = sb1("stmp", [128, 4, 128], F32)
                        ssq = sb1("ssq", [128, 4], F32)
                        junk = sb1("junk", [128, 128], F32)
                        og = sb1("og", [128, 512], BF16)

                        def hg_s1(it):
                            k2 = it % 2
                            tc_ = slice(it * 128, (it + 1) * 128)
                            for hd in range(4):
                                tr(ptk[:, hd, :], ktT[:, hd, tc_], identb[:], [B["ktT", hd], B["identb"]], [B["ptk"]])
                            cp(ktok2[k2][:], ptk[:], [B["ptk"]], [B["ktok", k2]], eng="vector")
                            for hd in range(4):
                                mm(pss[:, hd, :], ktT[:, hd, tc_], qtT[:, hd, tc_], True, True, [B["ktT", hd], B["qtT", hd]], [B["pss"]])
                            tt(smk2[k2][:], pss[:], mhg.unsqueeze(1).broadcast_to([128, 4, 128]), ALU.mult, [B["pss"], B["cst"]], [B["smk", k2]])

                        def hg_s2(it):
                            k2 = it % 2
                            tc_ = slice(it * 128, (it + 1) * 128)
                            ktok, smk = ktok2[k2], smk2[k2]
                            for half in range(2):
                                rs_ = slice(half * 64, (half + 1) * 64)
                                ch_i = it * 2 + half
                                for hd in range(4):
                                    hc = slice(hd * 128, (hd + 1) * 128)
                                    mm(pso[rs_, hc], smk[rs_, hd, rs_], vtok[rs_, it, hc], True, False, [B["smk", k2], B["vtok", it]], [B["pso"]])
                                    mm(pso[rs_, hc], qtT[:, hd, it * 128 + half * 64:it * 128 + (half + 1) * 64], Shgb[:, l, hd, :], False, True,
                                       [B["qtT", hd], B["Shgb", hd]], [B["pso"]])
                                    pS_ = psS[:, (hd % 2) * 4, :]
                                    pSb = B["psS", hd % 2]
                                    mm(pS_, ktok[rs_, hd, :], vtok[rs_, it, hc], True, True, [B["ktok", k2], B["vtok", it]], [pSb])
                                    e_ = elast[:, hd, ch_i:ch_i + 1]
                                    ts(stmp[:, hd, :], Shg[:, l, hd, :], e_, None, ALU.mult, None, [B["Shg", hd], B["elast"]], [B["stmp", hd]])
                                    stt(Shg[:, l, hd, :], pS_, e_, stmp[:, hd, :], ALU.mult, ALU.add,
                                        [pSb, B["elast"], B["stmp", hd]], [B["Shg", hd]])
                                    cp(Shgb[:, l, hd, :], Shg[:, l, hd, :], [B["Shg", hd]], [B["Shgb", hd]])
                            for hd in range(4):
                                hc = slice(hd * 128, (hd + 1) * 128)
                                act(junk[:], pso[:, hc], AF.Square, [B["pso"]], [B["junk"], B["ssq"]], accum_out=ssq[:, hd:hd + 1])
                            act(ssq[:], ssq[:], AF.Ln, [B["ssq"]], [B["ssq"]], scale=1.0 / 128, bias=EPS)
                            act(ssq[:], ssq[:], AF.Exp, [B["ssq"]], [B["ssq"]], scale=-0.5)
                            for hd in range(4):
                                hc = slice(hd * 128, (hd + 1) * 128)
                                stt(og[:, hc], pso[:, hc], ssq[:, hd:hd + 1], wgt[:, it, hc], ALU.mult, ALU.mult,
                                    [B["pso"], B["ssq"], B["wgt", it]], [B["og"]])
                            for j in range(4):
                                tr(pto[:, j, :], og[:, j * 128:(j + 1) * 128], identb[:], [B["og"], B["identb"]], [B["pto"]])
                            cp(oT[:, 0:4, tc_], pto[:], [B["pto"]], [B["oT", j] for j in range(4)])

                        hg_s1(0)
                        for it in range(NT):
                            if it + 1 < NT:
                                hg_s1(it + 1)
                            hg_s2(it)
                        kb.barrier()

                if "ssd" in flags:
                    with contextlib.ExitStack() as s2:
                        def sb2(name, shape, dt):
                            return s2.enter_context(nc.sbuf_tensor(_u(name), list(shape), dt))

                        def ps2(name, shape, dt=F32):
                            return _psum(nc, s2, name, list(shape), dt)
                        xT = sb2("xT", [128, 4, TS], BF16)
                        BT = sb2("BT", [128, 2, TS], BF16)
                        CT = sb2("CT", [128, 2, TS], BF16)
                        rawx = [sb2(f"rawx{i}", [128, TS + 3], F32) for i in range(2)]
                        acc = [sb2(f"acc{i}", [128, TS], F32) for i in range(2)]
                        zs = sb2("zs", [128, NT, 512], BF16)
                        dtt = sb2("dtt", [128, NT, 8], F32)
                        adt = sb2("adt", [128, NT, 8], F32)
                        xtok = sb2("xtok", [128, NT, 512], BF16)
                        xdt = sb2("xdt", [128, NT, 512], BF16)
                        Btok = sb2("Btok", [128, NT, 256], BF16)
                        wdt = sb2("wdt", [128, DC, 8], BF16)
                        pq = [ps2(f"pq{i}", [128, 512]) for i in range(2)]
                        kq = 0
                        kb.dma("gpsimd", wdt[:], winv[:, :, 3584:3592], writes=[B["wdt"]])
                        sx = [pre["x0"], pre["x1"]] if "x0" in pre else [wload(2560), wload(3072)]
                        for c8 in range(8):
                            r = rawx[c8 % 2]
                            rb = B["rawx", c8 % 2]
                            cp(r[:, 0:3], halo[:, l, c8, :], [B["halo"]], [rb], eng="vector")
                            for blk in range(NBLK):
                                pst = pq[kq % 2]
                                pb = B["pq", kq % 2]
                                kq += 1
                                cs = slice(blk * 512, (blk + 1) * 512)
                                s = sx[c8 // 4]
                                for c in range(DC):
                                    mm(pst[:], wsl[s][:, c, (c8 % 4) * 128:(c8 % 4 + 1) * 128], xn[:, c, cs], c == 0, c == DC - 1,
                                       [B["wsl", s, c // 2], B["xn", c, blk]], [pb])
                                cp(r[:, 3 + blk * 512:3 + (blk + 1) * 512], pst[:], [pb], [rb])
                            cp(halo[:, l, c8, :], r[:, TS:TS + 3], [rb], [B["halo"]], eng="vector")
                            a_ = acc[c8 % 2]
                            ab = B["acc", c8 % 2]
                            ts(a_[:], r[:, 0:TS], fpc(FP_CONVW(l, 0, c8)), None, ALU.mult, None, [rb, B["FP"]], [ab])
                            for k_ in range(1, 4):
                                stt(a_[:], r[:, k_:TS + k_], fpc(FP_CONVW(l, k_, c8)), a_[:], ALU.mult, ALU.add, [rb, B["FP"], ab], [ab])
                            if c8 < 4:
                                dst, db = xT[:, c8, :], B["xT", c8]
                            elif c8 < 6:
                                dst, db = BT[:, c8 - 4, :], B["BT", c8 - 4]
                            else:
                                dst, db = CT[:, c8 - 6, :], B["CT", c8 - 6]
                            act(dst, a_[:], AF.Silu, [ab, B["FP"]], [db], bias=fpc(FP_CONVB(l, c8)))
                        STOP = int(os.environ.get("SSD_STOP", "99"))
                        if STOP < 99:
                            A("vector", lambda e: e.memset(oT[:, 4:8, :], 0.0), (), [B["oT", j] for j in range(4, 8)])
                        sz = wload(2048) if STOP >= 2 else 0
                        psm = ps2("psm", [128, 512])
                        pdt = psm[:, 0:NT * 8].rearrange("p (t h) -> p t h", h=8)
                        for it in range(NT if STOP >= 2 else 0):
                            blk = it // 4
                            pst = pq[kq % 2]
                            pb = B["pq", kq % 2]
                            kq += 1
                            for c in range(DC):
                                mm(pst[:], xn[:, c, it * 128:(it + 1) * 128], wsl[sz][:, c, :], c == 0, c == DC - 1,
                                   [B["wsl", sz, c // 2], B["xn", c, blk]], [pb])
                            act(zs[:, it, :], pst[:], AF.Silu, [pb], [B["zs", it]])
                            for c in range(DC):
                                mm(pdt[:, it, :], xn[:, c, it * 128:(it + 1) * 128], wdt[:, c, :], c == 0, c == DC - 1,
                                   [B["wdt"], B["xn", c, blk]], [B["psm"]])
                        dtb = RP[:, RP_DTB + l * 8:RP_DTB + (l + 1) * 8].unsqueeze(1).broadcast_to([128, NT, 8])
                        arow = RP[:, RP_A + l * 8:RP_A + (l + 1) * 8].unsqueeze(1).broadcast_to([128, NT, 8])
                        if STOP >= 2:
                            tt(dtt[:], pdt[:], dtb, ALU.add, [B["psm"], B["RP"]], [B["dtt"]])
                            act(dtt[:], dtt[:], AF.Exp, [B["dtt"]], [B["dtt"]])
                            act(dtt[:], dtt[:], AF.Ln, [B["dtt"]], [B["dtt"]], bias=1.0)
                            tt(adt[:], dtt[:], arow, ALU.mult, [B["dtt"], B["RP"]], [B["adt"]])
                        ptx = ps2("ptx", [128, 6, 128], BF16)
                        for it in range(NT if STOP >= 3 else 0):
                            tc_ = slice(it * 128, (it + 1) * 128)
                            for j in range(4):
                                tr(ptx[:, j, :], xT[:, j, tc_], identb[:], [B["xT", j], B["identb"]], [B["ptx"]])
                            for g in range(2):
                                tr(ptx[:, 4 + g, :], BT[:, g, tc_], identb[:], [B["BT", g], B["identb"]], [B["ptx"]])
                            cp(xtok[:, it, :], ptx[:, 0:4, :].rearrange("p j n -> p (j n)"), [B["ptx"]], [B["xtok", it]])
                            cp(Btok[:, it, :], ptx[:, 4:6, :].rearrange("p j n -> p (j n)"), [B["ptx"]], [B["Btok", it]])
                            tt(xdt[:, it, :].rearrange("p (h d) -> p h d", d=64), xtok[:, it, :].rearrange("p (h d) -> p h d", d=64),
                               dtt[:, it, :].unsqueeze(2).broadcast_to([128, 8, 64]), ALU.mult, [B["xtok", it], B["dtt"]], [B["xdt", it]])
                        pa_ = psm[:, 64:80]
                        pd_ = [ps2(f"pd{i}", [128, 4, 128]) for i in range(2)]
                        psc = psm[:, 128:384].rearrange("p (g n) -> p g n", n=128)
                        pyd = pq[1]
                        pyo = ps2("pyo", [128, 512])
                        acs = sb2("acs", [128, 8], F32)
                        ea2 = [sb2(f"ea{i}", [128, 8], F32) for i in range(2)]
                        cd2 = [sb2(f"cd{i}", [128, 8], F32) for i in range(2)]
                        ds2 = [sb2(f"ds{i}", [128, 8], F32) for i in range(2)]
                        rh = [sb2(f"rh{i}", [128, 128], F32) for i in range(2)]
                        LT = sb2("LT", [128, 8, 128], F32)
                        scm = sb2("scm", [128, 2, 128], F32)
                        Wh2 = [sb2(f"Wh{i}", [128, 8, 128], BF16) for i in range(2)]
                        t1_ = sb2("t1s", [128, 512], F32)
                        yy = sb2("yy", [128, 512], F32)
                        xD = sb2("xD", [128, 512], F32)
                        xdtd = sb2("xdtd", [128, 512], BF16)
                        ss2 = sb2("ss2", [128, 2], F32)
                        junk2 = sb2("junk2", [128, 256], F32)
                        yn = sb2("yn", [128, 512], BF16)
                        drow = RP[:, RP_D + l * 8:RP_D + (l + 1) * 8].unsqueeze(2).broadcast_to([128, 8, 64])
                        v8 = lambda ap: ap.rearrange("p (h d) -> p h d", d=64)

                        def ssd_s1(it):
                            k2 = it % 2
                            tc_ = slice(it * 128, (it + 1) * 128)
                            ea, cd, ds_, Wh = ea2[k2], cd2[k2], ds2[k2], Wh2[k2]
                            mm(pa_[:, 0:8], tri, adt[:, it, :], True, True, [B["cst"], B["adt"]], [B["psm"]])
                            mm(pa_[:, 8:16], ones, adt[:, it, :], True, True, [B["cst"], B["adt"]], [B["psm"]])
                            cp(acs[:], pa_[:, 0:8], [B["psm"]], [B["acs"]])
                            act(ea[:], pa_[:, 0:8], AF.Exp, [B["psm"]], [B["ea", k2]])
                            act(cd[:], pa_[:, 8:16], AF.Exp, [B["psm"]], [B["cd", k2]])
                            tt(ds_[:], pa_[:, 8:16], acs[:], ALU.subtract, [B["psm"], B["acs"]], [B["ds", k2]])
                            act(ds_[:], ds_[:], AF.Exp, [B["ds", k2]], [B["ds", k2]])
                            for g in range(2):
                                mm(psc[:, g, :], BT[:, g, tc_], CT[:, g, tc_], True, True, [B["BT", g], B["CT", g]], [B["psm"]])
                            tt(scm[:], psc[:], tri.unsqueeze(1).broadcast_to([128, 2, 128]), ALU.mult, [B["psm"], B["cst"]], [B["scm"]])
                            for h in range(8):
                                r_ = rh[h % 2]
                                ts(r_[:], tri, adt[:, it, h:h + 1], None, ALU.mult, None, [B["cst"], B["adt"]], [B["rh", h % 2]])
                                mm(pd_[h // 4][:, h % 4, :], ups, r_[:], True, True, [B["cst"], B["rh", h % 2]], [B["pd", h // 4]])
                            for k_ in range(2):
                                act(LT[:, k_ * 4:(k_ + 1) * 4, :], pd_[k_][:], AF.Exp, [B["pd", k_]], [B["LT"]])
                            for g in range(2):
                                tt(Wh[:, g * 4:(g + 1) * 4, :], LT[:, g * 4:(g + 1) * 4, :], scm[:, g, :].unsqueeze(1).broadcast_to([128, 4, 128]), ALU.mult,
                                   [B["LT"], B["scm"]], [B["Wh", k2]])

                        def ssd_s2(it):
                            k2 = it % 2
                            tc_ = slice(it * 128, (it + 1) * 128)
                            ea, cd, ds_, Wh = ea2[k2], cd2[k2], ds2[k2], Wh2[k2]
                            for h in range(8):
                                mm(pyd[:, h * 64:(h + 1) * 64], Wh[:, h, :], xdt[:, it, h * 64:(h + 1) * 64], True, True, [B["Wh", k2], B["xdt", it]], [B["pq", 1]])
                            for g in range(2):
                                mm(pyo[:, g * 256:(g + 1) * 256], CT[:, g, tc_], Hstb[:, l, g * 256:(g + 1) * 256], True, True,
                                   [B["CT", g], B["Hstb"]], [B["pyo"]])
                            tt(v8(t1_[:]), v8(pyo[:]), ea[:].unsqueeze(2).broadcast_to([128, 8, 64]), ALU.mult, [B["pyo"], B["ea", k2]], [B["t1s"]])
                            tt(yy[:], pyd[:], t1_[:], ALU.add, [B["pq", 1], B["t1s"]], [B["yy"]])
                            tt(v8(xD[:]), v8(xtok[:, it, :]), drow, ALU.mult, [B["xtok", it], B["RP"]], [B["xD"]])
                            tt(yy[:], yy[:], xD[:], ALU.add, [B["yy"], B["xD"]], [B["yy"]])
                            tt(yy[:], yy[:], zs[:, it, :], ALU.mult, [B["yy"], B["zs", it]], [B["yy"]])
                            for g in range(2):
                                act(junk2[:], yy[:, g * 256:(g + 1) * 256], AF.Square, [B["yy"]], [B["junk2"], B["ss2"]], accum_out=ss2[:, g:g + 1])
                            act(ss2[:], ss2[:], AF.Ln, [B["ss2"]], [B["ss2"]], scale=1.0 / 256, bias=EPS)
                            act(ss2[:], ss2[:], AF.Exp, [B["ss2"]], [B["ss2"]], scale=-0.5)
                            for g in range(2):
                                gc = slice(g * 256, (g + 1) * 256)
                                stt(yn[:, gc], yy[:, gc], ss2[:, g:g + 1], RP[:, RP_NW + l * 512 + g * 256:RP_NW + l * 512 + (g + 1) * 256], ALU.mult, ALU.mult,
                                    [B["yy"], B["ss2"], B["RP"]], [B["yn"]])
                            for j in range(4):
                                tr(ptx[:, j, :], yn[:, j * 128:(j + 1) * 128], identb[:], [B["yn"], B["identb"]], [B["ptx"]])
                            cp(oT[:, 4:8, tc_], ptx[:, 0:4, :], [B["ptx"]], [B["oT", j] for j in range(4, 8)])
                            tt(v8(xdtd[:]), v8(xdt[:, it, :]), ds_[:].unsqueeze(2).broadcast_to([128, 8, 64]), ALU.mult, [B["xdt", it], B["ds", k2]], [B["xdtd"]])
                            pst = pq[0]
                            for g in range(2):
                                gc = slice(g * 256, (g + 1) * 256)
                                mm(pst[:, gc], Btok[:, it, g * 128:(g + 1) * 128], xdtd[:, gc], True, True, [B["Btok", it], B["xdtd"]], [B["pq", 0]])
                            tt(v8(Hst[:, l, :]), v8(Hst[:, l, :]), cd[:].unsqueeze(2).broadcast_to([128, 8, 64]), ALU.mult, [B["Hst"], B["cd", k2]], [B["Hst"]])
                            tt(Hst[:, l, :], Hst[:, l, :], pst[:], ALU.add, [B["Hst"], B["pq", 0]], [B["Hst"]])
                            cp(Hstb[:, l, :], Hst[:, l, :], [B["Hst"]], [B["Hstb"]])

                        ssd_s1(0)
                        for it in range(NT):
                            if it + 1 < NT:
                                ssd_s1(it + 1)
                            ssd_s2(it)
                        kb.barrier()

                if dbg and l == 0:
                    with contextlib.ExitStack() as sd:
                        of = sd.enter_context(nc.sbuf_tensor(_u("of"), [128, 12, TS], F32))
                        cp(of[:].rearrange("p j n -> p (j n)"), oT[:].rearrange("p j n -> p (j n)"), [B["oT", j] for j in range(12)], [B["of"]], eng="vector")
                        kb.dma("sync", dbg_o.rearrange("(j p) n -> p j n", p=128)[:, :, seg * TS:(seg + 1) * TS], of[:], reads=[B["of"]], writes=[B["dbg_o"]])
                        kb.barrier()

                with contextlib.ExitStack() as s3:
                    def sb3(name, shape, dt):
                        return s3.enter_context(nc.sbuf_tensor(_u(name), list(shape), dt))

                    def ps3(name, shape, dt=F32):
                        return _psum(nc, s3, name, list(shape), dt)
                    wo = sb3("wo", [128, 12, D], BF16)
                    ysb = sb3("ysb", [128, DC, 512], F32)
                    sqt = sb3("sq8", [128, DC, 512], BF16)
                    rstd = sb3("rstd", [128, 512], F32)
                    tmpt = [sb3(f"ptmp{i}", [128, 512], F32) for i in range(2)]
                    ssp = ps3("ssp", [128, 512])
                    py = [ps3(f"py{i}", [128, 512]) for i in range(2)]
                    wov = w_out[l].rearrange("(j p) n -> p j n", p=128)
                    for j in range(12):
                        kb.dma("gpsimd", wo[:, j, :], wov[:, j, :], writes=[B["wo", j]])
                    cnt = 0
                    for blk in range(NBLK):
                        cs = slice(blk * 512, (blk + 1) * 512)
                        for m in range(DC):
                            k = cnt % 2
                            cnt += 1
                            for j in range(12):
                                mm(py[k][:], wo[:, j, m * 128:(m + 1) * 128], oT[:, j, cs], j == 0, j == 11, [B["wo", j], B["oT", j]], [B["py", k]])
                            cp(ysb[:, m, :], py[k][:], [B["py", k]], [B["ysb", m]])
                        post_residual(lambda m: FP_G(l, 3, m), blk, ysb, sqt, ssp, rstd, tmpt)

        def load_x(seg, xs8):
            for it in range(NT):
                r0 = seg * TS + it * 128
                kb.dma("sync", xs8[it][:], x[r0:r0 + 128, :], writes=[B["xs", it]])

        def xpose_in(xs8, px):
            for it in range(NT):
                k = it % 2
                for c in range(DC):
                    tr(px[k][:, c // 4, (c % 4) * 128:(c % 4 + 1) * 128], xs8[it][:, c * 128:(c + 1) * 128], ident, [B["xs", it], B["cst"]], [B["px", k]])
                cp(hT[:, :, it * 128:(it + 1) * 128], px[k][:].rearrange("p a (b n) -> p (a b) n", n=128), [B["px", k]],
                   [B["h", c, it // 4] for c in range(DC)])

        def store_out(seg, xo, pxo):
            for it in range(NT):
                k = it % 2
                r0 = seg * TS + it * 128
                for c in range(DC):
                    tr(pxo[k][:, c // 4, (c % 4) * 128:(c % 4 + 1) * 128], hT[:, c, it * 128:(it + 1) * 128], ident,
                       [B["h", c, it // 4], B["cst"]], [B["pxo", k]])
                cp(xo[k][:], pxo[k][:].rearrange("p a n -> p (a n)"), [B["pxo", k]], [B["xo", k]])
                kb.dma("sync", out[r0:r0 + 128, :], xo[k][:], reads=[B["xo", k]], writes=[B["out", seg, it]])

        kb.barrier()
        with contextlib.ExitStack() as sc:
            xs8 = [sc.enter_context(nc.sbuf_tensor(_u(f"xs{i}"), [128, D], F32)) for i in range(NT)]
            px = [_psum(nc, sc, f"px{i}", [128, 2, 512], F32) for i in range(2)]
            load_x(0, xs8)
            xpose_in(xs8, px)
        for seg in range(n_seg):
            for l in range(depth):
                if "ffn" in flags:
                    ffn_phase(l, 0)
                if any(k in flags for k in ("hg", "ssd", "s5")):
                    mixer_phase(l, seg)
                if "ffn" in flags:
                    ffn_phase(l, 1)
            kb.barrier()
            with contextlib.ExitStack() as sc:
                nxt = seg + 1 < n_seg
                xo = [sc.enter_context(nc.sbuf_tensor(_u(f"xo{i}"), [128, D], F32)) for i in range(2)]
                pxo = [_psum(nc, sc, f"pxo{i}", [128, 2, 512], F32) for i in range(2)]
                if nxt:
                    xs8 = [sc.enter_context(nc.sbuf_tensor(_u(f"xs{i}"), [128, D], F32)) for i in range(NT)]
                    px = [_psum(nc, sc, f"px{i}", [128, 2, 512], F32) for i in range(2)]
                    load_x(seg + 1, xs8)
                store_out(seg, xo, pxo)
                if nxt:
                    xpose_in(xs8, px)
        kb.barrier()
        kb.replay()
    return nc


PARAM_NAMES = ["norm_g", "ffn_w_gate", "ffn_w_up", "ffn_w_down", "w_in", "w_out", "hg_lb_logits", "hg_gnorm",
               "ssd_conv_w", "ssd_conv_b", "ssd_dt_bias", "ssd_A_log", "ssd_D", "ssd_norm", "s5_A_re", "s5_A_im",
               "s5_B_re", "s5_B_im", "s5_C_re", "s5_C_im", "s5_D", "s5_log_dt", "s5_glu_w", "s5_glu_b"]


def run(inputs, n_seg, depth=DEPTH, flags=("ffn", "hg", "ssd", "s5"), dbg=False):
    x = np.ascontiguousarray(np.asarray(inputs["x"], dtype=np.float32))
    n_seq = x.shape[0]
    nc = build(n_seg, depth, flags, dbg)
    cst = make_consts()
    params = {k: np.ascontiguousarray(np.asarray(inputs[k], dtype=np.float32)) for k in PARAM_NAMES}
    in_maps = []
    for c in range(N_CORES):
        m = {"x": x[c % n_seq], "cst": cst}
        m.update(params)
        in_maps.append(m)
    res = run_bass_kernel_spmd(nc, in_maps, core_ids=list(range(N_CORES)))
    outs = np.stack([res.results[c]["out"] for c in range(n_seq)], axis=0)
    if dbg:
        return outs, np.stack([res.results[c]["dbg_o"] for c in range(n_seq)], axis=0)
    return outs


def kernel(**inputs):
    x = np.asarray(inputs["x"])
    n_seg = x.shape[1] // TS
    return run(inputs, n_seg).astype(np.float32)
```

```python
import contextlib
import math
import numpy as np
import concourse.bass as bass
import concourse.mybir as mybir
from concourse.bass_utils import run_bass_kernel_spmd

F32 = mybir.dt.float32
BF16 = mybir.dt.bfloat16
AF = mybir.ActivationFunctionType
ALU = mybir.AluOpType

P = 128
D = 1024
DC = 8
FF = 2816
FC = 22
TS = 1024
NT = TS // 128
NBLK = TS // 512
DIN = 4104
EPS = 1e-6
DEPTH = 2
N_CORES = 8
SEQ = 8192
BATCH = 2


class Buf:
    __slots__ = ("w", "r", "psum")

    def __init__(self):
        self.w = None
        self.r = []
        self.psum = False


_PSUM_NAMES = {"pp", "psm", "pc", "ptt", "prr", "ssp", "pg", "pu", "py", "pu5", "pe5", "py5", "pq", "ptk", "pss",
               "pso", "psS", "pto", "ptx", "pd", "pyo", "px", "pxo"}


class BufTable(dict):
    def __missing__(self, k):
        b = Buf()
        ks = k if isinstance(k, tuple) else (k,)
        b.psum = any(x in _PSUM_NAMES for x in ks if isinstance(x, str))
        self[k] = b
        return b


class _Eng:
    def __init__(self, name):
        self.name = name
        self.sems = []
        self.count = 0
        self.ops = []
        self.seen = {}


SEM_ROLL = 30000
_UC = [0]


def _psum(nc, stack, name, shape, dt):
    shape = list(shape)
    esz = 2 if dt == BF16 else 4
    n = 1
    for d in shape[1:]:
        n *= d
    per_bank = 2048 // esz
    nb = (n * esz + 2047) // 2048
    t = stack.enter_context(nc.psum_tensor(_u(name), [128, nb * per_bank], dt))
    v = t[0:shape[0], 0:n]
    if len(shape) == 3:
        v = v.rearrange("p (a b) -> p a b", b=shape[2])
    elif len(shape) == 4:
        v = v.rearrange("p (a b c) -> p a b c", b=shape[2], c=shape[3])
    return v


def _u(name):
    _UC[0] += 1
    return f"{name}_{_UC[0]}"


class KB:
    def __init__(self, nc, n_dma_ch=32):
        self.nc = nc
        self.eng = {n: _Eng(n) for n in ("tensor", "vector", "scalar", "gpsimd", "sync")}
        self.nsem = 0
        for e in self.eng.values():
            e.sems.append(self._newsem())
        self.dma_chs = {q: [{"sem": self._newsem(), "count": 0, "tok": None} for _ in range(n_dma_ch // 2)] for q in ("hw", "sw")}
        self.dma_rrs = {"hw": 0, "sw": 0}

    def _newsem(self):
        self.nsem += 1
        return self.nsem - 1

    def _collect(self, e, reads, writes, skip_own=False):
        best = {}
        for b in reads:
            if b.w is not None:
                s, v = b.w
                if v > best.get(s, 0):
                    best[s] = v
            if b.psum:
                for (s, v) in b.r:
                    if s not in e.sems and v > best.get(s, 0):
                        best[s] = v
        for b in writes:
            if b.w is not None:
                s, v = b.w
                if v > best.get(s, 0):
                    best[s] = v
            for (s, v) in b.r:
                if v > best.get(s, 0):
                    best[s] = v
        waits = []
        for s, v in best.items():
            if skip_own and s in e.sems:
                continue
            if e.seen.get(s, 0) >= v:
                continue
            e.seen[s] = v
            waits.append((s, v))
        return waits

    def _commit(self, tok, reads, writes):
        for b in reads:
            b.r.append(tok)
            if len(b.r) > 64:
                best = {}
                for (s, v) in b.r:
                    if v > best.get(s, 0):
                        best[s] = v
                b.r = list(best.items())
        for b in writes:
            b.w = tok
            b.r = []

    def op(self, engname, fn, reads=(), writes=()):
        e = self.eng[engname]
        waits = self._collect(e, reads, writes, skip_own=(engname == "tensor"))
        if e.count >= SEM_ROLL:
            e.sems.append(self._newsem())
            e.count = 0
        e.count += 1
        tok = (e.sems[-1], e.count)
        e.ops.append((waits, fn, (tok[0], 1)))
        self._commit(tok, reads, writes)
        return tok

    def dma(self, engname, out, in_, reads=(), writes=(), **kw):
        e = self.eng[engname]
        q = "sw" if engname == "gpsimd" else "hw"
        chs = self.dma_chs[q]
        ch = chs[self.dma_rrs[q]]
        self.dma_rrs[q] = (self.dma_rrs[q] + 1) % len(chs)
        waits = self._collect(e, reads, writes)
        if ch["tok"] is not None:
            s, v = ch["tok"]
            if e.seen.get(s, 0) < v:
                e.seen[s] = v
                waits.append((s, v))
        ch["count"] += 16
        tok = (ch["sem"], ch["count"])
        ch["tok"] = tok
        e.ops.append((waits, lambda eng: eng.dma_start(out=out, in_=in_, **kw), (tok[0], 16)))
        self._commit(tok, reads, writes)
        return tok

    def barrier(self):
        toks = []
        for e in self.eng.values():
            if e.count > 0:
                toks.append((e.sems[-1], e.count))
        for chs in self.dma_chs.values():
            for ch in chs:
                if ch["tok"] is not None:
                    toks.append(ch["tok"])
        for e in self.eng.values():
            waits = []
            for (s, v) in toks:
                if e.seen.get(s, 0) >= v:
                    continue
                e.seen[s] = v
                waits.append((s, v))
            if waits:
                e.ops.append((waits, None, None))

    def wait_all(self, engname, bufs):
        e = self.eng[engname]
        waits = self._collect(e, bufs, ())
        e.ops.append((waits, None, None))

    def replay(self):
        nc = self.nc
        with contextlib.ExitStack() as st:
            sems = [st.enter_context(nc.semaphore(f"s{i}")) for i in range(self.nsem)]
            block = st.enter_context(nc.Block())

            def mk(e):
                def body(eng):
                    for waits, fn, inc in e.ops:
                        for (s, v) in waits:
                            eng.wait_ge(sems[s], v)
                        if fn is not None:
                            fn(eng).then_inc(sems[inc[0]], inc[1])
                return body
            for name in ("sync", "gpsimd", "scalar", "vector", "tensor"):
                e = self.eng[name]
                if e.ops:
                    getattr(block, name)(mk(e))


C_ID = 0
C_ONES = 128
C_TRI = 256
C_UPS = 384
C_MHG = 512
C_MS5 = 640
C_RST = 768
C_NIDX = C_RST + TS
C_JIDX = C_NIDX + 129
C_W = C_JIDX + 24


def make_consts():
    c = np.zeros((128, C_W), np.float32)
    r = np.arange(128)
    c[:, C_ID:C_ID + 128] = np.eye(128)
    c[:, C_ONES:C_ONES + 128] = 1.0
    c[:, C_TRI:C_TRI + 128] = (r[:, None] <= r[None, :])
    c[:, C_UPS:C_UPS + 128] = (r[:, None] > r[None, :])
    c[:, C_MHG:C_MHG + 128] = (r[:, None] <= r[None, :]) & ((r[:, None] // 64) == (r[None, :] // 64))
    c[:, C_MS5:C_MS5 + 128] = ((r[None, :] // 16) >= (r[:, None] // 16))
    rst = np.ones(TS, np.float32)
    rst[::64] = 0.0
    c[:, C_RST:C_RST + TS] = rst[None, :]
    c[:, C_NIDX:C_NIDX + 129] = np.arange(129)[None, :]
    j = np.concatenate([-np.arange(1, 9), np.arange(7, -1, -1), np.arange(1, 9)]).astype(np.float32)
    c[:, C_JIDX:C_JIDX + 24] = j[None, :]
    return c


def FP_G(l, i, c):
    return (l * 6 + i) * 8 + c


FP_CW = 96


def FP_CONVW(l, k, c):
    return FP_CW + (l * 4 + k) * 8 + c


FP_CB0 = FP_CW + 64


def FP_CONVB(l, c):
    return FP_CB0 + l * 8 + c


FP_SD0 = FP_CB0 + 16


def FP_S5D(l, c):
    return FP_SD0 + l * 4 + c


FP_GB0 = FP_SD0 + 8


def FP_GLUB(l, c):
    return FP_GB0 + l * 4 + c


FP_LB0 = FP_GB0 + 8
FP_GS0 = FP_LB0 + 8


def FP_GS(l, f, c):
    return FP_GS0 + (l * 2 + f) * 8 + c


FP_LBV = FP_GS0 + 32


def FP_LB(l, hd):
    return FP_LBV + l * 4 + hd


FP_OMLV = FP_LBV + 8


def FP_OML(l, hd):
    return FP_OMLV + l * 4 + hd


FP_W = FP_OMLV + 8

RP_GN = 0
RP_DTB = 256
RP_A = RP_DTB + 16
RP_D = RP_A + 16
RP_NW = RP_D + 16
RP_W = RP_NW + 1024


def build(n_seg, depth=DEPTH, flags=("ffn", "hg", "ssd", "s5"), dbg=False):
    nc = bass.Bass("TRN2", target_bir_lowering=False)
    NTOK = n_seg * TS

    def din(name, shape, dt=F32):
        return nc.dram_tensor(name, list(shape), dt, kind="ExternalInput").ap()

    x = din("x", [NTOK, D])
    cst_d = din("cst", [128, C_W])
    norm_g = din("norm_g", [DEPTH, 6, D])
    w_gate = din("ffn_w_gate", [DEPTH, 2, D, FF])
    w_up = din("ffn_w_up", [DEPTH, 2, D, FF])
    w_down = din("ffn_w_down", [DEPTH, 2, FF, D])
    w_in = din("w_in", [DEPTH, D, DIN])
    w_out = din("w_out", [DEPTH, 1536, D])
    lb_log = din("hg_lb_logits", [DEPTH, 512])
    hg_gn = din("hg_gnorm", [DEPTH, 128])
    conv_w = din("ssd_conv_w", [DEPTH, 4, 1024])
    conv_b = din("ssd_conv_b", [DEPTH, 1024])
    dt_bias = din("ssd_dt_bias", [DEPTH, 8])
    a_log = din("ssd_A_log", [DEPTH, 8])
    ssd_d = din("ssd_D", [DEPTH, 8])
    ssd_nw = din("ssd_norm", [DEPTH, 512])
    s5_are = din("s5_A_re", [DEPTH, 32, 64])
    s5_aim = din("s5_A_im", [DEPTH, 32, 64])
    s5_bre = din("s5_B_re", [DEPTH, 32, 64, 16])
    s5_bim = din("s5_B_im", [DEPTH, 32, 64, 16])
    s5_cre = din("s5_C_re", [DEPTH, 32, 16, 64])
    s5_cim = din("s5_C_im", [DEPTH, 32, 16, 64])
    s5_dsk = din("s5_D", [DEPTH, 512])
    s5_ldt = din("s5_log_dt", [DEPTH, 32])
    glu_w = din("s5_glu_w", [DEPTH, 512, 512])
    glu_b = din("s5_glu_b", [DEPTH, 512])
    out = nc.dram_tensor("out", [NTOK, D], F32, kind="ExternalOutput").ap()
    dbg_o = nc.dram_tensor("dbg_o", [1536, NTOK], F32, kind="ExternalOutput").ap() if dbg else None

    def dscr(name, shape, dt):
        return nc.dram_tensor(name, list(shape), dt, kind="Internal").ap()

    ud = dscr("ud", [512, TS], BF16)
    yd = dscr("yd", [512, TS], F32)
    tabT = dscr("tabT", [DEPTH, 128, 32 * 128], BF16)
    tabR = dscr("tabR", [DEPTH, 2, 128, 32 * 64], BF16)
    tabO = dscr("tabO", [DEPTH, 2, 64, 32 * 128], BF16)
    tabCS = dscr("tabCS", [DEPTH, 2, 64, 32 * 129], F32)
    tabRho = dscr("tabRho", [DEPTH, 64, 32], F32)

    kb = KB(nc)
    B = BufTable()
    A = kb.op

    def mm(out_, lhsT, rhs, start, stop, reads, writes, **kw):
        A("tensor", lambda e: e.matmul(out_, lhsT=lhsT, rhs=rhs, start=start, stop=stop, **kw), reads, writes)

    def tr(out_, in_, ident, reads, writes):
        A("tensor", lambda e: e.transpose(out=out_, in_=in_, identity=ident), reads, writes)

    def act(out_, in_, func, reads, writes, **kw):
        A("scalar", lambda e: e.activation(out=out_, in_=in_, func=func, **kw), reads, writes)

    def tt(out_, in0, in1, op, reads, writes, eng="vector"):
        A(eng, lambda e: e.tensor_tensor(out=out_, in0=in0, in1=in1, op=op), reads, writes)

    def ts(out_, in0, s1, s2, op0, op1, reads, writes, eng="vector"):
        if op1 is None:
            A(eng, lambda e: e.tensor_scalar(out=out_, in0=in0, scalar1=s1, scalar2=None, op0=op0), reads, writes)
        else:
            A(eng, lambda e: e.tensor_scalar(out=out_, in0=in0, scalar1=s1, scalar2=s2, op0=op0, op1=op1), reads, writes)

    def stt(out_, in0, scalar, in1, op0, op1, reads, writes):
        A("vector", lambda e: e.scalar_tensor_tensor(out=out_, in0=in0, scalar=scalar, in1=in1, op0=op0, op1=op1), reads, writes)

    def scan(out_, d0, d1, init, reads, writes):
        A("vector", lambda e: e.tensor_tensor_scan(out=out_, data0=d0, data1=d1, initial=init, op0=ALU.mult, op1=ALU.add), reads, writes)

    def recip(t_, b_):
        A("vector", lambda e: e.reciprocal(out=t_, in_=t_), [b_], [b_])

    def cp(out_, in_, reads, writes, eng="scalar"):
        if eng == "scalar":
            A("scalar", lambda e: e.copy(out=out_, in_=in_), reads, writes)
        else:
            A(eng, lambda e: e.tensor_copy(out=out_, in_=in_), reads, writes)

    with contextlib.ExitStack() as glob:
        def gsb(name, shape, dt):
            return glob.enter_context(nc.sbuf_tensor(_u(name), list(shape), dt))

        cst = gsb("cst", [128, C_W], F32)
        identb = gsb("identb", [128, 128], BF16)
        onesb = gsb("onesb", [128, 128], BF16)
        FP = gsb("FP", [128, FP_W], F32)
        RP = gsb("RP", [128, RP_W], F32)
        hT = gsb("hT", [128, DC, TS], F32)
        Shg = gsb("Shg", [128, DEPTH, 4, 128], F32)
        Shgb = gsb("Shgb", [128, DEPTH, 4, 128], BF16)
        Hst = gsb("Hst", [128, DEPTH, 512], F32)
        Hstb = gsb("Hstb", [128, DEPTH, 512], BF16)
        halo = gsb("halo", [128, DEPTH, 8, 3], F32)
        Xc = gsb("Xc", [128, DEPTH, 2, 16], F32)
        rho8 = gsb("rho8", [64, DEPTH, 32], F32)
        rho8s = gsb("rho8s", [128, DEPTH, 16], F32)

        ident = cst[:, C_ID:C_ID + 128]
        ones = cst[:, C_ONES:C_ONES + 128]
        tri = cst[:, C_TRI:C_TRI + 128]
        ups = cst[:, C_UPS:C_UPS + 128]
        mhg = cst[:, C_MHG:C_MHG + 128]
        ms5 = cst[:, C_MS5:C_MS5 + 128]
        rstm = cst[:, C_RST:C_RST + TS]

        def fpc(col):
            return FP[:, col:col + 1]

        with contextlib.ExitStack() as sc:
            def sb(name, shape, dt):
                return sc.enter_context(nc.sbuf_tensor(_u(name), list(shape), dt))

            def ps(name, shape, dt=F32):
                return _psum(nc, sc, name, list(shape), dt)

            kb.dma("sync", cst[:], cst_d[:, :], writes=[B["cst"]])
            cp(identb[:], ident, [B["cst"]], [B["identb"]], eng="vector")
            cp(onesb[:], ones, [B["cst"]], [B["onesb"]], eng="vector")
            for t_, nm in ((Shg, "Shg"), (Hst, "Hst"), (halo, "halo"), (Xc, "Xc"), (Shgb, "Shgb"), (Hstb, "Hstb")):
                A("gpsimd", lambda e, t_=t_: e.memset(t_[:], 0.0), (), [B[nm]])
            stgA = sb("stgA", [128, 128], F32)
            stgB = sb("stgB", [128, 128], F32)
            A("vector", lambda e: e.memset(stgA[:], 0.0), (), [B["stgA"]])
            A("vector", lambda e: e.memset(stgB[:], 0.0), (), [B["stgB"]])
            kb.dma("sync", stgA[0:96, :], norm_g.rearrange("l i (c p) -> (l i c) p", p=128), writes=[B["stgA"]])
            kb.dma("sync", stgB[0:64, :], conv_w.rearrange("l k (c p) -> (l k c) p", p=128), writes=[B["stgB"]])
            kb.dma("sync", stgB[64:80, :], conv_b.rearrange("l (c p) -> (l c) p", p=128), writes=[B["stgB"]])
            kb.dma("sync", stgB[80:88, :], s5_dsk.rearrange("l (c p) -> (l c) p", p=128), writes=[B["stgB"]])
            kb.dma("sync", stgB[88:96, :], glu_b.rearrange("l (c p) -> (l c) p", p=128), writes=[B["stgB"]])
            kb.dma("sync", stgB[96:104, :], lb_log.rearrange("l (c p) -> (l c) p", p=128), writes=[B["stgB"]])
            pp = ps("pp", [128, 256])
            tr(pp[:, 0:128], stgA[:], ident, [B["stgA"], B["cst"]], [B["pp"]])
            tr(pp[:, 128:256], stgB[:], ident, [B["stgB"], B["cst"]], [B["pp"]])
            cp(FP[:, 0:96], pp[:, 0:96], [B["pp"]], [B["FP"]])
            cp(FP[:, 96:96 + 104], pp[:, 128:128 + 104], [B["pp"]], [B["FP"]])
            for l in range(DEPTH):
                for f in range(2):
                    c0 = FP_G(l, 1 + 4 * f, 0)
                    ts(FP[:, FP_GS(l, f, 0):FP_GS(l, f, 0) + 8], FP[:, c0:c0 + 8], 0.5, None, ALU.mult, None, [B["FP"]], [B["FP"]])
            A("vector", lambda e: e.memset(FP[:, FP_LBV:FP_LBV + 4], 0.0), (), [B["FP"]])
            tt(FP[:, FP_LBV + 4:FP_LBV + 8], FP[:, FP_LB0 + 4:FP_LB0 + 8], FP[:, FP_LB0:FP_LB0 + 4], ALU.subtract, [B["FP"]], [B["FP"]])
            act(FP[:, FP_LBV + 4:FP_LBV + 8], FP[:, FP_LBV + 4:FP_LBV + 8], AF.Sigmoid, [B["FP"]], [B["FP"]])
            ts(FP[:, FP_OMLV:FP_OMLV + 8], FP[:, FP_LBV:FP_LBV + 8], -1.0, 1.0, ALU.mult, ALU.add, [B["FP"]], [B["FP"]])
            for l in range(DEPTH):
                kb.dma("sync", RP[:, RP_GN + l * 128:RP_GN + (l + 1) * 128], hg_gn[l:l + 1, :].broadcast_to([128, 128]), writes=[B["RP"]])
                kb.dma("sync", RP[:, RP_DTB + l * 8:RP_DTB + (l + 1) * 8], dt_bias[l:l + 1, :].broadcast_to([128, 8]), writes=[B["RP"]])
                kb.dma("sync", RP[:, RP_A + l * 8:RP_A + (l + 1) * 8], a_log[l:l + 1, :].broadcast_to([128, 8]), writes=[B["RP"]])
                kb.dma("sync", RP[:, RP_D + l * 8:RP_D + (l + 1) * 8], ssd_d[l:l + 1, :].broadcast_to([128, 8]), writes=[B["RP"]])
                kb.dma("sync", RP[:, RP_NW + l * 512:RP_NW + (l + 1) * 512], ssd_nw[l:l + 1, :].broadcast_to([128, 512]), writes=[B["RP"]])
            act(RP[:, RP_A:RP_A + 16], RP[:, RP_A:RP_A + 16], AF.Exp, [B["RP"]], [B["RP"]])
            ts(RP[:, RP_A:RP_A + 16], RP[:, RP_A:RP_A + 16], -1.0, None, ALU.mult, None, [B["RP"]], [B["RP"]])

            if "s5" in flags:
                TWO_PI = 2.0 * math.pi
                MAGIC = 12582912.0
                for l in range(depth):
                  with contextlib.ExitStack() as sl:
                    kb.barrier()

                    def sb(name, shape, dt, sl=sl):
                        return sl.enter_context(nc.sbuf_tensor(_u(name), list(shape), dt))

                    def ps(name, shape, dt=F32, sl=sl):
                        return _psum(nc, sl, name, list(shape), dt)
                    an = sb(f"an{l}", [32, 2, 64], F32)
                    kb.dma("sync", an[:, 0, :], s5_are[l], writes=[B["an"]])
                    kb.dma("sync", an[:, 1, :], s5_aim[l], writes=[B["an"]])
                    pa = ps(f"pa{l}", [64, 64])
                    tr(pa[:, 0:32], an[:, 0, :], ident[0:32, 0:32], [B["an"], B["cst"]], [B["psm"]])
                    tr(pa[:, 32:64], an[:, 1, :], ident[0:32, 0:32], [B["an"], B["cst"]], [B["psm"]])
                    aa = sb(f"aa{l}", [64, 2, 32], F32)
                    cp(aa[:, 0, :], pa[:, 0:32], [B["psm"]], [B["aa"]])
                    cp(aa[:, 1, :], pa[:, 32:64], [B["psm"]], [B["aa"]])
                    dl = sb(f"dl{l}", [64, 32], F32)
                    kb.dma("sync", dl[:], s5_ldt[l:l + 1, :].broadcast_to([64, 32]), writes=[B["dl"]])
                    act(dl[:], dl[:], AF.Exp, [B["dl"]], [B["dl"]])
                    ard = sb(f"ard{l}", [64, 32], F32)
                    t1 = sb(f"t1{l}", [64, 32], F32)
                    tt(ard[:], aa[:, 0, :], dl[:], ALU.mult, [B["aa"], B["dl"]], [B["ard"]])
                    tt(t1[:], aa[:, 1, :], dl[:], ALU.mult, [B["aa"], B["dl"]], [B["t1"]])
                    ts(t1[:], t1[:], 1.0 / TWO_PI, None, ALU.mult, None, [B["t1"]], [B["t1"]])

                    def frac_(dst, src, shift, bs):
                        tmpn = "fr_tmp"
                        shp = list(src.shape)
                        tmp = sb(f"frt{l}_{kb.eng['vector'].count}", shp, F32)
                        ts(tmp[:], src, shift, MAGIC, ALU.add, ALU.add, bs, [B[tmpn]])
                        ts(tmp[:], tmp[:], -MAGIC, None, ALU.add, None, [B[tmpn]], [B[tmpn]])
                        stt(dst, src, shift, tmp[:], ALU.add, ALU.subtract, bs + [B[tmpn]], [B["fr_dst"]])

                    jv = cst[0:64, C_JIDX:C_JIDX + 24]
                    mag = sb(f"mag{l}", [64, 32, 24], F32)
                    ang = sb(f"ang{l}", [64, 32, 24], F32)
                    tt(mag[:], ard[:].unsqueeze(2).broadcast_to([64, 32, 24]), jv.unsqueeze(1).broadcast_to([64, 32, 24]), ALU.mult, [B["ard"], B["cst"]], [B["mag"]])
                    act(mag[:], mag[:], AF.Exp, [B["mag"]], [B["mag"]])
                    t1r = sb(f"t1r{l}", [64, 32], F32)
                    frac_(t1r[:], t1[:], 0.0, [B["t1"]])
                    tt(ang[:], t1r[:].unsqueeze(2).broadcast_to([64, 32, 24]), jv.unsqueeze(1).broadcast_to([64, 32, 24]), ALU.mult, [B["fr_dst"], B["cst"]], [B["ang"]])
                    sj = sb(f"sj{l}", [64, 32, 24], F32)
                    cj = sb(f"cj{l}", [64, 32, 24], F32)
                    frac_(sj[:], ang[:], 0.0, [B["ang"]])
                    act(sj[:], sj[:], AF.Sin, [B["fr_dst"]], [B["sj"]], scale=TWO_PI)
                    frac_(cj[:], ang[:], 0.25, [B["ang"]])
                    act(cj[:], cj[:], AF.Sin, [B["fr_dst"]], [B["cj"]], scale=TWO_PI)
                    Lr = sb(f"Lr{l}", [64, 32, 24], F32)
                    Li = sb(f"Li{l}", [64, 32, 24], F32)
                    tt(Lr[:], mag[:], cj[:], ALU.mult, [B["mag"], B["cj"]], [B["Lr"]])
                    tt(Li[:], mag[:], sj[:], ALU.mult, [B["mag"], B["sj"]], [B["Li"]])
                    cp(rho8[:, l, :], mag[:, :, 23], [B["mag"]], [B["rho8"]], eng="vector")
                    kb.dma("sync", tabRho[l], rho8[:, l, :], reads=[B["rho8"]], writes=[B["tabRho"]])
                    fr = sb(f"fr{l}", [64, 32], F32)
                    fi = sb(f"fi{l}", [64, 32], F32)
                    nr = sb(f"nr{l}", [64, 32], F32)
                    den = sb(f"den{l}", [64, 32], F32)
                    tq = sb(f"tq{l}", [64, 32], F32)
                    ar_, ai_ = aa[:, 0, :], aa[:, 1, :]
                    lr1, li1 = Lr[:, :, 16], Li[:, :, 16]
                    ts(nr[:], lr1, -1.0, None, ALU.add, None, [B["Lr"]], [B["nr"]])
                    tt(den[:], ar_, ar_, ALU.mult, [B["aa"]], [B["den"]])
                    tt(tq[:], ai_, ai_, ALU.mult, [B["aa"]], [B["tq"]])
                    tt(den[:], den[:], tq[:], ALU.add, [B["den"], B["tq"]], [B["den"]])
                    recip(den[:], B["den"])
                    tt(fr[:], nr[:], ar_, ALU.mult, [B["nr"], B["aa"]], [B["fr"]])
                    tt(tq[:], li1, ai_, ALU.mult, [B["Li"], B["aa"]], [B["tq"]])
                    tt(fr[:], fr[:], tq[:], ALU.add, [B["fr"], B["tq"]], [B["fr"]])
                    tt(fr[:], fr[:], den[:], ALU.mult, [B["fr"], B["den"]], [B["fr"]])
                    tt(fi[:], li1, ar_, ALU.mult, [B["Li"], B["aa"]], [B["fi"]])
                    tt(tq[:], nr[:], ai_, ALU.mult, [B["nr"], B["aa"]], [B["tq"]])
                    tt(fi[:], fi[:], tq[:], ALU.subtract, [B["fi"], B["tq"]], [B["fi"]])
                    tt(fi[:], fi[:], den[:], ALU.mult, [B["fi"], B["den"]], [B["fi"]])
                    Bn = sb(f"Bn{l}", [64, 2, 32, 16], F32)
                    kb.dma("sync", Bn[:, 0], s5_bre[l].rearrange("g p c -> p g c"), writes=[B["Bn"]])
                    kb.dma("sync", Bn[:, 1], s5_bim[l].rearrange("g p c -> p g c"), writes=[B["Bn"]])
                    Cn = sb(f"Cn{l}", [64, 2, 32, 16], F32)
                    cnat = sb(f"cnat{l}", [128, 2, 4, 64], F32)
                    kb.dma("sync", cnat[:, 0], s5_cre[l].rearrange("(a g) c p -> (g c) a p", a=4), writes=[B["cnat"]])
                    kb.dma("sync", cnat[:, 1], s5_cim[l].rearrange("(a g) c p -> (g c) a p", a=4), writes=[B["cnat"]])
                    pc_ = ps(f"pc{l}", [64, 2, 512])
                    for ri in range(2):
                        for a4 in range(4):
                            tr(pc_[:, ri, a4 * 128:(a4 + 1) * 128], cnat[:, ri, a4, :], ident, [B["cnat"], B["cst"]], [B["pc"]])
                    cp(Cn[:].rearrange("p r g c -> p (r g c)"), pc_[:].rearrange("p r n -> p (r n)"), [B["pc"]], [B["Cn"]])
                    Bb = sb(f"Bb{l}", [64, 2, 32, 16], F32)
                    tw = sb(f"tw{l}", [64, 32, 16], F32)
                    frb = fr[:].unsqueeze(2).broadcast_to([64, 32, 16])
                    fib = fi[:].unsqueeze(2).broadcast_to([64, 32, 16])
                    tt(Bb[:, 0], frb, Bn[:, 0], ALU.mult, [B["fr"], B["Bn"]], [B["Bb"]])
                    tt(tw[:], fib, Bn[:, 1], ALU.mult, [B["fi"], B["Bn"]], [B["tw"]])
                    tt(Bb[:, 0], Bb[:, 0], tw[:], ALU.subtract, [B["Bb"], B["tw"]], [B["Bb"]])
                    tt(Bb[:, 1], frb, Bn[:, 1], ALU.mult, [B["fr"], B["Bn"]], [B["Bb"]])
                    tt(tw[:], fib, Bn[:, 0], ALU.mult, [B["fi"], B["Bn"]], [B["tw"]])
                    tt(Bb[:, 1], Bb[:, 1], tw[:], ALU.add, [B["Bb"], B["tw"]], [B["Bb"]])

                    NG = 8
                    for g0 in range(0, 32, NG):
                        with contextlib.ExitStack() as s2:
                            def sb2(name, shape, dt):
                                return s2.enter_context(nc.sbuf_tensor(_u(name), list(shape), dt))
                            kb.barrier()
                            PBt = sb2(f"PBt{l}_{g0}", [64, 2, NG, 128], F32)
                            PBr = sb2(f"PBr{l}_{g0}", [64, 2, NG, 128], F32)
                            PC = sb2(f"PC{l}_{g0}", [64, 2, NG, 128], F32)
                            Tst = sb2(f"Tst{l}_{g0}", [128, NG, 128], BF16)
                            Rst = sb2(f"Rst{l}_{g0}", [128, 2, NG, 64], BF16)
                            Ost = sb2(f"Ost{l}_{g0}", [64, 2, NG, 128], BF16)
                            v4 = lambda t_, r: t_[:, r].rearrange("p g (s c) -> p g s c", c=16)
                            def cprod2(dst_re, dst_im, j0, M, neg_im, bname):
                                lr = Lr[:, g0:g0 + NG, j0:j0 + 8].unsqueeze(3).broadcast_to([64, NG, 8, 16])
                                li = Li[:, g0:g0 + NG, j0:j0 + 8].unsqueeze(3).broadcast_to([64, NG, 8, 16])
                                mr = M[:, 0, g0:g0 + NG, :].unsqueeze(2).broadcast_to([64, NG, 8, 16])
                                mi = M[:, 1, g0:g0 + NG, :].unsqueeze(2).broadcast_to([64, NG, 8, 16])
                                t2 = sb2(f"cpt{l}_{bname}_{g0}", [64, NG, 8, 16], F32)
                                rds = [B["Lr"], B["Li"], B["Bb"], B["Cn"]]
                                tt(dst_re, lr, mr, ALU.mult, rds, [B[bname]])
                                tt(t2[:], li, mi, ALU.mult, rds, [B[bname + "t"]])
                                tt(dst_re, dst_re, t2[:], ALU.subtract, [B[bname], B[bname + "t"]], [B[bname]])
                                tt(dst_im, lr, mi, ALU.mult, rds, [B[bname + "i"]])
                                tt(t2[:], li, mr, ALU.mult, rds, [B[bname + "t"]])
                                if neg_im:
                                    stt(dst_im, dst_im, -1.0, t2[:], ALU.mult, ALU.subtract, [B[bname + "i"], B[bname + "t"]], [B[bname + "i"]])
                                else:
                                    tt(dst_im, dst_im, t2[:], ALU.add, [B[bname + "i"], B[bname + "t"]], [B[bname + "i"]])
                            cprod2(v4(PBt, 0), v4(PBt, 1), 0, Bb, False, "PBt")
                            cprod2(v4(PBr, 0), v4(PBr, 1), 8, Bb, False, "PBr")
                            cprod2(v4(PC, 0), v4(PC, 1), 16, Cn, True, "PC")
                            ptt = _psum(nc, s2, f"ptt{l}_{g0}", [128, 2, 512], F32)
                            prr = _psum(nc, s2, f"prr{l}_{g0}", [128, 2, 512], F32)
                            for gg in range(NG):
                                bk, of = gg // 4, (gg % 4) * 128
                                mm(ptt[:, bk, of:of + 128], PBt[:, 0, gg, :], PC[:, 0, gg, :], True, False, [B["PBt"], B["PC"]], [B["ptt", bk]])
                                mm(ptt[:, bk, of:of + 128], PBt[:, 1, gg, :], PC[:, 1, gg, :], False, True, [B["PBti"], B["PCi"]], [B["ptt", bk]])
                            for bk in range(2):
                                tt(Tst[:, bk * 4:(bk + 1) * 4, :], ptt[:, bk, :].rearrange("p (g n) -> p g n", n=128),
                                   ms5.unsqueeze(1).broadcast_to([128, 4, 128]), ALU.mult, [B["ptt", bk], B["cst"]], [B["Tst"]])
                            for gg in range(NG):
                                for ri in range(2):
                                    mm(prr[:, ri, gg * 64:(gg + 1) * 64], PBr[:, ri, gg, :], ident[0:64, 0:64], True, True,
                                       [B["PBr"], B["PBri"], B["cst"]], [B["prr", ri]])
                            for ri in range(2):
                                cp(Rst[:, ri].rearrange("p g n -> p (g n)"), prr[:, ri, :], [B["prr", ri]], [B["Rst"]])
                            cp(Ost[:].rearrange("p r g n -> p (r g n)"), PC[:].rearrange("p r g n -> p (r g n)"), [B["PC"], B["PCi"]], [B["Ost"]], eng="vector")
                            kb.dma("sync", tabT[l, :, g0 * 128:(g0 + NG) * 128], Tst[:].rearrange("p g n -> p (g n)"), reads=[B["Tst"]], writes=[B["tabT"]])
                            for ri in range(2):
                                kb.dma("sync", tabR[l, ri, :, g0 * 64:(g0 + NG) * 64], Rst[:, ri].rearrange("p g n -> p (g n)"), reads=[B["Rst"]], writes=[B["tabR"]])
                                kb.dma("sync", tabO[l, ri, :, g0 * 128:(g0 + NG) * 128], Ost[:, ri].rearrange("p g n -> p (g n)"), reads=[B["Ost"]], writes=[B["tabO"]])
                            t8 = sb2(f"t8{l}_{g0}", [64, NG], F32)
                            t8r = sb2(f"t8r{l}_{g0}", [64, NG], F32)
                            ts(t8[:], t1r[:, g0:g0 + NG], 8.0, None, ALU.mult, None, [B["fr_dst"]], [B["t8"]])
                            tmp8 = sb2(f"tmp8{l}_{g0}", [64, NG], F32)
                            ts(tmp8[:], t8[:], MAGIC, None, ALU.add, None, [B["t8"]], [B["tmp8"]])
                            ts(tmp8[:], tmp8[:], -MAGIC, None, ALU.add, None, [B["tmp8"]], [B["tmp8"]])
                            tt(t8r[:], t8[:], tmp8[:], ALU.subtract, [B["t8"], B["tmp8"]], [B["t8r"]])
                            an2 = sb2(f"an2{l}_{g0}", [64, NG, 129], F32)
                            nv = cst[0:64, C_NIDX:C_NIDX + 129]
                            tt(an2[:], t8r[:].unsqueeze(2).broadcast_to([64, NG, 129]), nv.unsqueeze(1).broadcast_to([64, NG, 129]), ALU.mult, [B["t8r"], B["cst"]], [B["an2"]])
                            cs2 = sb2(f"cs2{l}_{g0}", [64, 2, NG, 129], F32)
                            tm2 = sb2(f"tm2{l}_{g0}", [64, NG, 129], F32)
                            for ri, shift in ((0, 0.25), (1, 0.0)):
                                ts(tm2[:], an2[:], shift, MAGIC, ALU.add, ALU.add, [B["an2"]], [B["tm2"]])
                                ts(tm2[:], tm2[:], -MAGIC, None, ALU.add, None, [B["tm2"]], [B["tm2"]])
                                stt(cs2[:, ri], an2[:], shift, tm2[:], ALU.add, ALU.subtract, [B["an2"], B["tm2"]], [B["cs2"]])
                                act(cs2[:, ri], cs2[:, ri], AF.Sin, [B["cs2"]], [B["cs2"]], scale=TWO_PI)
                                kb.dma("sync", tabCS[l, ri, :, g0 * 129:(g0 + NG) * 129], cs2[:, ri].rearrange("p g n -> p (g n)"), reads=[B["cs2"]], writes=[B["tabCS"]])
                    kb.barrier()
        kb.barrier()
        if "s5" in flags:
            for l in range(depth):
                for dq in range(2):
                    for h in range(2):
                        kb.dma("sync", rho8s[h * 64:(h + 1) * 64, l, dq * 8:(dq + 1) * 8], tabRho[l, :, dq * 16 + h * 8:dq * 16 + h * 8 + 8],
                               reads=[B["tabRho"]], writes=[B["rho8s"]])

        def rms_stats(src_ap, src_bufs, sq8, ssp, rstd, tag):
            act(sq8[:], src_ap, AF.Square, list(src_bufs), [B[tag, "sq8"]])
            for c in range(DC):
                mm(ssp[:], onesb[:], sq8[:, c, :], c == 0, c == DC - 1, [B["onesb"], B[tag, "sq8"]], [B[tag, "ssp"]])
            act(rstd[:], ssp[:], AF.Ln, [B[tag, "ssp"]], [B[tag, "rstd"]], scale=1.0 / D, bias=EPS)
            act(rstd[:], rstd[:], AF.Exp, [B[tag, "rstd"]], [B[tag, "rstd"]], scale=-0.5)

        def prenorm(l, i, xn, sq8, ssp, rstd):
            for blk in range(NBLK):
                cs = slice(blk * 512, (blk + 1) * 512)
                rms_stats(hT[:, :, cs], [B["h", c, blk] for c in range(DC)], sq8, ssp, rstd, "pre")
                for c in range(DC):
                    stt(xn[:, c, cs], hT[:, c, cs], fpc(FP_G(l, i, c)), rstd[:], ALU.mult, ALU.mult,
                        [B["h", c, blk], B["FP"], B["pre", "rstd"]], [B["xn", c, blk]])

        def post_residual(gcol_fn, blk, ysb, sq8, ssp, rstd, tmpt):
            cs = slice(blk * 512, (blk + 1) * 512)
            rms_stats(ysb[:], [B["ysb", c] for c in range(DC)], sq8, ssp, rstd, "post")
            for m in range(DC):
                t = tmpt[m % 2]
                stt(t[:], ysb[:, m, :], fpc(gcol_fn(m)), rstd[:], ALU.mult, ALU.mult, [B["ysb", m], B["FP"], B["post", "rstd"]], [B["ptmp", m % 2], B["sg", m % 2]])
                tt(hT[:, m, cs], hT[:, m, cs], t[:], ALU.add, [B["h", m, blk], B["ptmp", m % 2]], [B["h", m, blk]])

        def ffn_phase(l, f):
            kb.barrier()
            with contextlib.ExitStack() as sc:
                def sb(name, shape, dt):
                    return sc.enter_context(nc.sbuf_tensor(_u(name), list(shape), dt))

                def ps(name, shape, dt=F32):
                    return _psum(nc, sc, name, list(shape), dt)
                xn = sb("xn", [128, DC, TS], BF16)
                actb = sb("actb", [128, FC, TS], BF16)
                wd = sb("wd", [128, FC, D], BF16)
                NW = 2
                wg = [sb(f"wg{i}", [128, DC, 256], BF16) for i in range(NW)]
                wu = [sb(f"wu{i}", [128, DC, 256], BF16) for i in range(NW)]
                ysb = sb("ysb", [128, DC, 512], BF16)
                sqt = sb("sq8", [128, DC, 512], BF16)
                rstd = sb("rstd", [128, 512], F32)
                sgt = [sb(f"sg{i}", [128, 512], F32) for i in range(2)]
                tmpt = sgt
                ssp = ps("ssp", [128, 512])
                pg = [ps(f"pg{i}", [128, 512]) for i in range(2)]
                pu = [ps(f"pu{i}", [128, 512]) for i in range(2)]
                py = [ps(f"py{i}", [128, 512]) for i in range(2)]

                prenorm(l, 4 * f, xn, sqt, ssp, rstd)
                wgv = w_gate[l, f].rearrange("(c p) n -> p c n", p=128)
                wuv = w_up[l, f].rearrange("(c p) n -> p c n", p=128)
                wdv = w_down[l, f].rearrange("(j p) n -> p j n", p=128)
                cnt = 0
                wd_next = 0
                for jg in range(FC // 2):
                    s = jg % NW
                    if jg == 0:
                        for cq in range(4):
                            kb.dma("gpsimd", wg[s][:, 2 * cq:2 * cq + 2, :], wgv[:, 2 * cq:2 * cq + 2, 0:256], writes=[B["wg", s, cq]])
                        for cq in range(4):
                            kb.dma("gpsimd", wu[s][:, 2 * cq:2 * cq + 2, :], wuv[:, 2 * cq:2 * cq + 2, 0:256], writes=[B["wu", s, cq]])
                    else:
                        kb.dma("gpsimd", wg[s][:], wgv[:, :, jg * 256:(jg + 1) * 256], writes=[B["wg", s, cq] for cq in range(4)])
                        kb.dma("gpsimd", wu[s][:], wuv[:, :, jg * 256:(jg + 1) * 256], writes=[B["wu", s, cq] for cq in range(4)])
                    if jg >= 1:
                        for _ in range(3):
                            if wd_next < FC:
                                kb.dma("gpsimd", wd[:, wd_next, :], wdv[:, wd_next, :], writes=[B["wd", wd_next]])
                                wd_next += 1
                    for blk in range(NBLK):
                        cs = slice(blk * 512, (blk + 1) * 512)
                        for jj in range(2):
                            j = jg * 2 + jj
                            k = cnt % 2
                            cnt += 1
                            for c in range(DC):
                                mm(pg[k][:], wg[s][:, c, jj * 128:(jj + 1) * 128], xn[:, c, cs], c == 0, c == DC - 1,
                                   [B["wg", s, c // 2], B["xn", c, blk]], [B["pg", k]])
                            for c in range(DC):
                                mm(pu[k][:], wu[s][:, c, jj * 128:(jj + 1) * 128], xn[:, c, cs], c == 0, c == DC - 1,
                                   [B["wu", s, c // 2], B["xn", c, blk]], [B["pu", k]])
                            act(sgt[k][:], pg[k][:], AF.Silu, [B["pg", k]], [B["sg", k]])
                            tt(actb[:, j, cs], sgt[k][:], pu[k][:], ALU.mult, [B["sg", k], B["pu", k]], [B["actb", j, blk]])
                while wd_next < FC:
                    kb.dma("gpsimd", wd[:, wd_next, :], wdv[:, wd_next, :], writes=[B["wd", wd_next]])
                    wd_next += 1
                cnt = 0
                for blk in range(NBLK):
                    cs = slice(blk * 512, (blk + 1) * 512)
                    for m in range(DC):
                        k = cnt % 2
                        cnt += 1
                        for j in range(FC):
                            mm(py[k][:], wd[:, j, m * 128:(m + 1) * 128], actb[:, j, cs], j == 0, j == FC - 1,
                               [B["wd", j], B["actb", j, blk]], [B["py", k]])
                        cp(ysb[:, m, :], py[k][:], [B["py", k]], [B["ysb", m]])
                    post_residual(lambda m: FP_GS(l, f, m), blk, ysb, sqt, ssp, rstd, tmpt)

        def mixer_phase(l, seg):
            kb.barrier()
            with contextlib.ExitStack() as sc:
                def sb(name, shape, dt):
                    return sc.enter_context(nc.sbuf_tensor(_u(name), list(shape), dt))

                def ps(name, shape, dt=F32):
                    return _psum(nc, sc, name, list(shape), dt)
                xn = sb("xn", [128, DC, TS], BF16)
                oT = sb("oT", [128, 12, TS], BF16)
                NW = 2
                wsl = [sb(f"wsl{i}", [128, DC, 512], BF16) for i in range(NW)]
                wstate = {"n": 0}
                winv = w_in[l].rearrange("(c p) n -> p c n", p=128)

                def wload(col0, ncols=512):
                    s = wstate["n"] % NW
                    wstate["n"] += 1
                    for cq in range(4):
                        kb.dma("gpsimd", wsl[s][:, 2 * cq:2 * cq + 2, 0:ncols], winv[:, 2 * cq:2 * cq + 2, col0:col0 + ncols], writes=[B["wsl", s, cq]])
                    return s

                pre = {}
                if "s5" in flags:
                    pre["u"] = wload(3592)
                with contextlib.ExitStack() as s0:
                    sqt = s0.enter_context(nc.sbuf_tensor(_u("sq8"), [128, DC, 512], BF16))
                    rstd = s0.enter_context(nc.sbuf_tensor(_u("rstd"), [128, 512], F32))
                    ssp = _psum(nc, s0, "ssp", [128, 512], F32)
                    prenorm(l, 2, xn, sqt, ssp, rstd)
                    kb.barrier()

                if not all(k in flags for k in ("hg", "ssd", "s5")):
                    A("vector", lambda e: e.memset(oT[:], 0.0), (), [B["oT", j] for j in range(12)])

                if "s5" in flags:
                    with contextlib.ExitStack() as s5:
                        def sb5(name, shape, dt):
                            return s5.enter_context(nc.sbuf_tensor(_u(name), list(shape), dt))

                        def ps5(name, shape, dt=F32):
                            return _psum(nc, s5, name, list(shape), dt)
                        uS = sb5("uS", [128, 4, 8, 128], F32)
                        U = sb5("U", [128, 32, 128], BF16)
                        gw = sb5("gw", [128, 4, 512], BF16)
                        pu_ = [ps5(f"pu5{i}", [128, 512]) for i in range(2)]
                        kb.dma("gpsimd", gw[:], glu_w[l].rearrange("(c p) n -> p c n", p=128), writes=[B["gw"]])
                        s = pre["u"]
                        k = 0
                        for ch in range(4):
                            for blk in range(NBLK):
                                pst = pu_[k % 2]
                                pb = B["pu5", k % 2]
                                cs = slice(blk * 512, (blk + 1) * 512)
                                for c in range(DC):
                                    mm(pst[:], wsl[s][:, c, ch * 128:(ch + 1) * 128], xn[:, c, cs], c == 0, c == DC - 1,
                                       [B["wsl", s, c // 2], B["xn", c, blk]], [pb])
                                cp(uS[:, ch, :, blk * 64:(blk + 1) * 64], pst[:].rearrange("p (n s) -> p s n", s=8), [pb], [B["uS", ch]])
                                k += 1
                        if "hg" in flags:
                            pre["f"] = wload(512)
                            pre["q"] = wload(0)
                        for ch in range(4):
                            kb.dma("gpsimd", ud[ch * 128:(ch + 1) * 128, :], uS[:, ch].rearrange("p s n -> p (s n)"), reads=[B["uS", ch]], writes=[B["ud"]])
                        udv = ud.rearrange("(g c) (s n) -> s c g n", c=16, s=8)
                        for s_ in range(8):
                            kb.dma("sync", U[s_ * 16:(s_ + 1) * 16, :, :], udv[s_], reads=[B["ud"]], writes=[B["U"]])
                        ydv = yd.rearrange("(g c) (t n) -> t c g n", c=16, t=8)
                        with contextlib.ExitStack() as s6:
                            def sb6(name, shape, dt):
                                return s6.enter_context(nc.sbuf_tensor(_u(name), list(shape), dt))
                            kb.barrier()
                            cs2 = sb6("cs2m", [128, 2, 8, 129], F32)
                            Tm = sb6("Tm", [128, 16, 128], BF16)
                            Rm = sb6("Rm", [128, 2, 16, 64], BF16)
                            Om = sb6("Om", [128, 2, 8, 128], BF16)
                            Ysb = sb6("Ysb", [128, 16, 128], F32)
                            Eq = sb6("Eq", [128, 2, 8, 128], F32)
                            Zq = sb6("Zq", [128, 2, 8, 129], F32)
                            Xq = Zq
                            Vq = sb6("Vq", [128, 2, 8, 129], F32)
                            Xb = sb6("Xb", [128, 2, 8, 128], BF16)
                            tq_ = sb6("tq5", [128, 8, 129], F32)
                            pe_ = [_psum(nc, s6, f"pe5{i}", [128, 2, 512], F32) for i in range(2)]
                            py_ = [_psum(nc, s6, f"py5{i}", [128, 512], F32) for i in range(2)]

                            def load_R(dq_):
                                g0_ = dq_ * 16
                                for ri in range(2):
                                    kb.dma("sync", Rm[:, ri].rearrange("p g n -> p (g n)"), tabR[l, ri, :, g0_ * 64:(g0_ + 16) * 64], reads=[B["tabR"]], writes=[B["Rm"]])

                            def load_TO(dq_):
                                g0_ = dq_ * 16
                                kb.dma("sync", Tm[:].rearrange("p g n -> p (g n)"), tabT[l, :, g0_ * 128:(g0_ + 16) * 128], reads=[B["tabT"]], writes=[B["Tm"]])
                                for ri in range(2):
                                    for h in range(2):
                                        gh = g0_ + h * 8
                                        kb.dma("sync", Om[h * 64:(h + 1) * 64, ri].rearrange("p g n -> p (g n)"), tabO[l, ri, :, gh * 128:(gh + 8) * 128],
                                               reads=[B["tabO"]], writes=[B["Om"]])

                            def load_cs(dq_):
                                for ri in range(2):
                                    for h in range(2):
                                        gh = dq_ * 16 + h * 8
                                        kb.dma("sync", cs2[h * 64:(h + 1) * 64, ri].rearrange("p g n -> p (g n)"), tabCS[l, ri, :, gh * 129:(gh + 8) * 129],
                                               reads=[B["tabCS"]], writes=[B["cs2m"]])
                            load_R(0)
                            load_cs(0)
                            load_TO(0)
                            for dq in range(2):
                                G0 = dq * 16
                                XS = slice(dq * 8, (dq + 1) * 8)
                                if dq > 0:
                                    load_TO(dq)
                                for hb in range(2):
                                    for h in range(2):
                                        hp = slice(h * 64, (h + 1) * 64)
                                        for gg in range(4):
                                            idx = hb * 4 + gg
                                            g = G0 + h * 8 + idx
                                            for ri in range(2):
                                                mm(pe_[hb][hp, ri, gg * 128:(gg + 1) * 128], Rm[:, ri, h * 8 + idx, :], U[:, g, :], True, True,
                                                   [B["Rm"], B["U"]], [B["pe5", hb]])
                                    cp(Eq[:, :, hb * 4:(hb + 1) * 4, :], pe_[hb][:].rearrange("p r (g n) -> p r g n", n=128), [B["pe5", hb]], [B["Eq"]])
                                if dq + 1 < 2:
                                    load_R(dq + 1)
                                cc = cs2[:, 0, :, 1:129]
                                ss_ = cs2[:, 1, :, 1:129]
                                Er, Ei = Eq[:, 0], Eq[:, 1]
                                t128 = tq_[:, :, 0:128]
                                tt(Zq[:, 0, :, 0:128], cc, Er, ALU.mult, [B["cs2m"], B["Eq"]], [B["Zq0"]])
                                tt(t128, ss_, Ei, ALU.mult, [B["cs2m"], B["Eq"]], [B["tq5"]])
                                tt(Zq[:, 0, :, 0:128], Zq[:, 0, :, 0:128], t128, ALU.add, [B["Zq0"], B["tq5"]], [B["Zq0"]])
                                tt(Zq[:, 1, :, 0:128], cc, Ei, ALU.mult, [B["cs2m"], B["Eq"]], [B["Zq1"]])
                                tt(t128, ss_, Er, ALU.mult, [B["cs2m"], B["Eq"]], [B["tq5"]])
                                tt(Zq[:, 1, :, 0:128], Zq[:, 1, :, 0:128], t128, ALU.subtract, [B["Zq1"], B["tq5"]], [B["Zq1"]])
                                for ri in range(2):
                                    cp(Vq[:, ri, :, 0], Xc[:, l, ri, XS], [B["Xc"]], [B["Vq", ri]], eng="vector")
                                    for idx in range(8):
                                        c_ = dq * 8 + idx
                                        scan(Vq[:, ri, idx, 1:129], rho8s[:, l, c_:c_ + 1].broadcast_to([128, 128]),
                                             Zq[:, ri, idx, 0:128], Xc[:, l, ri, c_:c_ + 1],
                                             [B["rho8s"], B["Zq0"], B["Zq1"], B["Xc"]], [B["Vq", ri]])
                                ca = cs2[:, 0]
                                sa = cs2[:, 1]
                                tt(Xq[:, 0], ca, Vq[:, 0], ALU.mult, [B["cs2m"], B["Vq", 0]], [B["Xq0"], B["Zq0"], B["Zq1"]])
                                tt(tq_[:], sa, Vq[:, 1], ALU.mult, [B["cs2m"], B["Vq", 1]], [B["tq5"]])
                                tt(Xq[:, 0], Xq[:, 0], tq_[:], ALU.subtract, [B["Xq0"], B["tq5"]], [B["Xq0"]])
                                tt(Xq[:, 1], ca, Vq[:, 1], ALU.mult, [B["cs2m"], B["Vq", 1]], [B["Xq1"], B["Zq0"], B["Zq1"]])
                                tt(tq_[:], sa, Vq[:, 0], ALU.mult, [B["cs2m"], B["Vq", 0]], [B["tq5"]])
                                tt(Xq[:, 1], Xq[:, 1], tq_[:], ALU.add, [B["Xq1"], B["tq5"]], [B["Xq1"]])
                                for ri in range(2):
                                    cp(Xb[:, ri], Xq[:, ri, :, 0:128], [B["Xq0"], B["Xq1"]], [B["Xb"]])
                                    cp(Xc[:, l, ri, XS], Xq[:, ri, :, 128], [B["Xq0"], B["Xq1"], B["Vq", 0], B["Vq", 1]], [B["Xc"]], eng="vector")
                                if dq + 1 < 2:
                                    load_cs(dq + 1)
                                for quad in range(4):
                                    pyq = py_[quad % 2]
                                    bq = B["py5", quad % 2]
                                    for gg in range(4):
                                        gl16 = quad * 4 + gg
                                        h, idx = gl16 // 8, gl16 % 8
                                        hp = slice(h * 64, (h + 1) * 64)
                                        g = G0 + gl16
                                        o_ = pyq[:, gg * 128:(gg + 1) * 128]
                                        mm(o_, Tm[:, gl16, :], U[:, g, :], True, False, [B["Tm"], B["U"]], [bq])
                                        mm(o_, Om[hp, 0, idx, :], Xb[hp, 0, idx, :], False, False, [B["Om"], B["Xb"]], [bq])
                                        mm(o_, Om[hp, 1, idx, :], Xb[hp, 1, idx, :], False, True, [B["Om"], B["Xb"]], [bq])
                                    cp(Ysb[:, quad * 4:quad * 4 + 4, :], pyq[:].rearrange("p (g n) -> p g n", n=128), [bq], [B["Ysb"]])
                                for t_ in range(8):
                                    kb.dma("sync", ydv[t_][:, G0:G0 + 16, :], Ysb[t_ * 16:(t_ + 1) * 16, :, :], reads=[B["Ysb"]], writes=[B["yd"]])
                        kb.barrier()
                        with contextlib.ExitStack() as s7:
                            def sb7(name, shape, dt):
                                return s7.enter_context(nc.sbuf_tensor(_u(name), list(shape), dt))
                            yS = sb7("yS", [128, 4, TS], F32)
                            y2 = sb7("y2", [128, 4, TS], F32)
                            y2b = sb7("y2b", [128, 4, TS], BF16)
                            t5 = [sb7(f"t5{i}", [128, TS], F32) for i in range(2)]
                            sgl = [sb7(f"sgl{i}", [128, 512], F32) for i in range(2)]
                            for ch in range(4):
                                kb.dma("sync", yS[:, ch, :], yd[ch * 128:(ch + 1) * 128, :], reads=[B["yd"]], writes=[B["yS", ch]])
                            KG = 2.0 * math.sqrt(2.0 / math.pi)
                            for ch in range(4):
                                t = t5[ch % 2]
                                tb = B["t5", ch % 2]
                                y1 = yS[:, ch, :]
                                stt(y1, uS[:, ch].rearrange("p s n -> p (s n)"), fpc(FP_S5D(l, ch)), y1, ALU.mult, ALU.add,
                                    [B["uS", ch], B["FP"], B["yS", ch]], [B["yS", ch]])
                                tt(t[:], y1, y1, ALU.mult, [B["yS", ch]], [tb])
                                ts(t[:], t[:], 0.044715, 1.0, ALU.mult, ALU.add, [tb], [tb])
                                tt(t[:], t[:], y1, ALU.mult, [tb, B["yS", ch]], [tb])
                                act(t[:], t[:], AF.Sigmoid, [tb], [tb], scale=KG)
                                tt(y2[:, ch, :], y1, t[:], ALU.mult, [B["yS", ch], tb], [B["y2", ch]])
                                cp(y2b[:, ch, :], y2[:, ch, :], [B["y2", ch]], [B["y2b", ch]])
                            k = 0
                            for oc in range(4):
                                for blk in range(NBLK):
                                    cs = slice(blk * 512, (blk + 1) * 512)
                                    pst = pu_[k % 2]
                                    for ic in range(4):
                                        mm(pst[:], gw[:, ic, oc * 128:(oc + 1) * 128], y2b[:, ic, cs], ic == 0, ic == 3,
                                           [B["gw"], B["y2b", ic]], [B["pu5", k % 2]])
                                    sg = sgl[k % 2]
                                    act(sg[:], pst[:], AF.Sigmoid, [B["pu5", k % 2]], [B["sgl", k % 2]], bias=fpc(FP_GLUB(l, oc)))
                                    o_ = oT[:, 8 + oc, :].rearrange("p (n t) -> p t n", t=8)[:, blk * 4:(blk + 1) * 4, :]
                                    tt(o_, y2[:, oc, cs].rearrange("p (t n) -> p t n", n=128), sg[:].rearrange("p (t n) -> p t n", n=128), ALU.mult,
                                       [B["y2", oc], B["sgl", k % 2]], [B["oT", 8 + oc]])
                                    k += 1
                        kb.barrier()

                if "hg" in flags:
                    with contextlib.ExitStack() as s1:
                        def sb1(name, shape, dt):
                            return s1.enter_context(nc.sbuf_tensor(_u(name), list(shape), dt))

                        def ps1(name, shape, dt=F32):
                            return _psum(nc, s1, name, list(shape), dt)
                        qtT = sb1("qtT", [128, 4, TS], BF16)
                        ktT = sb1("ktT", [128, 4, TS], BF16)
                        elast = sb1("elast", [128, 4, TS // 64], F32)
                        vtok = sb1("vtok", [128, NT, 512], BF16)
                        wgt = sb1("wgt", [128, NT, 512], F32)
                        fT2 = [sb1(f"fT{i}", [128, TS], F32) for i in range(2)]
                        lf2 = [sb1(f"lf{i}", [128, TS], F32) for i in range(2)]
                        bb2 = [sb1(f"bb{i}", [128, TS], F32) for i in range(2)]
                        qs2 = [sb1(f"qs{i}", [128, TS], F32) for i in range(2)]
                        pq = [ps1(f"pq{i}", [128, 512]) for i in range(2)]
                        sf = pre["f"] if "f" in pre else wload(512)
                        sq_ = pre["q"] if "q" in pre else wload(0)
                        kqc = [0]

                        def hg_prep_step(step, hd, S):
                            fT, lf, bb, qs = fT2[S], lf2[S], bb2[S], qs2[S]
                            bf, bl, bbb, bq = B["fT", S], B["lf", S], B["bb", S], B["qs", S]
                            if step == 0:
                                for blk in range(NBLK):
                                    cs = slice(blk * 512, (blk + 1) * 512)
                                    pst = pq[kqc[0] % 2]
                                    pb = B["pq", kqc[0] % 2]
                                    kqc[0] += 1
                                    for c in range(DC):
                                        mm(pst[:], wsl[sf][:, c, hd * 128:(hd + 1) * 128], xn[:, c, cs], c == 0, c == DC - 1,
                                           [B["wsl", sf, c // 2], B["xn", c, blk]], [pb])
                                    act(fT[:, cs], pst[:], AF.Sigmoid, [pb], [bf])
                            elif step == 1:
                                ts(fT[:], fT[:], fpc(FP_OML(l, hd)), fpc(FP_LB(l, hd)), ALU.mult, ALU.add, [bf, B["FP"]], [bf])
                            elif step == 2:
                                act(lf[:], fT[:], AF.Ln, [bf], [bl])
                            elif step == 3:
                                scan(bb[:], rstm, lf[:], 0.0, [B["cst"], bl], [bbb])
                            elif step == 4:
                                act(lf[:], bb[:], AF.Exp, [bbb], [bl])
                                act(bb[:], bb[:], AF.Exp, [bbb], [bbb], scale=-1.0)
                            elif step == 5:
                                ts(fT[:], fT[:], -1.0, 1.0, ALU.mult, ALU.add, [bf], [bf])
                                tt(ktT[:, hd, :], fT[:], bb[:], ALU.mult, [bf, bbb], [B["ktT", hd]])
                                cp(elast[:, hd, :], lf[:].rearrange("p (n c) -> p n c", c=64)[:, :, 63], [bl], [B["elast"]], eng="vector")
                            elif step == 6:
                                for blk in range(NBLK):
                                    cs = slice(blk * 512, (blk + 1) * 512)
                                    pst = pq[kqc[0] % 2]
                                    pb = B["pq", kqc[0] % 2]
                                    kqc[0] += 1
                                    for c in range(DC):
                                        mm(pst[:], wsl[sq_][:, c, hd * 128:(hd + 1) * 128], xn[:, c, cs], c == 0, c == DC - 1,
                                           [B["wsl", sq_, c // 2], B["xn", c, blk]], [pb])
                                    act(qs[:, cs], pst[:], AF.Silu, [pb], [bq])
                            elif step == 7:
                                tt(qtT[:, hd, :], qs[:], lf[:], ALU.mult, [bq, bl], [B["qtT", hd]])

                        for hp in range(2):
                            for step in range(8):
                                for S in range(2):
                                    hg_prep_step(step, hp * 2 + S, S)
                        kq = kqc[0]
                        si = wload(1024)
                        sg_ = wload(1536)
                        gnb = RP[:, RP_GN + l * 128:RP_GN + (l + 1) * 128].unsqueeze(1).broadcast_to([128, 4, 128])
                        for it in range(NT):
                            blk = it // 4
                            pst = pq[kq % 2]
                            pb = B["pq", kq % 2]
                            kq += 1
                            for c in range(DC):
                                mm(pst[:], xn[:, c, it * 128:(it + 1) * 128], wsl[si][:, c, :], c == 0, c == DC - 1,
                                   [B["wsl", si, c // 2], B["xn", c, blk]], [pb])
                            cp(vtok[:, it, :], pst[:], [pb], [B["vtok", it]])
                            pst = pq[kq % 2]
                            pb = B["pq", kq % 2]
                            kq += 1
                            for c in range(DC):
                                mm(pst[:], xn[:, c, it * 128:(it + 1) * 128], wsl[sg_][:, c, :], c == 0, c == DC - 1,
                                   [B["wsl", sg_, c // 2], B["xn", c, blk]], [pb])
                            act(wgt[:, it, :], pst[:], AF.Silu, [pb], [B["wgt", it]])
                            tt(wgt[:, it, :].rearrange("p (h v) -> p h v", v=128), wgt[:, it, :].rearrange("p (h v) -> p h v", v=128), gnb, ALU.mult,
                               [B["wgt", it], B["RP"]], [B["wgt", it]])
                        if "ssd" in flags:
                            pre["x0"] = wload(2560)
                            pre["x1"] = wload(3072)
                        ptk = ps1("ptk", [128, 4, 128], BF16)
                        pss = ps1("pss", [128, 4, 128])
                        pso = ps1("pso", [128, 512])
                        psS = ps1("psS", [128, 8, 128])
                        pto = ps1("pto", [128, 4, 128], BF16)
                        ktok2 = [sb1(f"ktok{i}", [128, 4, 128], BF16) for i in range(2)]
                        smk2 = [sb1(f"smk{i}", [128, 4, 128], BF16) for i in range(2)]
                        stmp = sb1("stmp", [128, 4, 128], F32)
                        ssq = sb1("ssq", [128, 4], F32)
                        junk = sb1("junk", [128, 128], F32)
                        og = sb1("og", [128, 512], BF16)

                        def hg_s1(it):
                            k2 = it % 2
                            tc_ = slice(it * 128, (it + 1) * 128)
                            for hd in range(4):
                                tr(ptk[:, hd, :], ktT[:, hd, tc_], identb[:], [B["ktT", hd], B["identb"]], [B["ptk"]])
                            cp(ktok2[k2][:], ptk[:], [B["ptk"]], [B["ktok", k2]], eng="vector")
                            for hd in range(4):
                                mm(pss[:, hd, :], ktT[:, hd, tc_], qtT[:, hd, tc_], True, True, [B["ktT", hd], B["qtT", hd]], [B["pss"]])
                            tt(smk2[k2][:], pss[:], mhg.unsqueeze(1).broadcast_to([128, 4, 128]), ALU.mult, [B["pss"], B["cst"]], [B["smk", k2]])

                        def hg_s2(it):
                            k2 = it % 2
                            tc_ = slice(it * 128, (it + 1) * 128)
                            ktok, smk = ktok2[k2], smk2[k2]
                            for half in range(2):
                                rs_ = slice(half * 64, (half + 1) * 64)
                                ch_i = it * 2 + half
                                for hd in range(4):
                                    hc = slice(hd * 128, (hd + 1) * 128)
                                    mm(pso[rs_, hc], smk[rs_, hd, rs_], vtok[rs_, it, hc], True, False, [B["smk", k2], B["vtok", it]], [B["pso"]])
                                    mm(pso[rs_, hc], qtT[:, hd, it * 128 + half * 64:it * 128 + (half + 1) * 64], Shgb[:, l, hd, :], False, True,
                                       [B["qtT", hd], B["Shgb", hd]], [B["pso"]])
                                    pS_ = psS[:, (hd % 2) * 4, :]
                                    pSb = B["psS", hd % 2]
                                    mm(pS_, ktok[rs_, hd, :], vtok[rs_, it, hc], True, True, [B["ktok", k2], B["vtok", it]], [pSb])
                                    e_ = elast[:, hd, ch_i:ch_i + 1]
                                    ts(stmp[:, hd, :], Shg[:, l, hd, :], e_, None, ALU.mult, None, [B["Shg", hd], B["elast"]], [B["stmp", hd]])
                                    stt(Shg[:, l, hd, :], pS_, e_, stmp[:, hd, :], ALU.mult, ALU.add,
                                        [pSb, B["elast"], B["stmp", hd]], [B["Shg", hd]])
                                    cp(Shgb[:, l, hd, :], Shg[:, l, hd, :], [B["Shg", hd]], [B["Shgb", hd]])
                            for hd in range(4):
                                hc = slice(hd * 128, (hd + 1) * 128)
                                act(junk[:], pso[:, hc], AF.Square, [B["pso"]], [B["junk"], B["ssq"]], accum_out=ssq[:, hd:hd + 1])
                            act(ssq[:], ssq[:], AF.Ln, [B["ssq"]], [B["ssq"]], scale=1.0 / 128, bias=EPS)
                            act(ssq[:], ssq[:], AF.Exp, [B["ssq"]], [B["ssq"]], scale=-0.5)
                            for hd in range(4):
                                hc = slice(hd * 128, (hd + 1) * 128)
                                stt(og[:, hc], pso[:, hc], ssq[:, hd:hd + 1], wgt[:, it, hc], ALU.mult, ALU.mult,
                                    [B["pso"], B["ssq"], B["wgt", it]], [B["og"]])
                            for j in range(4):
                                tr(pto[:, j, :], og[:, j * 128:(j + 1) * 128], identb[:], [B["og"], B["identb"]], [B["pto"]])
                            cp(oT[:, 0:4, tc_], pto[:], [B["pto"]], [B["oT", j] for j in range(4)])

                        hg_s1(0)
                        for it in range(NT):
                            if it + 1 < NT:
                                hg_s1(it + 1)
                            hg_s2(it)
                        kb.barrier()

                if "ssd" in flags:
                    with contextlib.ExitStack() as s2:
                        def sb2(name, shape, dt):
                            return s2.enter_context(nc.sbuf_tensor(_u(name), list(shape), dt))

                        def ps2(name, shape, dt=F32):
                            return _psum(nc, s2, name, list(shape), dt)
                        xT = sb2("xT", [128, 4, TS], BF16)
                        BT = sb2("BT", [128, 2, TS], BF16)
                        CT = sb2("CT", [128, 2, TS], BF16)
                        rawx = [sb2(f"rawx{i}", [128, TS + 3], F32) for i in range(2)]
                        acc = [sb2(f"acc{i}", [128, TS], F32) for i in range(2)]
                        zs = sb2("zs", [128, NT, 512], BF16)
                        dtt = sb2("dtt", [128, NT, 8], F32)
                        adt = sb2("adt", [128, NT, 8], F32)
                        xtok = sb2("xtok", [128, NT, 512], BF16)
                        xdt = sb2("xdt", [128, NT, 512], BF16)
                        Btok = sb2("Btok", [128, NT, 256], BF16)
                        wdt = sb2("wdt", [128, DC, 8], BF16)
                        pq = [ps2(f"pq{i}", [128, 512]) for i in range(2)]
                        kq = 0
                        kb.dma("gpsimd", wdt[:], winv[:, :, 3584:3592], writes=[B["wdt"]])
                        sx = [pre["x0"], pre["x1"]] if "x0" in pre else [wload(2560), wload(3072)]
                        for c8 in range(8):
                            r = rawx[c8 % 2]
                            rb = B["rawx", c8 % 2]
                            cp(r[:, 0:3], halo[:, l, c8, :], [B["halo"]], [rb], eng="vector")
                            for blk in range(NBLK):
                                pst = pq[kq % 2]
                                pb = B["pq", kq % 2]
                                kq += 1
                                cs = slice(blk * 512, (blk + 1) * 512)
                                s = sx[c8 // 4]
                                for c in range(DC):
                                    mm(pst[:], wsl[s][:, c, (c8 % 4) * 128:(c8 % 4 + 1) * 128], xn[:, c, cs], c == 0, c == DC - 1,
                                       [B["wsl", s, c // 2], B["xn", c, blk]], [pb])
                                cp(r[:, 3 + blk * 512:3 + (blk + 1) * 512], pst[:], [pb], [rb])
                            cp(halo[:, l, c8, :], r[:, TS:TS + 3], [rb], [B["halo"]], eng="vector")
                            a_ = acc[c8 % 2]
                            ab = B["acc", c8 % 2]
                            ts(a_[:], r[:, 0:TS], fpc(FP_CONVW(l, 0, c8)), None, ALU.mult, None, [rb, B["FP"]], [ab])
                            for k_ in range(1, 4):
                                stt(a_[:], r[:, k_:TS + k_], fpc(FP_CONVW(l, k_, c8)), a_[:], ALU.mult, ALU.add, [rb, B["FP"], ab], [ab])
                            if c8 < 4:
                                dst, db = xT[:, c8, :], B["xT", c8]
                            elif c8 < 6:
                                dst, db = BT[:, c8 - 4, :], B["BT", c8 - 4]
                            else:
                                dst, db = CT[:, c8 - 6, :], B["CT", c8 - 6]
                            act(dst, a_[:], AF.Silu, [ab, B["FP"]], [db], bias=fpc(FP_CONVB(l, c8)))
                        sz = wload(2048)
                        psm = ps2("psm", [128, 512])
                        pdt = psm[:, 0:NT * 8].rearrange("p (t h) -> p t h", h=8)
                        for it in range(NT):
                            blk = it // 4
                            pst = pq[kq % 2]
                            pb = B["pq", kq % 2]
                            kq += 1
                            for c in range(DC):
                                mm(pst[:], xn[:, c, it * 128:(it + 1) * 128], wsl[sz][:, c, :], c == 0, c == DC - 1,
                                   [B["wsl", sz, c // 2], B["xn", c, blk]], [pb])
                            act(zs[:, it, :], pst[:], AF.Silu, [pb], [B["zs", it]])
                            for c in range(DC):
                                mm(pdt[:, it, :], xn[:, c, it * 128:(it + 1) * 128], wdt[:, c, :], c == 0, c == DC - 1,
                                   [B["wdt"], B["xn", c, blk]], [B["psm"]])
                        dtb = RP[:, RP_DTB + l * 8:RP_DTB + (l + 1) * 8].unsqueeze(1).broadcast_to([128, NT, 8])
                        arow = RP[:, RP_A + l * 8:RP_A + (l + 1) * 8].unsqueeze(1).broadcast_to([128, NT, 8])
                        tt(dtt[:], pdt[:], dtb, ALU.add, [B["psm"], B["RP"]], [B["dtt"]])
                        act(dtt[:], dtt[:], AF.Exp, [B["dtt"]], [B["dtt"]])
                        act(dtt[:], dtt[:], AF.Ln, [B["dtt"]], [B["dtt"]], bias=1.0)
                        tt(adt[:], dtt[:], arow, ALU.mult, [B["dtt"], B["RP"]], [B["adt"]])
                        ptx = ps2("ptx", [128, 6, 128], BF16)
                        for it in range(NT):
                            tc_ = slice(it * 128, (it + 1) * 128)
                            for j in range(4):
                                tr(ptx[:, j, :], xT[:, j, tc_], identb[:], [B["xT", j], B["identb"]], [B["ptx"]])
                            for g in range(2):
                                tr(ptx[:, 4 + g, :], BT[:, g, tc_], identb[:], [B["BT", g], B["identb"]], [B["ptx"]])
                            cp(xtok[:, it, :], ptx[:, 0:4, :].rearrange("p j n -> p (j n)"), [B["ptx"]], [B["xtok", it]])
                            cp(Btok[:, it, :], ptx[:, 4:6, :].rearrange("p j n -> p (j n)"), [B["ptx"]], [B["Btok", it]])
                            tt(xdt[:, it, :].rearrange("p (h d) -> p h d", d=64), xtok[:, it, :].rearrange("p (h d) -> p h d", d=64),
                               dtt[:, it, :].unsqueeze(2).broadcast_to([128, 8, 64]), ALU.mult, [B["xtok", it], B["dtt"]], [B["xdt", it]])
                        pa_ = psm[:, 64:80]
                        pd_ = [ps2(f"pd{i}", [128, 4, 128]) for i in range(2)]
                        psc = psm[:, 128:384].rearrange("p (g n) -> p g n", n=128)
                        pyd = pq[1]
                        pyo = ps2("pyo", [128, 512])
                        acs = sb2("acs", [128, 8], F32)
                        ea2 = [sb2(f"ea{i}", [128, 8], F32) for i in range(2)]
                        cd2 = [sb2(f"cd{i}", [128, 8], F32) for i in range(2)]
                        ds2 = [sb2(f"ds{i}", [128, 8], F32) for i in range(2)]
                        rh = [sb2(f"rh{i}", [128, 128], F32) for i in range(2)]
                        LT = sb2("LT", [128, 8, 128], F32)
                        scm = sb2("scm", [128, 2, 128], F32)
                        Wh2 = [sb2(f"Wh{i}", [128, 8, 128], BF16) for i in range(2)]
                        t1_ = sb2("t1s", [128, 512], F32)
                        yy = sb2("yy", [128, 512], F32)
                        xD = sb2("xD", [128, 512], F32)
                        xdtd = sb2("xdtd", [128, 512], BF16)
                        ss2 = sb2("ss2", [128, 2], F32)
                        junk2 = sb2("junk2", [128, 256], F32)
                        yn = sb2("yn", [128, 512], BF16)
                        drow = RP[:, RP_D + l * 8:RP_D + (l + 1) * 8].unsqueeze(2).broadcast_to([128, 8, 64])
                        v8 = lambda ap: ap.rearrange("p (h d) -> p h d", d=64)

                        def ssd_s1(it):
                            k2 = it % 2
                            tc_ = slice(it * 128, (it + 1) * 128)
                            ea, cd, ds_, Wh = ea2[k2], cd2[k2], ds2[k2], Wh2[k2]
                            mm(pa_[:, 0:8], tri, adt[:, it, :], True, True, [B["cst"], B["adt"]], [B["psm"]])
                            mm(pa_[:, 8:16], ones, adt[:, it, :], True, True, [B["cst"], B["adt"]], [B["psm"]])
                            cp(acs[:], pa_[:, 0:8], [B["psm"]], [B["acs"]])
                            act(ea[:], pa_[:, 0:8], AF.Exp, [B["psm"]], [B["ea", k2]])
                            act(cd[:], pa_[:, 8:16], AF.Exp, [B["psm"]], [B["cd", k2]])
                            tt(ds_[:], pa_[:, 8:16], acs[:], ALU.subtract, [B["psm"], B["acs"]], [B["ds", k2]])
                            act(ds_[:], ds_[:], AF.Exp, [B["ds", k2]], [B["ds", k2]])
                            for g in range(2):
                                mm(psc[:, g, :], BT[:, g, tc_], CT[:, g, tc_], True, True, [B["BT", g], B["CT", g]], [B["psm"]])
                            tt(scm[:], psc[:], tri.unsqueeze(1).broadcast_to([128, 2, 128]), ALU.mult, [B["psm"], B["cst"]], [B["scm"]])
                            for h in range(8):
                                r_ = rh[h % 2]
                                ts(r_[:], tri, adt[:, it, h:h + 1], None, ALU.mult, None, [B["cst"], B["adt"]], [B["rh", h % 2]])
                                mm(pd_[h // 4][:, h % 4, :], ups, r_[:], True, True, [B["cst"], B["rh", h % 2]], [B["pd", h // 4]])
                            for k_ in range(2):
                                act(LT[:, k_ * 4:(k_ + 1) * 4, :], pd_[k_][:], AF.Exp, [B["pd", k_]], [B["LT"]])
                            for g in range(2):
                                tt(Wh[:, g * 4:(g + 1) * 4, :], LT[:, g * 4:(g + 1) * 4, :], scm[:, g, :].unsqueeze(1).broadcast_to([128, 4, 128]), ALU.mult,
                                   [B["LT"], B["scm"]], [B["Wh", k2]])

                        def ssd_s2(it):
                            k2 = it % 2
                            tc_ = slice(it * 128, (it + 1) * 128)
                            ea, cd, ds_, Wh = ea2[k2], cd2[k2], ds2[k2], Wh2[k2]
                            for h in range(8):
                                mm(pyd[:, h * 64:(h + 1) * 64], Wh[:, h, :], xdt[:, it, h * 64:(h + 1) * 64], True, True, [B["Wh", k2], B["xdt", it]], [B["pq", 1]])
                            for g in range(2):
                                mm(pyo[:, g * 256:(g + 1) * 256], CT[:, g, tc_], Hstb[:, l, g * 256:(g + 1) * 256], True, True,
                                   [B["CT", g], B["Hstb"]], [B["pyo"]])
                            tt(v8(t1_[:]), v8(pyo[:]), ea[:].unsqueeze(2).broadcast_to([128, 8, 64]), ALU.mult, [B["pyo"], B["ea", k2]], [B["t1s"]])
                            tt(yy[:], pyd[:], t1_[:], ALU.add, [B["pq", 1], B["t1s"]], [B["yy"]])
                            tt(v8(xD[:]), v8(xtok[:, it, :]), drow, ALU.mult, [B["xtok", it], B["RP"]], [B["xD"]])
                            tt(yy[:], yy[:], xD[:], ALU.add, [B["yy"], B["xD"]], [B["yy"]])
                            tt(yy[:], yy[:], zs[:, it, :], ALU.mult, [B["yy"], B["zs", it]], [B["yy"]])
                            for g in range(2):
                                act(junk2[:], yy[:, g * 256:(g + 1) * 256], AF.Square, [B["yy"]], [B["junk2"], B["ss2"]], accum_out=ss2[:, g:g + 1])
                            act(ss2[:], ss2[:], AF.Ln, [B["ss2"]], [B["ss2"]], scale=1.0 / 256, bias=EPS)
                            act(ss2[:], ss2[:], AF.Exp, [B["ss2"]], [B["ss2"]], scale=-0.5)
                            for g in range(2):
                                gc = slice(g * 256, (g + 1) * 256)
                                stt(yn[:, gc], yy[:, gc], ss2[:, g:g + 1], RP[:, RP_NW + l * 512 + g * 256:RP_NW + l * 512 + (g + 1) * 256], ALU.mult, ALU.mult,
                                    [B["yy"], B["ss2"], B["RP"]], [B["yn"]])
                            for j in range(4):
                                tr(ptx[:, j, :], yn[:, j * 128:(j + 1) * 128], identb[:], [B["yn"], B["identb"]], [B["ptx"]])
                            cp(oT[:, 4:8, tc_], ptx[:, 0:4, :], [B["ptx"]], [B["oT", j] for j in range(4, 8)])
                            tt(v8(xdtd[:]), v8(xdt[:, it, :]), ds_[:].unsqueeze(2).broadcast_to([128, 8, 64]), ALU.mult, [B["xdt", it], B["ds", k2]], [B["xdtd"]])
                            pst = pq[0]
                            for g in range(2):
                                gc = slice(g * 256, (g + 1) * 256)
                                mm(pst[:, gc], Btok[:, it, g * 128:(g + 1) * 128], xdtd[:, gc], True, True, [B["Btok", it], B["xdtd"]], [B["pq", 0]])
                            tt(v8(Hst[:, l, :]), v8(Hst[:, l, :]), cd[:].unsqueeze(2).broadcast_to([128, 8, 64]), ALU.mult, [B["Hst"], B["cd", k2]], [B["Hst"]])
                            tt(Hst[:, l, :], Hst[:, l, :], pst[:], ALU.add, [B["Hst"], B["pq", 0]], [B["Hst"]])
                            cp(Hstb[:, l, :], Hst[:, l, :], [B["Hst"]], [B["Hstb"]])

                        ssd_s1(0)
                        for it in range(NT):
                            if it + 1 < NT:
                                ssd_s1(it + 1)
                            ssd_s2(it)
                        kb.barrier()

                if dbg and l == 0:
                    with contextlib.ExitStack() as sd:
                        of = sd.enter_context(nc.sbuf_tensor(_u("of"), [128, 12, TS], F32))
                        cp(of[:].rearrange("p j n -> p (j n)"), oT[:].rearrange("p j n -> p (j n)"), [B["oT", j] for j in range(12)], [B["of"]], eng="vector")
                        kb.dma("sync", dbg_o.rearrange("(j p) n -> p j n", p=128)[:, :, seg * TS:(seg + 1) * TS], of[:], reads=[B["of"]], writes=[B["dbg_o"]])
                        kb.barrier()

                with contextlib.ExitStack() as s3:
                    def sb3(name, shape, dt):
                        return s3.enter_context(nc.sbuf_tensor(_u(name), list(shape), dt))

                    def ps3(name, shape, dt=F32):
                        return _psum(nc, s3, name, list(shape), dt)
                    wo = sb3("wo", [128, 12, D], BF16)
                    ysb = sb3("ysb", [128, DC, 512], F32)
                    sqt = sb3("sq8", [128, DC, 512], BF16)
                    rstd = sb3("rstd", [128, 512], F32)
                    tmpt = [sb3(f"ptmp{i}", [128, 512], F32) for i in range(2)]
                    ssp = ps3("ssp", [128, 512])
                    py = [ps3(f"py{i}", [128, 512]) for i in range(2)]
                    wov = w_out[l].rearrange("(j p) n -> p j n", p=128)
                    for j in range(12):
                        kb.dma("gpsimd", wo[:, j, :], wov[:, j, :], writes=[B["wo", j]])
                    cnt = 0
                    for blk in range(NBLK):
                        cs = slice(blk * 512, (blk + 1) * 512)
                        for m in range(DC):
                            k = cnt % 2
                            cnt += 1
                            for j in range(12):
                                mm(py[k][:], wo[:, j, m * 128:(m + 1) * 128], oT[:, j, cs], j == 0, j == 11, [B["wo", j], B["oT", j]], [B["py", k]])
                            cp(ysb[:, m, :], py[k][:], [B["py", k]], [B["ysb", m]])
                        post_residual(lambda m: FP_G(l, 3, m), blk, ysb, sqt, ssp, rstd, tmpt)

        def load_x(seg, xs8):
            for it in range(NT):
                r0 = seg * TS + it * 128
                kb.dma("sync", xs8[it][:], x[r0:r0 + 128, :], writes=[B["xs", it]])

        def xpose_in(xs8, px):
            for it in range(NT):
                k = it % 2
                for c in range(DC):
                    tr(px[k][:, c // 4, (c % 4) * 128:(c % 4 + 1) * 128], xs8[it][:, c * 128:(c + 1) * 128], ident, [B["xs", it], B["cst"]], [B["px", k]])
                cp(hT[:, :, it * 128:(it + 1) * 128], px[k][:].rearrange("p a (b n) -> p (a b) n", n=128), [B["px", k]],
                   [B["h", c, it // 4] for c in range(DC)])

        def store_out(seg, xo, pxo):
            for it in range(NT):
                k = it % 2
                r0 = seg * TS + it * 128
                for c in range(DC):
                    tr(pxo[k][:, c // 4, (c % 4) * 128:(c % 4 + 1) * 128], hT[:, c, it * 128:(it + 1) * 128], ident,
                       [B["h", c, it // 4], B["cst"]], [B["pxo", k]])
                cp(xo[k][:], pxo[k][:].rearrange("p a n -> p (a n)"), [B["pxo", k]], [B["xo", k]])
                kb.dma("sync", out[r0:r0 + 128, :], xo[k][:], reads=[B["xo", k]], writes=[B["out", seg, it]])

        kb.barrier()
        with contextlib.ExitStack() as sc:
            xs8 = [sc.enter_context(nc.sbuf_tensor(_u(f"xs{i}"), [128, D], F32)) for i in range(NT)]
            px = [_psum(nc, sc, f"px{i}", [128, 2, 512], F32) for i in range(2)]
            load_x(0, xs8)
            xpose_in(xs8, px)
        for seg in range(n_seg):
            for l in range(depth):
                if "ffn" in flags:
                    ffn_phase(l, 0)
                if any(k in flags for k in ("hg", "ssd", "s5")):
                    mixer_phase(l, seg)
                if "ffn" in flags:
                    ffn_phase(l, 1)
            kb.barrier()
            with contextlib.ExitStack() as sc:
                nxt = seg + 1 < n_seg
                xo = [sc.enter_context(nc.sbuf_tensor(_u(f"xo{i}"), [128, D], F32)) for i in range(2)]
                pxo = [_psum(nc, sc, f"pxo{i}", [128, 2, 512], F32) for i in range(2)]
                if nxt:
                    xs8 = [sc.enter_context(nc.sbuf_tensor(_u(f"xs{i}"), [128, D], F32)) for i in range(NT)]
                    px = [_psum(nc, sc, f"px{i}", [128, 2, 512], F32) for i in range(2)]
                    load_x(seg + 1, xs8)
                store_out(seg, xo, pxo)
                if nxt:
                    xpose_in(xs8, px)
        kb.barrier()
        kb.replay()
    return nc


PARAM_NAMES = ["norm_g", "ffn_w_gate", "ffn_w_up", "ffn_w_down", "w_in", "w_out", "hg_lb_logits", "hg_gnorm",
               "ssd_conv_w", "ssd_conv_b", "ssd_dt_bias", "ssd_A_log", "ssd_D", "ssd_norm", "s5_A_re", "s5_A_im",
               "s5_B_re", "s5_B_im", "s5_C_re", "s5_C_im", "s5_D", "s5_log_dt", "s5_glu_w", "s5_glu_b"]


def run(inputs, n_seg, depth=DEPTH, flags=("ffn", "hg", "ssd", "s5"), dbg=False):
    x = np.ascontiguousarray(np.asarray(inputs["x"], dtype=np.float32))
    n_seq = x.shape[0]
    nc = build(n_seg, depth, flags, dbg)
    cst = make_consts()
    params = {k: np.ascontiguousarray(np.asarray(inputs[k], dtype=np.float32)) for k in PARAM_NAMES}
    in_maps = []
    for c in range(N_CORES):
        m = {"x": x[c % n_seq], "cst": cst}
        m.update(params)
        in_maps.append(m)
    res = run_bass_kernel_spmd(nc, in_maps, core_ids=list(range(N_CORES)))
    outs = np.stack([res.results[c]["out"] for c in range(n_seq)], axis=0)
    if dbg:
        return outs, np.stack([res.results[c]["dbg_o"] for c in range(n_seq)], axis=0)
    return outs


def kernel(**inputs):
    x = np.asarray(inputs["x"])
    n_seg = x.shape[1] // TS
    return run(inputs, n_seg).astype(np.float32)
```
